# Optimizing a Trainium2 kernel written in Bass

```python
import math
import jax, jax.numpy as jnp
from jax import lax
import numpy as np

D_MODEL = 4096
BATCH = 2
SEQ = 4096
DEPTH = 4

N_MIXERS = 2
N_SSM_LAYERS = (DEPTH + 1) // 2
N_ATTN_LAYERS = DEPTH // 2
SSM_GROUP = 16
SSM_GROUPS = D_MODEL // SSM_GROUP
SSM_STATE = 64
DT_MIN = 0.001
DT_MAX = 0.1
HEAD_DIM = 64
N_HEADS = D_MODEL // HEAD_DIM
N_KV_HEADS = 8
GQA_GROUP = N_HEADS // N_KV_HEADS
WINDOW = 128
BLOCK = WINDOW
Q_WIDTH = N_HEADS * HEAD_DIM
KV_WIDTH = N_KV_HEADS * HEAD_DIM
QKV_WIDTH = Q_WIDTH + 2 * KV_WIDTH
REL_BUCKETS = 32
REL_MAX_DIST = 128
D_FF = ((8 * D_MODEL + 3 * 256 - 1) // (3 * 256)) * 256
ALPHA = (2 * DEPTH) ** 0.25
BETA = (8 * DEPTH) ** -0.25
N_MODS = 6
ADA_INIT = 0.1
LN_EPS = 1e-5
NEG_INF = -1e30

kernel_name = "hybrid_s5_swa_sink_deepnorm_adaln"


def _layernorm(x, g, b):
    xf = x.astype(jnp.float32)
    mu = jnp.mean(xf, axis=-1, keepdims=True)
    var = jnp.mean(jnp.square(xf - mu), axis=-1, keepdims=True)
    y = (xf - mu) * lax.rsqrt(var + LN_EPS) * g.astype(jnp.float32) + b.astype(jnp.float32)
    return y.astype(x.dtype)


def _rel_bucket(dist):
    max_exact = REL_BUCKETS // 2
    d = jnp.maximum(dist, 0)
    log_ratio = jnp.log(jnp.maximum(d, 1).astype(jnp.float32) / max_exact) / math.log(REL_MAX_DIST / max_exact)
    large = max_exact + (log_ratio * (REL_BUCKETS - max_exact)).astype(jnp.int32)
    large = jnp.minimum(large, REL_BUCKETS - 1)
    return jnp.where(d < max_exact, d, large)


def _s5_mixer(h, a_re, a_im, log_dt, b_re, b_im, c_re, c_im, d_skip, w_val, w_gate):
    bsz, seq, _ = h.shape
    f32 = jnp.float32
    x = h.astype(f32)
    xg = x.reshape(bsz, seq, SSM_GROUPS, SSM_GROUP)
    a_re = a_re.astype(f32)
    a_im = a_im.astype(f32)
    dt = jnp.exp(log_dt.astype(f32))[:, None]
    mag = jnp.exp(a_re * dt)
    ab_re = mag * jnp.cos(a_im * dt)
    ab_im = mag * jnp.sin(a_im * dt)
    den = a_re * a_re + a_im * a_im
    nr = ab_re - 1.0
    k_re = (nr * a_re + ab_im * a_im) / den
    k_im = (ab_im * a_re - nr * a_im) / den
    b_re = b_re.astype(f32)
    b_im = b_im.astype(f32)
    bb_re = k_re[..., None] * b_re - k_im[..., None] * b_im
    bb_im = k_re[..., None] * b_im + k_im[..., None] * b_re
    u_re = jnp.einsum('blgc,gpc->lbgp', xg, bb_re)
    u_im = jnp.einsum('blgc,gpc->lbgp', xg, bb_im)
    shp = (seq, 1, SSM_GROUPS, SSM_STATE)
    at_re = jnp.broadcast_to(ab_re[None, None], shp)
    at_im = jnp.broadcast_to(ab_im[None, None], shp)

    def combine(e1, e2):
        a1r, a1i, b1r, b1i = e1
        a2r, a2i, b2r, b2i = e2
        return (a2r * a1r - a2i * a1i,
                a2r * a1i + a2i * a1r,
                a2r * b1r - a2i * b1i + b2r,
                a2r * b1i + a2i * b1r + b2i)

    _, _, s_re, s_im = lax.associative_scan(combine, (at_re, at_im, u_re, u_im), axis=0)
    y = (jnp.einsum('lbgp,gcp->blgc', s_re, c_re.astype(f32))
         - jnp.einsum('lbgp,gcp->blgc', s_im, c_im.astype(f32)))
    y = y.reshape(bsz, seq, D_MODEL) + d_skip.astype(f32) * x
    u = jax.nn.gelu(y).astype(h.dtype)
    return (u @ w_val) * jax.nn.sigmoid(u @ w_gate)


def _swa_sink_attention(h, w_qkv, b_qkv, w_o, sinks, rel_bias):
    bsz, seq, _ = h.shape
    nb = seq // BLOCK
    qkv = h @ w_qkv + b_qkv
    q, k, v = jnp.split(qkv, [Q_WIDTH, Q_WIDTH + KV_WIDTH], axis=-1)
    q = q.reshape(bsz, nb, BLOCK, N_KV_HEADS, GQA_GROUP, HEAD_DIM)
    k = k.reshape(bsz, nb, BLOCK, N_KV_HEADS, HEAD_DIM)
    v = v.reshape(bsz, nb, BLOCK, N_KV_HEADS, HEAD_DIM)

    def with_prev(t):
        prev = jnp.pad(t[:, :-1], ((0, 0), (1, 0), (0, 0), (0, 0), (0, 0)))
        return jnp.concatenate([prev, t], axis=2)

    kb = with_prev(k)
    vb = with_prev(v)
    logits = jnp.einsum('bnqkgd,bnskd->bnkgqs', q, kb,
                        preferred_element_type=jnp.float32) * (HEAD_DIM ** -0.5)
    qi = jnp.arange(BLOCK)[:, None]
    sj = jnp.arange(2 * BLOCK)[None, :]
    dist = qi + BLOCK - sj
    band = (dist >= 0) & (dist < WINDOW)
    blk = jnp.arange(nb)[:, None, None]
    mask = band[None] & ((blk > 0) | (sj >= BLOCK)[None])
    bias = rel_bias.astype(jnp.float32)[_rel_bucket(dist)]
    bias = jnp.transpose(bias, (2, 0, 1)).reshape(N_KV_HEADS, GQA_GROUP, BLOCK, 2 * BLOCK)
    logits = jnp.where(mask[None, :, None, None], logits + bias, NEG_INF)
    sink = jnp.broadcast_to(sinks.astype(jnp.float32).reshape(N_KV_HEADS, GQA_GROUP)[None, None, :, :, None, None],
                            logits.shape[:-1] + (1,))
    probs = jax.nn.softmax(jnp.concatenate([logits, sink], axis=-1), axis=-1)[..., :-1]
    out = jnp.einsum('bnkgqs,bnskd->bnqkgd', probs.astype(vb.dtype), vb)
    out = out.reshape(bsz, seq, Q_WIDTH)
    return out @ w_o


def _swiglu(h, w_in, w_down):
    gu = h @ w_in
    g, u = jnp.split(gu, 2, axis=-1)
    return (jax.nn.silu(g) * u) @ w_down


def setup_inputs(seed: int = 0) -> dict:
    key = jax.random.key(seed)
    ks = jax.random.split(key, 24)
    f32 = jnp.float32

    def nrm(k, shape, scale):
        return jax.random.normal(k, shape, f32) * scale

    n_idx = jnp.arange(SSM_STATE, dtype=f32)
    return {
        "x": nrm(ks[0], (BATCH, SEQ, D_MODEL), 1.0),
        "c": nrm(ks[1], (BATCH, D_MODEL), 1.0),
        "w_ada": nrm(ks[2], (D_MODEL, N_MODS * D_MODEL), ADA_INIT * D_MODEL ** -0.5),
        "b_ada": nrm(ks[3], (N_MODS * D_MODEL,), 0.01),
        "ada_table": nrm(ks[4], (DEPTH, N_MODS, D_MODEL), ADA_INIT),
        "ln_g": 1.0 + nrm(ks[5], (DEPTH, 2, D_MODEL), 0.02),
        "ln_b": nrm(ks[6], (DEPTH, 2, D_MODEL), 0.02),
        "ffn_w_in": nrm(ks[7], (DEPTH, D_MODEL, 2 * D_FF), D_MODEL ** -0.5),
        "ffn_w_down": nrm(ks[8], (DEPTH, D_FF, D_MODEL), BETA * D_FF ** -0.5),
        "attn_w_qkv": nrm(ks[9], (N_ATTN_LAYERS, D_MODEL, QKV_WIDTH), D_MODEL ** -0.5),
        "attn_b_qkv": nrm(ks[10], (N_ATTN_LAYERS, QKV_WIDTH), 0.01),
        "attn_w_o": nrm(ks[11], (N_ATTN_LAYERS, Q_WIDTH, D_MODEL), BETA * Q_WIDTH ** -0.5),
        "attn_sinks": nrm(ks[12], (N_ATTN_LAYERS, N_HEADS), 0.5),
        "rel_bias": nrm(ks[13], (REL_BUCKETS, N_HEADS), 0.5),
        "ssm_a_re": -0.5 * jnp.exp(nrm(ks[14], (N_SSM_LAYERS, SSM_GROUPS, SSM_STATE), 0.05)),
        "ssm_a_im": math.pi * n_idx + nrm(ks[15], (N_SSM_LAYERS, SSM_GROUPS, SSM_STATE), 0.05),
        "ssm_log_dt": jax.random.uniform(ks[16], (N_SSM_LAYERS, SSM_GROUPS), f32,
                                         math.log(DT_MIN), math.log(DT_MAX)),
        "ssm_b_re": nrm(ks[17], (N_SSM_LAYERS, SSM_GROUPS, SSM_STATE, SSM_GROUP), (2 * SSM_GROUP) ** -0.5),
        "ssm_b_im": nrm(ks[18], (N_SSM_LAYERS, SSM_GROUPS, SSM_STATE, SSM_GROUP), (2 * SSM_GROUP) ** -0.5),
        "ssm_c_re": nrm(ks[19], (N_SSM_LAYERS, SSM_GROUPS, SSM_GROUP, SSM_STATE), SSM_STATE ** -0.5),
        "ssm_c_im": nrm(ks[20], (N_SSM_LAYERS, SSM_GROUPS, SSM_GROUP, SSM_STATE), SSM_STATE ** -0.5),
        "ssm_d": nrm(ks[21], (N_SSM_LAYERS, D_MODEL), 0.5),
        "ssm_w_val": nrm(ks[22], (N_SSM_LAYERS, D_MODEL, D_MODEL), BETA * D_MODEL ** -0.5),
        "ssm_w_gate": nrm(ks[23], (N_SSM_LAYERS, D_MODEL, D_MODEL), D_MODEL ** -0.5),
    }


def reference(x, c, w_ada, b_ada, ada_table, ln_g, ln_b, ffn_w_in, ffn_w_down,
              attn_w_qkv, attn_b_qkv, attn_w_o, attn_sinks, rel_bias,
              ssm_a_re, ssm_a_im, ssm_log_dt, ssm_b_re, ssm_b_im, ssm_c_re, ssm_c_im,
              ssm_d, ssm_w_val, ssm_w_gate):
    bsz = x.shape[0]
    mods_shared = (jax.nn.silu(c) @ w_ada + b_ada).reshape(bsz, N_MODS, D_MODEL)
    for layer in range(DEPTH):
        mods = mods_shared + ada_table[layer]
        shift_m = mods[:, 0, None, :]
        scale_m = mods[:, 1, None, :]
        gate_m = mods[:, 2, None, :]
        shift_f = mods[:, 3, None, :]
        scale_f = mods[:, 4, None, :]
        gate_f = mods[:, 5, None, :]
        j = layer // N_MIXERS
        h = x * (1.0 + scale_m) + shift_m
        if layer % N_MIXERS == 0:
            y = _s5_mixer(h, ssm_a_re[j], ssm_a_im[j], ssm_log_dt[j], ssm_b_re[j], ssm_b_im[j],
                          ssm_c_re[j], ssm_c_im[j], ssm_d[j], ssm_w_val[j], ssm_w_gate[j])
        else:
            y = _swa_sink_attention(h, attn_w_qkv[j], attn_b_qkv[j], attn_w_o[j],
                                    attn_sinks[j], rel_bias)
        x = _layernorm(ALPHA * x + (1.0 + gate_m) * y, ln_g[layer, 0], ln_b[layer, 0])
        h = x * (1.0 + scale_f) + shift_f
        y = _swiglu(h, ffn_w_in[layer], ffn_w_down[layer])
        x = _layernorm(ALPHA * x + (1.0 + gate_f) * y, ln_g[layer, 1], ln_b[layer, 1])
    return x
```

```python
import math
from contextlib import ExitStack, contextmanager

import numpy as np
import ml_dtypes

import concourse.bass as bass
import concourse.mybir as mybir
from concourse.bass_utils import run_bass_kernel_spmd

F32 = mybir.dt.float32
BF16 = mybir.dt.bfloat16
I32 = mybir.dt.int32
AF = mybir.ActivationFunctionType
ALU = mybir.AluOpType
AX = mybir.AxisListType

SEM_ROLL = 30000

D = 4096
KC = 32
DEPTH = 4
DFF = 11008
NJ = 86
TT = 1024
HT = 512
NHEAD = 64
NKV = 8
ALPHA = (2 * DEPTH) ** 0.25
LN_EPS = 1e-5
EPS_P = LN_EPS / (ALPHA * ALPHA)
PI_LO = 3.1415925
TWO_PI = 2.0 * math.pi
CW1 = 6.28125
CW2 = TWO_PI - 6.28125
MAGIC = 12582912.0


class Buf:
    def __init__(self, prog, name, base_ap, space, root=None):
        self.prog = prog
        self.name = name
        self.base = base_ap
        self.space = space
        self.w = {}
        self.r = {}
        self.sem = None
        self.ndma = 0
        self.root = root if root is not None else self

    def ap(self):
        return self.base

    def __getitem__(self, idx):
        return self.base[idx]

    def view(self, ap, name=None):
        return Buf(self.prog, name or self.name + "_v", ap, self.space, root=self.root)


class Prog:
    ENG = ("pe", "act", "dve", "pool", "sp")

    def __init__(self):
        self.nc = bass.Bass("TRN2", target_bir_lowering=False)
        self.es = ExitStack()
        self.rec = {e: [] for e in self.ENG}
        self.cnt = {e: 0 for e in self.ENG}
        self.esem = {}
        self.nsem = 0
        for e in ("pe", "act", "dve", "pool"):
            self.esem[e] = self._newsem(e)
        self.waited = {e: {} for e in self.ENG}
        self.dma_owner = {}
        self.free_dma_sems = []
        self.out_bufs = []
        self.ninst = 0
        self.scopes = []

    def _newsem(self, name):
        self.nsem += 1
        s = self.es.enter_context(self.nc.semaphore("s%d_%s" % (self.nsem, name)))
        return s

    def dram(self, name, shape, dtype, kind="Internal"):
        t = self.nc.dram_tensor(name, list(shape), dtype, kind=kind)
        b = Buf(self, name, t.ap(), "dram")
        if kind == "ExternalOutput":
            self.out_bufs.append(b)
        return b

    def sb(self, name, shape, dtype):
        st = self.scopes[-1][0] if self.scopes else self.es
        self.nsb = getattr(self, "nsb", 0) + 1
        name = "%s_%d" % (name, self.nsb)
        t = st.enter_context(self.nc.sbuf_tensor(name, list(shape), dtype))
        b = Buf(self, name, t[:], "sb")
        if self.scopes:
            self.scopes[-1][1].append(b)
        return b

    def ps(self, name, shape, dtype):
        t = self.es.enter_context(self.nc.psum_tensor(name, list(shape), dtype))
        return Buf(self, name, t[:], "ps")

    @contextmanager
    def scope(self):
        st = ExitStack()
        bufs = []
        self.scopes.append((st, bufs))
        try:
            yield
        finally:
            self.barrier()
            self.scopes.pop()
            for b in bufs:
                if b.sem is not None:
                    self.free_dma_sems.append((b.sem, b.ndma))
                    del self.dma_owner[id(b.sem)]
            st.close()

    def barrier(self):
        tok = {}
        for e in ("pe", "act", "dve", "pool"):
            tok[self.esem[e]] = self.cnt[e]
        for o in list(self.dma_owner.values()):
            tok[o.sem] = 16 * o.ndma
        for e in self.ENG:
            self._emit_waits(e, dict(tok), force_pe=True)

    def _emit_waits(self, eng, tokens, force_pe=False):
        for sem, val in tokens.items():
            owner = self.dma_owner.get(id(sem))
            if owner is not None:
                val = 16 * owner.ndma
            elif eng == "pe" and sem is self.esem["pe"] and not force_pe:
                continue
            if val <= 0:
                continue
            if self.waited[eng].get(id(sem), -1) >= val:
                continue
            self.waited[eng][id(sem)] = val
            self.rec[eng].append(("wait", sem, val))

    @staticmethod
    def _merge(tok, d):
        for s, v in d.items():
            if tok.get(s, -1) < v:
                tok[s] = v

    def op(self, eng, fn, reads=(), writes=()):
        tok = {}
        for b in reads:
            self._merge(tok, b.w)
        for b in writes:
            self._merge(tok, b.w)
            self._merge(tok, b.r)
        self._emit_waits(eng, tok)
        if self.cnt[eng] >= SEM_ROLL:
            self.esem[eng] = self._newsem(eng)
            self.cnt[eng] = 0
        self.cnt[eng] += 1
        sem = self.esem[eng]
        c = self.cnt[eng]
        self.rec[eng].append(("op", fn, sem, 1))
        self.ninst += 1
        for b in reads:
            b.r[sem] = c
        for b in writes:
            b.w = {sem: c}
            b.r = {}

    def _dma_sem(self, owner):
        owner = owner.root
        if owner.sem is None:
            if self.free_dma_sems:
                owner.sem, owner.ndma = self.free_dma_sems.pop()
            else:
                owner.sem = self._newsem("d_" + owner.name)
            self.dma_owner[id(owner.sem)] = owner
        return owner

    def dma(self, q, dst, dst_ap, src, src_ap, **kw):
        tok = {}
        self._merge(tok, src.w)
        self._merge(tok, dst.w)
        self._merge(tok, dst.r)
        self._emit_waits(q, tok)
        owner = dst if dst.space == "sb" else src
        owner = self._dma_sem(owner)
        owner.ndma += 1
        sem = owner.sem
        val = 16 * owner.ndma
        self.rec[q].append(("op", (lambda e: e.dma_start(out=dst_ap, in_=src_ap, **kw)), sem, 16))
        self.ninst += 1
        src.r[sem] = val
        dst.w = {sem: val}
        dst.r = {}

    def finish(self):
        self.barrier()
        engmap = {"pe": "tensor", "act": "scalar", "dve": "vector", "pool": "gpsimd", "sp": "sync"}
        with self.nc.Block() as block:
            for e in self.ENG:
                recs = self.rec[e]

                def body(engine, recs=recs):
                    for r in recs:
                        if r[0] == "wait":
                            engine.wait_ge(r[1], r[2])
                        else:
                            r[1](engine).then_inc(r[2], r[3])

                getattr(block, engmap[e])(body)
        return self.nc


def _ap(x):
    return x[1] if isinstance(x, tuple) else x.ap()


def _bf(x):
    return x[0] if isinstance(x, tuple) else x


def tt(P, eng, out, a, b, op):
    oa, aa, ba = _ap(out), _ap(a), _ap(b)
    P.op(eng, lambda e: e.tensor_tensor(out=oa, in0=aa, in1=ba, op=op), reads=[_bf(a), _bf(b)], writes=[_bf(out)])


def ts(P, eng, out, a, s1, s2, op0, op1=None, extra_reads=()):
    oa, aa = _ap(out), _ap(a)
    if eng == "pool" and op1 is None:
        if op0 == ALU.mult:
            op1, s2 = ALU.add, 0.0
        elif op0 == ALU.add:
            op1, s2 = ALU.mult, 1.0
    if op1 is None:
        P.op(eng, lambda e: e.tensor_scalar(out=oa, in0=aa, scalar1=s1, scalar2=None, op0=op0),
             reads=[_bf(a)] + list(extra_reads), writes=[_bf(out)])
    else:
        P.op(eng, lambda e: e.tensor_scalar(out=oa, in0=aa, scalar1=s1, scalar2=s2, op0=op0, op1=op1),
             reads=[_bf(a)] + list(extra_reads), writes=[_bf(out)])


def stt(P, out, a, s, b, op0, op1, extra_reads=()):
    oa, aa, ba = _ap(out), _ap(a), _ap(b)
    P.op("dve", lambda e: e.scalar_tensor_tensor(out=oa, in0=aa, scalar=s, in1=ba, op0=op0, op1=op1),
         reads=[_bf(a), _bf(b)] + list(extra_reads), writes=[_bf(out)])


def act(P, out, a, func, scale=None, bias=None, extra_reads=()):
    oa, aa = _ap(out), _ap(a)
    kw = {}
    if scale is not None:
        kw["scale"] = scale
    if bias is not None:
        kw["bias"] = bias
    P.op("act", lambda e: e.activation(out=oa, in_=aa, func=func, **kw), reads=[_bf(a)] + list(extra_reads),
         writes=[_bf(out)])


def cp(P, eng, out, a):
    oa, aa = _ap(out), _ap(a)
    if eng == "act":
        P.op("act", lambda e: e.activation(out=oa, in_=aa, func=AF.Copy), reads=[_bf(a)], writes=[_bf(out)])
    else:
        P.op(eng, lambda e: e.tensor_copy(out=oa, in_=aa), reads=[_bf(a)], writes=[_bf(out)])


def mm(P, out, lhsT, rhs, start, stop):
    oa, la, ra = _ap(out), _ap(lhsT), _ap(rhs)
    P.op("pe", lambda e: e.matmul(oa, lhsT=la, rhs=ra, start=start, stop=stop), reads=[_bf(lhsT), _bf(rhs)],
         writes=[_bf(out)])


def memset(P, eng, out, val):
    oa = _ap(out)
    P.op(eng, lambda e: e.memset(oa, val), writes=[_bf(out)])


def _fm(v):
    v = np.asarray(v, np.float32)
    return np.ascontiguousarray(v.reshape(-1, 128).T)


def _pieces(w, ncol=128):
    K, N = w.shape
    a = w.reshape(K // 128, 128, N // ncol, ncol)
    return np.ascontiguousarray(a.transpose(2, 1, 0, 3)).reshape(N // ncol, 128, (K // 128) * ncol)


class VecLayout:
    def __init__(self):
        self.off = {}
        self.n = 0

    def add(self, name, ncols):
        self.off[name] = self.n
        self.n += ncols


def vec_layout():
    L = VecLayout()
    L.add("c", 32)
    for m in range(6):
        L.add("bada%d" % m, 32)
    for l in range(DEPTH):
        for m in range(6):
            L.add("tab%d_%d" % (l, m), 32)
        for s in range(2):
            L.add("lng%d_%d" % (l, s), 32)
            L.add("lnb%d_%d" % (l, s), 32)
    for j in range(2):
        L.add("ssmd%d" % j, 32)
        L.add("bq%d" % j, 32)
        L.add("bk%d" % j, 8)
    return L


VL = vec_layout()


def _rel_bucket_np(dist):
    max_exact = 16
    d = np.maximum(dist, 0)
    log_ratio = np.log(np.maximum(d, 1).astype(np.float32) / max_exact) / math.log(128 / max_exact)
    large = max_exact + (log_ratio * (32 - max_exact)).astype(np.int32)
    large = np.minimum(large, 31)
    return np.where(d < max_exact, d, large)


def prep_inputs(inp, b, layers, ntok):
    m = {}
    m["xT"] = np.ascontiguousarray(np.asarray(inp["x"][b, :ntok], np.float32).T)
    vec = np.zeros((128, VL.n), np.float32)

    def put(name, v):
        a = _fm(v)
        vec[:, VL.off[name]:VL.off[name] + a.shape[1]] = a

    put("c", inp["c"][b])
    for mm_ in range(6):
        put("bada%d" % mm_, inp["b_ada"][mm_ * D:(mm_ + 1) * D])
    for l in range(DEPTH):
        for mm_ in range(6):
            put("tab%d_%d" % (l, mm_), inp["ada_table"][l, mm_])
        for s in range(2):
            put("lng%d_%d" % (l, s), inp["ln_g"][l, s])
            put("lnb%d_%d" % (l, s), inp["ln_b"][l, s])
    for j in range(2):
        put("ssmd%d" % j, inp["ssm_d"][j])
        bqkv = np.asarray(inp["attn_b_qkv"][j], np.float32)
        put("bq%d" % j, bqkv[:D])
        bk = bqkv[D:D + 512].reshape(8, 64)
        put("bk%d" % j, np.concatenate([bk, bk], axis=1).reshape(-1))
    m["vecs"] = vec
    m["wada"] = np.ascontiguousarray(np.asarray(inp["w_ada"], np.float32))
    for l in layers:
        m["win%d" % l] = _pieces(np.asarray(inp["ffn_w_in"][l], np.float32))
        wd = np.asarray(inp["ffn_w_down"][l], np.float32)
        a = wd.reshape(2, 43, 128, 32, 128).transpose(3, 0, 2, 1, 4)
        m["wdn%d" % l] = np.ascontiguousarray(a).reshape(64, 128, 43 * 128)
        j = l // 2
        if l % 2 == 0:
            m["wval%d" % j] = _pieces(np.asarray(inp["ssm_w_val"][j], np.float32))
            m["wgate%d" % j] = _pieces(np.asarray(inp["ssm_w_gate"][j], np.float32))
            for nm, key in (("bpr", "ssm_b_re"), ("bpi", "ssm_b_im")):
                bsrc = np.asarray(inp[key][j], np.float32)
                out = np.zeros((128, 128, 128), np.float32)
                for pair in range(128):
                    q = pair % 4
                    for g2 in range(2):
                        g = 2 * pair + g2
                        r0 = (2 * q + g2) * 16
                        out[pair, r0:r0 + 16, g2 * 64:(g2 + 1) * 64] = bsrc[g].T
                m["%s%d" % (nm, j)] = out
            for nm, key in (("cpr", "ssm_c_re"), ("cpi", "ssm_c_im")):
                csrc = np.asarray(inp[key][j], np.float32)
                out = np.zeros((128, 128, 128), np.float32)
                for pair in range(128):
                    q = pair % 4
                    for g2 in range(2):
                        g = 2 * pair + g2
                        c0 = (2 * q + g2) * 16
                        out[pair, g2 * 64:(g2 + 1) * 64, c0:c0 + 16] = csrc[g].T
                m["%s%d" % (nm, j)] = out
            are = np.asarray(inp["ssm_a_re"][j], np.float32).reshape(128, 128).T
            aim = np.asarray(inp["ssm_a_im"][j], np.float32).reshape(128, 128).T
            ldt = np.repeat(np.asarray(inp["ssm_log_dt"][j], np.float32).reshape(128, 2, 1), 64, axis=2)
            ldt = ldt.reshape(128, 128).T
            m["ssc%d" % j] = np.ascontiguousarray(np.concatenate([are, aim, ldt], axis=1))
        else:
            wqkv = np.asarray(inp["attn_w_qkv"][j], np.float32)
            m["wq%d" % j] = _pieces(wqkv[:, :D])
            wk = wqkv[:, D:D + 512].reshape(D, 8, 64)
            wk2 = np.concatenate([wk, wk], axis=2).reshape(D, 1024)
            m["wk%d" % j] = _pieces(wk2)
            m["wv%d" % j] = _pieces(wqkv[:, D + 512:D + 1024])
            m["wo%d" % j] = _pieces(np.asarray(inp["attn_w_o"][j], np.float32))
            bv = np.asarray(inp["attn_b_qkv"][j], np.float32)[D + 512:]
            m["bvrep%d" % j] = np.ascontiguousarray(np.broadcast_to(bv[None, :], (128, 512)))
            sk = np.asarray(inp["attn_sinks"][j], np.float32)
            m["sinkrep%d" % j] = np.ascontiguousarray(np.repeat(sk, 64)[None, :])
    if any(l % 2 == 1 for l in layers):
        rb = np.asarray(inp["rel_bias"], np.float32)
        qi = np.arange(128)[:, None]
        sj = np.arange(256)[None, :]
        dist = qi + 128 - sj
        band = (dist >= 0) & (dist < 128)
        bias = rb[_rel_bucket_np(dist)]
        bias = np.where(band[:, :, None], bias, np.float32(-1e30))
        bt = bias.transpose(1, 2, 0).reshape(2, 128, 8, 8, 128)
        m["biastab"] = np.ascontiguousarray(bt.transpose(2, 0, 1, 3, 4)).reshape(16, 128, 1024)
    return m


class Model:
    def __init__(self, layers, ntok, stop=None):
        self.layers = list(layers)
        self.ntok = ntok
        self.nt = ntok // TT
        self.stop = stop
        P = self.P = Prog()
        self.inp = {}
        self.xT = self.din("xT", [D, ntok], F32)
        self.vecs_d = self.din("vecs", [128, VL.n], F32)
        self.wada = self.din("wada", [D, 6 * D], F32)
        for l in self.layers:
            self.din("win%d" % l, [172, 128, 4096], F32)
            self.din("wdn%d" % l, [64, 128, 43 * 128], F32)
            j = l // 2
            if l % 2 == 0:
                for nm in ("wval", "wgate"):
                    self.din("%s%d" % (nm, j), [32, 128, 4096], F32)
                for nm in ("bpr", "bpi", "cpr", "cpi"):
                    self.din("%s%d" % (nm, j), [128, 128, 128], F32)
                self.din("ssc%d" % j, [128, 384], F32)
            else:
                self.din("wq%d" % j, [32, 128, 4096], F32)
                self.din("wk%d" % j, [8, 128, 4096], F32)
                self.din("wv%d" % j, [4, 128, 4096], F32)
                self.din("wo%d" % j, [32, 128, 4096], F32)
                self.din("bvrep%d" % j, [128, 512], F32)
                self.din("sinkrep%d" % j, [1, 4096], F32)
        if any(l % 2 == 1 for l in self.layers):
            self.din("biastab", [16, 128, 1024], F32)
        self.outT = P.dram("outT", [D, ntok], F32, kind="ExternalOutput")
        self.xA = P.dram("xA", [D, ntok], F32)
        self.xB = P.dram("xB", [D, ntok], F32)
        self.zT = P.dram("zT", [D, TT], F32)
        self.uT = P.dram("uT", [D, TT], BF16)
        self.hidT = P.dram("hidT", [DFF, TT], BF16)
        self.modsD = P.dram("modsD", [1, 6 * D], F32)
        self.tabs = P.dram("tabs", [128, 128, 2048], F32)
        self.cpp = P.dram("cpp", [128, 128, 384], BF16)
        self.qTd = P.dram("qTd", [D, TT], BF16)
        self.kTd = P.dram("kTd", [8, 128, TT + 128], BF16)
        self.vpd = P.dram("vpd", [128, 9 * 8 * 2 * 128], BF16)
        self.vecs = P.sb("vecs_sb", [128, VL.n], F32)
        self.mods = P.sb("mods_sb", [128, DEPTH * 6 * 32], F32)
        self.ones_ln = P.sb("ones_ln", [128, 128], F32)
        self.consts = P.sb("consts", [128, 8], F32)
        self.st_re = P.sb("st_re", [128, 128], F32)
        self.st_im = P.sb("st_im", [128, 128], F32)
        self.psall = P.ps("psall", [128, 4096], F32)
        self.ps = [self.psall.view(self.psall[:, i * 512:(i + 1) * 512], "psb%d" % i) for i in range(8)]
        self.psw = [self.psall.view(self.psall[:, i * 1024:(i + 1) * 1024], "psw%d" % i) for i in range(2)]

    def din(self, name, shape, dtype):
        b = self.P.dram(name, shape, dtype, kind="ExternalInput")
        self.inp[name] = b
        return b

    def vcol(self, name, i=0, n=1):
        o = VL.off[name] + i
        return self.vecs[:, o:o + n]

    def mcol(self, l, m, i=0, n=1):
        o = (l * 6 + m) * 32 + i
        return self.mods[:, o:o + n]

    def setup(self):
        P = self.P
        P.dma("sp", self.vecs, self.vecs.ap(), self.vecs_d, self.vecs_d.ap())
        memset(P, "dve", self.ones_ln, 1.0 / D)
        memset(P, "dve", (self.consts, self.consts[:, 0:1]), math.pi / 2)
        memset(P, "dve", (self.consts, self.consts[:, 1:2]), EPS_P)
        memset(P, "dve", (self.consts, self.consts[:, 2:3]), 0.0)

    def phase_mods(self):
        P = self.P
        with P.scope():
            scs = P.sb("scs", [128, 32], F32)
            wts = [P.sb("wadat%d" % i, [128, 4096], F32) for i in range(3)]
            row = P.sb("modrow", [1, 4096], F32)
            msh = P.sb("msh", [128, 192], F32)
            act(P, scs, (self.vecs, self.vcol("c", 0, 32)), AF.Silu)
            n = 0
            for cg in range(6):
                for kc in range(KC):
                    wt = wts[n % 3]
                    n += 1
                    P.dma("sp", wt, wt.ap(), self.wada, self.wada[kc * 128:(kc + 1) * 128, cg * D:(cg + 1) * D])
                    for b in range(8):
                        mm(P, (self.ps[b], self.ps[b][0:1, :]), (scs, scs[:, kc:kc + 1]),
                           (wt, wt[:, b * 512:(b + 1) * 512]), kc == 0, kc == KC - 1)
                for b in range(8):
                    cp(P, "act" if b % 2 else "dve", (row, row[0:1, b * 512:(b + 1) * 512]),
                       (self.ps[b], self.ps[b][0:1, :]))
                P.dma("sp", self.modsD, self.modsD[0:1, cg * D:(cg + 1) * D], row, row.ap())
            P.dma("sp", msh, msh.ap(), self.modsD, self.modsD.ap().rearrange("o (j p) -> p (o j)", p=128),
                  allow_slow_non_contiguous=True)
            o = VL.off["bada0"]
            tt(P, "dve", msh, msh, (self.vecs, self.vecs[:, o:o + 192]), ALU.add)
            for l in range(DEPTH):
                o = VL.off["tab%d_0" % l]
                ml = (self.mods, self.mods[:, l * 192:(l + 1) * 192])
                tt(P, "dve", ml, msh, (self.vecs, self.vecs[:, o:o + 192]), ALU.add)
                for m in (1, 4):
                    mc = (self.mods, self.mcol(l, m, 0, 32))
                    ts(P, "dve", mc, mc, 1.0, None, ALU.add)
                for m in (2, 5):
                    mc = (self.mods, self.mcol(l, m, 0, 32))
                    ts(P, "dve", mc, mc, 1.0, 1.0 / ALPHA, ALU.add, ALU.mult)

    def ln_finalize(self, xout, t0, half, st_mu, st_e2, l, s):
        P = self.P
        with P.scope():
            mu = P.sb("ln_mu", [128, HT], F32)
            rstd = P.sb("ln_rstd", [128, HT], F32)
            tmp = P.sb("ln_tmp", [128, HT], F32)
            zc = [P.sb("ln_z%d" % i, [128, HT], F32) for i in range(4)]
            oc_ = [P.sb("ln_o%d" % i, [128, HT], F32) for i in range(3)]
            c0 = t0 + half * HT

            def load(oc):
                z = zc[oc % 4]
                P.dma("sp", z, z.ap(), self.zT, self.zT[oc * 128:(oc + 1) * 128, half * HT:(half + 1) * HT])

            load(0)
            load(1)
            load(2)
            cp(P, "act", mu, st_mu)
            tt(P, "dve", tmp, mu, mu, ALU.mult)
            tt(P, "dve", tmp, st_e2, tmp, ALU.subtract)
            act(P, tmp, tmp, AF.Sqrt, bias=self.consts[:, 1:2], extra_reads=[self.consts])
            P.op("dve", lambda e: e.reciprocal(out=rstd.ap(), in_=tmp.ap()), reads=[tmp], writes=[rstd])
            for oc in range(KC):
                z = zc[oc % 4]
                o = oc_[oc % 3]
                tt(P, "dve", z, z, mu, ALU.subtract)
                tt(P, "dve", z, z, rstd, ALU.mult)
                if oc + 3 < KC:
                    load(oc + 3)
                act(P, o, z, AF.Identity, scale=self.vcol("lng%d_%d" % (l, s), oc), bias=self.vcol("lnb%d_%d" % (l, s), oc),
                    extra_reads=[self.vecs])
                P.dma("act", xout, xout[oc * 128:(oc + 1) * 128, c0:c0 + HT], o, o.ap())

    def z_epilogue(self, y, xin, t0, half, oc, l, gm, zts, sqs, xcs, stats, idx, pre=None):
        P = self.P
        zt = zts[idx % len(zts)]
        sq = sqs[idx % len(sqs)]
        xc = xcs[idx % len(xcs)]
        c0 = t0 + half * HT
        P.dma("sp", xc, xc.ap(), xin, xin[oc * 128:(oc + 1) * 128, c0:c0 + HT])
        stt(P, zt, y, self.mcol(l, gm, oc), xc, ALU.mult, ALU.add, extra_reads=[self.mods])
        act(P, sq, zt, AF.Square)
        mm(P, stats[0], self.ones_ln, zt, oc == 0, oc == KC - 1)
        mm(P, stats[1], self.ones_ln, sq, oc == 0, oc == KC - 1)
        P.dma("act", self.zT, self.zT[oc * 128:(oc + 1) * 128, half * HT:(half + 1) * HT], zt, zt.ap())

    def build_h(self, hT, xin, c_lo, ncol, l, m_shift, m_scale, xcs, zero_first=0):
        P = self.P
        views = []
        for kc in range(KC):
            xc = xcs[kc % len(xcs)]
            hv = hT.view(hT[:, kc, :], "h_kc%d" % kc)
            views.append(hv)
            if zero_first:
                memset(P, "pool", (hv, hT[:, kc, 0:zero_first]), 0.0)
            lo = c_lo + zero_first
            n = ncol - zero_first
            P.dma("sp", xc, xc[:, 0:n], xin, xin[kc * 128:(kc + 1) * 128, lo:lo + n])
            ts(P, "dve" if kc % 2 else "pool", (hv, hT[:, kc, zero_first:ncol]), (xc, xc[:, 0:n]),
               self.mcol(l, m_scale, kc), self.mcol(l, m_shift, kc), ALU.mult, ALU.add, extra_reads=[self.mods])
        return views

    def phase_ffn_up(self, l, xin, t0):
        P = self.P
        win = self.inp["win%d" % l]
        with P.scope():
            hT = P.sb("f1_hT", [128, KC, TT], BF16)
            xcs = [P.sb("f1_xc%d" % i, [128, TT], F32) for i in range(2)]
            wg = [P.sb("f1_wg%d" % i, [128, 4096], BF16) for i in range(3)]
            wu = [P.sb("f1_wu%d" % i, [128, 4096], BF16) for i in range(3)]
            sg = [P.sb("f1_sg%d" % i, [128, HT], F32) for i in range(2)]
            hid = [P.sb("f1_hid%d" % i, [128, HT], BF16) for i in range(3)]
            hv = self.build_h(hT, xin, t0, TT, l, 3, 4, xcs)
            it = 0
            for j in range(NJ):
                g_t = wg[j % 3]
                u_t = wu[j % 3]
                P.dma("pool", g_t, g_t.ap(), win, win[j], max_dma_last_dim=8192)
                P.dma("pool", u_t, u_t.ap(), win, win[NJ + j], max_dma_last_dim=8192)
                for half in range(2):
                    pg = self.ps[(it % 2) * 2]
                    pu = self.ps[(it % 2) * 2 + 1]
                    for kc in range(KC):
                        mm(P, pg, (g_t, g_t[:, kc * 128:(kc + 1) * 128]), (hv[kc], hT[:, kc, half * HT:(half + 1) * HT]),
                           kc == 0, kc == KC - 1)
                    for kc in range(KC):
                        mm(P, pu, (u_t, u_t[:, kc * 128:(kc + 1) * 128]), (hv[kc], hT[:, kc, half * HT:(half + 1) * HT]),
                           kc == 0, kc == KC - 1)
                    s_ = sg[it % 2]
                    h_ = hid[it % 3]
                    act(P, s_, pg, AF.Silu)
                    tt(P, "dve", h_, pu, s_, ALU.mult)
                    P.dma("act", self.hidT, self.hidT[j * 128:(j + 1) * 128, half * HT:(half + 1) * HT], h_, h_.ap())
                    it += 1

    def phase_ffn_down(self, l, xin, xout, t0):
        P = self.P
        wdn = self.inp["wdn%d" % l]
        for half in range(2):
            with P.scope():
                hid = P.sb("f2_hid", [128, NJ, HT], BF16)
                wp = [P.sb("f2_w%d" % i, [128, 43 * 128], BF16) for i in range(3)]
                zts = [P.sb("f2_z%d" % i, [128, HT], F32) for i in range(2)]
                sqs = [P.sb("f2_q%d" % i, [128, HT], F32) for i in range(2)]
                xcs = [P.sb("f2_x%d" % i, [128, HT], F32) for i in range(2)]
                hvs = []
                for j0 in range(0, NJ, 8):
                    j1 = min(NJ, j0 + 8)
                    v = hid.view(hid[:, j0:j1, :], "hidv%d" % j0)
                    P.dma("sp", v, hid[:, j0:j1, :], self.hidT,
                          self.hidT[j0 * 128:j1 * 128, half * HT:(half + 1) * HT].rearrange("(j p) n -> p j n", p=128))
                    hvs.append(v)
                stats = (self.ps[6], self.ps[7])
                for oc in range(KC):
                    py = self.ps[oc % 2]
                    for kh in range(2):
                        w = wp[(oc * 2 + kh) % 3]
                        P.dma("pool", w, w.ap(), wdn, wdn[oc * 2 + kh], max_dma_last_dim=8192)
                        for k in range(43):
                            j = kh * 43 + k
                            mm(P, py, (w, w[:, k * 128:(k + 1) * 128]), (hvs[j // 8], hid[:, j, :]),
                               j == 0, j == NJ - 1)
                    self.z_epilogue(py, xin, t0, half, oc, l, 5, zts, sqs, xcs, stats, oc)
                self.ln_finalize(xout, t0, half, stats[0], stats[1], l, 1)

    def _range_reduce(self, eng_ang, out, ang, kk):
        P = self.P
        ts(P, "dve", kk, ang, 1.0 / TWO_PI, MAGIC, ALU.mult, ALU.add)
        ts(P, "dve", kk, kk, -MAGIC, None, ALU.add)
        stt(P, out, kk, -CW1, ang, ALU.mult, ALU.add)
        stt(P, out, kk, -CW2, out, ALU.mult, ALU.add)
        ts(P, "pool", out, out, PI_LO, -PI_LO, ALU.min, ALU.max)

    def _sincos(self, sn, cs, r, tmp):
        P = self.P
        act(P, sn, r, AF.Sin)
        act(P, tmp, r, AF.Abs)
        act(P, cs, tmp, AF.Sin, scale=-1.0, bias=self.consts[:, 0:1], extra_reads=[self.consts])

    def ssm_layer(self, l, xin, xout):
        P = self.P
        j = l // 2
        with P.scope():
            rmag = P.sb("s_rmag", [128, 128], F32)
            with P.scope():
                ssc = P.sb("s_ssc", [128, 384], F32)
                P.dma("sp", ssc, ssc.ap(), self.inp["ssc%d" % j], self.inp["ssc%d" % j].ap())
                a_re = (ssc, ssc[:, 0:128])
                a_im = (ssc, ssc[:, 128:256])
                nm = ["dt", "th", "kk", "thr", "sn1", "cs1", "tmp", "abre", "abim", "den", "nr", "kre", "kim", "nkim",
                      "nkre", "t2"]
                T = {n: P.sb("s_" + n, [128, 128], F32) for n in nm}
                act(P, T["dt"], (ssc, ssc[:, 256:384]), AF.Exp)
                tt(P, "dve", T["tmp"], a_re, T["dt"], ALU.mult)
                act(P, rmag, T["tmp"], AF.Exp)
                tt(P, "dve", T["th"], a_im, T["dt"], ALU.mult)
                self._range_reduce("dve", T["thr"], T["th"], T["kk"])
                self._sincos(T["sn1"], T["cs1"], T["thr"], T["tmp"])
                tt(P, "dve", T["abre"], rmag, T["cs1"], ALU.mult)
                tt(P, "dve", T["abim"], rmag, T["sn1"], ALU.mult)
                tt(P, "dve", T["den"], a_re, a_re, ALU.mult)
                tt(P, "dve", T["tmp"], a_im, a_im, ALU.mult)
                tt(P, "dve", T["den"], T["den"], T["tmp"], ALU.add)
                P.op("dve", lambda e: e.reciprocal(out=T["den"].ap(), in_=T["den"].ap()), reads=[T["den"]],
                     writes=[T["den"]])
                ts(P, "dve", T["nr"], T["abre"], -1.0, None, ALU.add)
                tt(P, "dve", T["kre"], T["nr"], a_re, ALU.mult)
                tt(P, "dve", T["tmp"], T["abim"], a_im, ALU.mult)
                tt(P, "dve", T["kre"], T["kre"], T["tmp"], ALU.add)
                tt(P, "dve", T["kre"], T["kre"], T["den"], ALU.mult)
                tt(P, "dve", T["kim"], T["abim"], a_re, ALU.mult)
                tt(P, "dve", T["tmp"], T["nr"], a_im, ALU.mult)
                tt(P, "dve", T["kim"], T["kim"], T["tmp"], ALU.subtract)
                tt(P, "dve", T["kim"], T["kim"], T["den"], ALU.mult)
                ts(P, "dve", T["nkim"], T["kim"], -1.0, None, ALU.mult)
                ts(P, "dve", T["nkre"], T["kre"], -1.0, None, ALU.mult)
                cr = [P.sb("s_cr%d" % i, [128, 128], F32) for i in range(2)]
                ci = [P.sb("s_ci%d" % i, [128, 128], F32) for i in range(2)]
                c1 = [P.sb("s_c1%d" % i, [128, 128], F32) for i in range(2)]
                cb = [P.sb("s_cb%d" % i, [128, 384], BF16) for i in range(2)]
                cpr = self.inp["cpr%d" % j]
                cpi = self.inp["cpi%d" % j]
                for pair in range(128):
                    a, b, c_, o = cr[pair % 2], ci[pair % 2], c1[pair % 2], cb[pair % 2]
                    P.dma("sp", a, a.ap(), cpr, cpr[pair])
                    P.dma("sp", b, b.ap(), cpi, cpi[pair])
                    pc = slice(pair, pair + 1)
                    ts(P, "pool", c_, a, T["kre"][:, pc], None, ALU.mult, extra_reads=[T["kre"]])
                    stt(P, (o, o[:, 0:128]), b, T["nkim"][:, pc], c_, ALU.mult, ALU.add, extra_reads=[T["nkim"]])
                    ts(P, "pool", c_, a, T["nkim"][:, pc], None, ALU.mult, extra_reads=[T["nkim"]])
                    stt(P, (o, o[:, 128:256]), b, T["nkre"][:, pc], c_, ALU.mult, ALU.add, extra_reads=[T["nkre"]])
                    ts(P, "dve", (o, o[:, 256:384]), (o, o[:, 0:128]), -1.0, None, ALU.mult)
                    P.dma("sp", self.cpp, self.cpp[pair], o, o.ap())
                io_i = P.sb("s_ioi", [128, TT], I32)
                io_f = P.sb("s_iof", [128, TT], F32)
                P.op("pool", lambda e: e.iota(io_i.ap(), pattern=[[1, TT]], base=1, channel_multiplier=0), writes=[io_i])
                cp(P, "dve", io_f, io_i)
                ang = [P.sb("s_ang%d" % i, [128, TT], F32) for i in range(2)]
                kk = [P.sb("s_kk%d" % i, [128, TT], F32) for i in range(2)]
                rr = [P.sb("s_rr%d" % i, [128, TT], F32) for i in range(2)]
                tb = [P.sb("s_tb%d" % i, [128, 2, TT], F32) for i in range(2)]
                for pair in range(128):
                    a, k_, r_, t_ = ang[pair % 2], kk[pair % 2], rr[pair % 2], tb[pair % 2]
                    ts(P, "pool", a, io_f, T["thr"][:, pair:pair + 1], None, ALU.mult, extra_reads=[T["thr"]])
                    self._range_reduce("dve", r_, a, k_)
                    act(P, (t_, t_[:, 1, :]), r_, AF.Sin)
                    act(P, a, r_, AF.Abs)
                    act(P, (t_, t_[:, 0, :]), a, AF.Sin, scale=-1.0, bias=self.consts[:, 0:1], extra_reads=[self.consts])
                    P.dma("sp", self.tabs, self.tabs[pair], t_, t_.ap().rearrange("p a n -> p (a n)"))
            memset(P, "dve", self.st_re, 0.0)
            memset(P, "dve", self.st_im, 0.0)
            for t in range(self.nt):
                self.phase_ssm_tile(l, xin, t * TT, rmag)
                self.phase_ssm_proj(l, xin, xout, t * TT)

    def phase_ssm_tile(self, l, xin, t0, rmag):
        P = self.P
        j = l // 2
        bpr = self.inp["bpr%d" % j]
        bpi = self.inp["bpi%d" % j]
        with P.scope():
            xcs = [P.sb("st_xc%d" % i, [128, TT], F32) for i in range(2)]
            h32s = [P.sb("st_h32%d" % i, [128, TT], F32) for i in range(2)]
            h16s = [P.sb("st_h16%d" % i, [128, TT], BF16) for i in range(2)]
            bps = [P.sb("st_bp%d" % i, [128, 8, 128], BF16) for i in range(2)]
            cpts = [P.sb("st_cp%d" % i, [128, 4, 384], BF16) for i in range(2)]
            tabs = [P.sb("st_tab%d" % i, [128, 2, TT], F32) for i in range(2)]
            rmats = [P.sb("st_rm%d" % i, [128, TT], F32) for i in range(2)]
            ones = P.sb("st_ones", [128, TT], F32)
            memset(P, "dve", ones, 1.0)
            mk = lambda nm, dt=F32, n=2, w=TT: [P.sb("st_%s%d" % (nm, i), [128, w], dt) for i in range(n)]
            t1, t2, t3, t4 = mk("t1", n=1)[0], mk("t2", n=1)[0], mk("t3", n=1)[0], mk("t4", n=1)[0]
            dsets = [[P.sb("st_dd%d_%d" % (a, b), [128, TT], BF16) for b in range(4)] for a in range(2)]
            utr, uti = mk("utr"), mk("uti")
            srs, sis = mk("sr"), mk("si")
            yv, ga, gb = mk("yv", w=HT), mk("ga", w=HT), mk("gb", w=HT)
            u16 = mk("u16", BF16, 3, HT)
            sm = P.sb("st_sm", [128, 8], F32)
            ure, uim = self.psw[0], self.psw[1]

            def prep(dc):
                xc, h32, h16 = xcs[dc % 2], h32s[dc % 2], h16s[dc % 2]
                bp, cpt = bps[dc % 2], cpts[dc % 2]
                P.dma("sp", xc, xc.ap(), xin, xin[dc * 128:(dc + 1) * 128, t0:t0 + TT])
                ts(P, "dve", h32, xc, self.mcol(l, 1, dc), self.mcol(l, 0, dc), ALU.mult, ALU.add, extra_reads=[self.mods])
                cp(P, "act", h16, h32)
                P.dma("pool", bp, bp[:, 0:4, :], bpr, bpr[dc * 4:(dc + 1) * 4].rearrange("q p c -> p q c"))
                P.dma("pool", bp, bp[:, 4:8, :], bpi, bpi[dc * 4:(dc + 1) * 4].rearrange("q p c -> p q c"))
                P.dma("sp", cpt, cpt.ap(), self.cpp, self.cpp[dc * 4:(dc + 1) * 4].rearrange("q p c -> p q c"))

            def load_tab(pair):
                tb = tabs[pair % 2]
                P.dma("sp", tb, tb.ap().rearrange("p a n -> p (a n)"), self.tabs, self.tabs[pair])
                act(P, rmats[pair % 2], ones, AF.Copy, scale=rmag[:, pair:pair + 1], extra_reads=[rmag])

            def bmm(pair):
                dc, q = pair // 4, pair % 4
                bp, h16 = bps[dc % 2], h16s[dc % 2]
                for half in range(2):
                    hs = slice(half * HT, (half + 1) * HT)
                    mm(P, (ure, ure[:, hs]), (bp, bp[:, q, :]), (h16, h16[:, hs]), True, True)
                    mm(P, (uim, uim[:, hs]), (bp, bp[:, 4 + q, :]), (h16, h16[:, hs]), True, True)

            prep(0)
            load_tab(0)
            bmm(0)
            for pair in range(128):
                dc, q = pair // 4, pair % 4
                i2 = pair % 2
                tb, rm = tabs[i2], rmats[i2]
                cpt = cpts[dc % 2]
                yb = (self.ps[4 + 2 * (dc % 2)], self.ps[5 + 2 * (dc % 2)])
                c_ = (tb, tb[:, 0, :])
                s_ = (tb, tb[:, 1, :])
                if q == 0 and dc + 1 < KC:
                    prep(dc + 1)
                tt(P, "dve", t1, ure, c_, ALU.mult)
                tt(P, "dve", t2, uim, s_, ALU.mult)
                tt(P, "dve", t3, uim, c_, ALU.mult)
                tt(P, "dve", t4, ure, s_, ALU.mult)
                if pair + 1 < 128:
                    load_tab(pair + 1)
                    bmm(pair + 1)
                tt(P, "dve", utr[i2], t1, t2, ALU.add)
                tt(P, "dve", uti[i2], t3, t4, ALU.subtract)
                sr, si = srs[i2], sis[i2]
                ini_r, ini_i = self.st_re[:, pair:pair + 1], self.st_im[:, pair:pair + 1]
                P.op("dve", lambda e, sr=sr, rm=rm, u=utr[i2], ini=ini_r: e.tensor_tensor_scan(
                    out=sr.ap(), data0=rm.ap(), data1=u.ap(), initial=ini, op0=ALU.mult, op1=ALU.add),
                    reads=[rm, utr[i2], self.st_re], writes=[sr])
                P.op("dve", lambda e, si=si, rm=rm, u=uti[i2], ini=ini_i: e.tensor_tensor_scan(
                    out=si.ap(), data0=rm.ap(), data1=u.ap(), initial=ini, op0=ALU.mult, op1=ALU.add),
                    reads=[rm, uti[i2], self.st_im], writes=[si])
                dd = dsets[i2]
                tt(P, "pool", dd[0], sr, c_, ALU.mult)
                tt(P, "pool", dd[1], si, s_, ALU.mult)
                tt(P, "pool", dd[2], sr, s_, ALU.mult)
                tt(P, "dve" if pair % 2 else "pool", dd[3], si, c_, ALU.mult)
                for half in range(2):
                    hs = slice(half * HT, (half + 1) * HT)
                    mm(P, yb[half], (cpt, cpt[:, q, 0:128]), (dd[0], dd[0][:, hs]), q == 0, False)
                    mm(P, yb[half], (cpt, cpt[:, q, 256:384]), (dd[1], dd[1][:, hs]), False, False)
                    mm(P, yb[half], (cpt, cpt[:, q, 128:256]), (dd[2], dd[2][:, hs]), False, False)
                    mm(P, yb[half], (cpt, cpt[:, q, 128:256]), (dd[3], dd[3][:, hs]), False, q == 3)
                L = slice(TT - 1, TT)
                smv = lambda k: (sm, sm[:, k:k + 1])
                act(P, smv(0), (sr, sr[:, L]), AF.Copy, scale=tb[:, 0, L], extra_reads=[tb])
                act(P, smv(1), (si, si[:, L]), AF.Copy, scale=tb[:, 1, L], extra_reads=[tb])
                act(P, smv(2), (sr, sr[:, L]), AF.Copy, scale=tb[:, 1, L], extra_reads=[tb])
                act(P, smv(3), (si, si[:, L]), AF.Copy, scale=tb[:, 0, L], extra_reads=[tb])
                act(P, (self.st_re, self.st_re[:, pair:pair + 1]), smv(1), AF.Identity, scale=-1.0, bias=sm[:, 0:1],
                    extra_reads=[sm])
                act(P, (self.st_im, self.st_im[:, pair:pair + 1]), smv(3), AF.Identity, bias=sm[:, 2:3], extra_reads=[sm])
                if q == 3:
                    h32 = h32s[dc % 2]
                    for half in range(2):
                        hs = slice(half * HT, (half + 1) * HT)
                        y_, a_, b_ = yv[half], ga[half], gb[half]
                        uo = u16[(dc * 2 + half) % 3]
                        stt(P, y_, (h32, h32[:, hs]), self.vcol("ssmd%d" % j, dc), yb[half], ALU.mult, ALU.add,
                            extra_reads=[self.vecs])
                        act(P, a_, y_, AF.Square)
                        ts(P, "dve", a_, a_, 0.044715, 1.0, ALU.mult, ALU.add)
                        tt(P, "dve", a_, a_, y_, ALU.mult)
                        act(P, b_, a_, AF.Sigmoid, scale=1.5957691216057308)
                        tt(P, "dve", uo, y_, b_, ALU.mult)
                        P.dma("sp", self.uT, self.uT[dc * 128:(dc + 1) * 128, hs], uo, uo.ap())

    def phase_ssm_proj(self, l, xin, xout, t0):
        P = self.P
        j = l // 2
        wv = self.inp["wval%d" % j]
        wg = self.inp["wgate%d" % j]
        with P.scope():
            u16 = P.sb("sp_u16", [128, KC, TT], BF16)
            wvt = [P.sb("sp_wv%d" % i, [128, 4096], BF16) for i in range(3)]
            wgt = [P.sb("sp_wg%d" % i, [128, 4096], BF16) for i in range(3)]
            mk = lambda nm, n=2: [P.sb("sp_%s%d" % (nm, i), [128, HT], F32) for i in range(n)]
            sgs, tmps, zts, sqs, xcs = mk("sg"), mk("tm"), mk("z"), mk("q"), mk("x")
            uv = []
            for k0 in range(0, KC, 8):
                v = u16.view(u16[:, k0:k0 + 8, :], "u16v%d" % k0)
                P.dma("sp", v, u16[:, k0:k0 + 8, :], self.uT,
                      self.uT[k0 * 128:(k0 + 8) * 128, :].rearrange("(j p) n -> p j n", p=128))
                uv.append(v)
            stats = [(self.ps[4], self.ps[5]), (self.ps[6], self.ps[7])]
            it = 0
            for oc in range(KC):
                a, b = wvt[oc % 3], wgt[oc % 3]
                P.dma("pool", a, a.ap(), wv, wv[oc], max_dma_last_dim=8192)
                P.dma("pool", b, b.ap(), wg, wg[oc], max_dma_last_dim=8192)
                for half in range(2):
                    hs = slice(half * HT, (half + 1) * HT)
                    pv, pg = self.ps[(it % 2) * 2], self.ps[(it % 2) * 2 + 1]
                    for kc in range(KC):
                        mm(P, pv, (a, a[:, kc * 128:(kc + 1) * 128]), (uv[kc // 8], u16[:, kc, hs]), kc == 0, kc == KC - 1)
                    for kc in range(KC):
                        mm(P, pg, (b, b[:, kc * 128:(kc + 1) * 128]), (uv[kc // 8], u16[:, kc, hs]), kc == 0, kc == KC - 1)
                    sg, tm = sgs[it % 2], tmps[it % 2]
                    act(P, sg, pg, AF.Sigmoid)
                    tt(P, "dve", tm, pv, sg, ALU.mult)
                    self.z_epilogue(tm, xin, t0, half, oc, l, 2, zts, sqs, xcs, stats[half], it)
                    it += 1
            for half in range(2):
                self.ln_finalize(xout, t0, half, stats[half][0], stats[half][1], l, 0)

    def attn_layer(self, l, xin, xout):
        for t in range(self.nt):
            import os as _os
            st_ = _os.environ.get("ATT_STOP", "")
            self.phase_attn_qkv(l, xin, t * TT, t == 0)
            if st_ == "qkv":
                continue
            self.phase_attn_core(l, t * TT, t == 0)
            if st_ == "core":
                continue
            self.phase_attn_out(l, xin, xout, t * TT)

    def phase_attn_qkv(self, l, xin, t0, first):
        P = self.P
        j = l // 2
        wq, wk, wv = self.inp["wq%d" % j], self.inp["wk%d" % j], self.inp["wv%d" % j]
        NC = TT + 128
        with P.scope():
            hT = P.sb("aq_hT", [128, KC, NC], BF16)
            xcs = [P.sb("aq_xc%d" % i, [128, NC], F32) for i in range(2)]
            wr = [P.sb("aq_w%d" % i, [128, 4096], BF16) for i in range(3)]
            wvs = [P.sb("aq_wv%d" % i, [128, 4096], BF16) for i in range(4)]
            q16 = [P.sb("aq_q%d" % i, [128, HT], BF16) for i in range(3)]
            vt = [P.sb("aq_vt%d" % i, [128, 512], F32) for i in range(2)]
            vpad = [P.sb("aq_vp%d" % i, [128, 8, 2, 128], BF16) for i in range(2)]
            bv = P.sb("aq_bv", [128, 512], F32)
            P.dma("sp", bv, bv.ap(), self.inp["bvrep%d" % j], self.inp["bvrep%d" % j].ap())
            for v_ in vpad:
                memset(P, "pool", v_, 0.0)
            hv = self.build_h(hT, xin, t0 - 128, NC, l, 0, 1, xcs, zero_first=128 if first else 0)
            for vp_ in range(4):
                P.dma("pool", wvs[vp_], wvs[vp_].ap(), wv, wv[vp_], max_dma_last_dim=8192)
            it = 0
            for oc in range(KC):
                w = wr[oc % 3]
                P.dma("pool", w, w.ap(), wq, wq[oc], max_dma_last_dim=8192)
                for half in range(2):
                    pb = self.ps[it % 4]
                    c0 = 128 + half * HT
                    for kc in range(KC):
                        mm(P, pb, (w, w[:, kc * 128:(kc + 1) * 128]), (hv[kc], hT[:, kc, c0:c0 + HT]), kc == 0, kc == KC - 1)
                    qo = q16[it % 3]
                    act(P, qo, pb, AF.Identity, bias=self.vcol("bq%d" % j, oc), extra_reads=[self.vecs])
                    P.dma("sp", self.qTd, self.qTd[oc * 128:(oc + 1) * 128, half * HT:(half + 1) * HT], qo, qo.ap())
                    it += 1
            for g in range(NKV):
                w = wr[(KC + g) % 3]
                P.dma("pool", w, w.ap(), wk, wk[g], max_dma_last_dim=8192)
                for (c0, n) in ((0, 512), (512, 512), (1024, 128)):
                    pb = self.ps[it % 4]
                    for kc in range(KC):
                        mm(P, (pb, pb[:, 0:n]), (w, w[:, kc * 128:(kc + 1) * 128]), (hv[kc], hT[:, kc, c0:c0 + n]),
                           kc == 0, kc == KC - 1)
                    ko = q16[it % 3]
                    act(P, (ko, ko[:, 0:n]), (pb, pb[:, 0:n]), AF.Identity, bias=self.vcol("bk%d" % j, g),
                        extra_reads=[self.vecs])
                    P.dma("sp", self.kTd, self.kTd[g, :, c0:c0 + n], ko, ko[:, 0:n])
                    it += 1
            for blk in range(9):
                pb = self.ps[4 + blk % 4]
                for vp_ in range(4):
                    for kc in range(KC):
                        mm(P, (pb, pb[:, vp_ * 128:(vp_ + 1) * 128]), (hv[kc], hT[:, kc, blk * 128:(blk + 1) * 128]),
                           (wvs[vp_], wvs[vp_][:, kc * 128:(kc + 1) * 128]), kc == 0, kc == KC - 1)
                v32 = vt[blk % 2]
                vp16 = vpad[blk % 2]
                tt(P, "dve", v32, pb, bv, ALU.add)
                src = v32.ap().rearrange("p (g d) -> p g d", g=8)
                cp(P, "act", (vp16, vp16[:, :, 0, 0:64]), (v32, src))
                cp(P, "dve", (vp16, vp16[:, :, 1, 64:128]), (v32, src))
                P.dma("sp", self.vpd, self.vpd[:, blk * 2048:(blk + 1) * 2048], vp16,
                      vp16.ap().rearrange("p g h c -> p (g h c)"))

    def phase_attn_core(self, l, t0, first):
        P = self.P
        j = l // 2
        NC = TT + 128
        bias_d = self.inp["biastab"]
        with P.scope():
            kT2 = P.sb("ac_k", [128, 8, NC], BF16)
            vp = P.sb("ac_v", [128, 9, 8, 2, 128], BF16)
            es = P.sb("ac_es", [1, 4096], F32)
            ones1 = P.sb("ac_one", [1, 128], F32)
            onesp = P.sb("ac_onep", [128, 2, 128], BF16)
            qgs = [P.sb("ac_q%d" % i, [128, 4, TT], BF16) for i in range(2)]
            bts = [P.sb("ac_b%d" % i, [128, 2, TT], F32) for i in range(2)]
            lts = [P.sb("ac_l%d" % i, [128, 512], F32) for i in range(2)]
            Es = [P.sb("ac_E%d" % i, [128, 512], BF16) for i in range(8)]
            recs = [P.sb("ac_r%d" % i, [128, 512], F32) for i in range(2)]
            o16s = [P.sb("ac_o%d" % i, [128, 512], BF16) for i in range(2)]
            P.dma("sp", kT2, kT2.ap(), self.kTd, self.kTd.ap().rearrange("g p n -> p g n"))
            P.dma("sp", vp, vp.ap().rearrange("p b g h c -> p (b g h c)"), self.vpd, self.vpd.ap())
            P.dma("sp", es, es.ap(), self.inp["sinkrep%d" % j], self.inp["sinkrep%d" % j].ap())
            act(P, es, es, AF.Exp)
            memset(P, "dve", ones1, 1.0)
            memset(P, "dve", onesp, 0.0)
            memset(P, "dve", (onesp, onesp[:, 0, 0:64]), 1.0)
            memset(P, "dve", (onesp, onesp[:, 1, 64:128]), 1.0)
            sit = 0
            nit = 0
            for g in range(NKV):
                qg, bt = qgs[g % 2], bts[g % 2]
                P.dma("sp", qg, qg.ap(), self.qTd, self.qTd[g * 512:(g + 1) * 512, :].rearrange("(c p) n -> p c n", p=128))
                P.dma("sp", bt, bt.ap(), bias_d, bias_d[2 * g:2 * g + 2].rearrange("c p n -> p c n"))
                for n in range(8):
                    chunks = []
                    if not (first and n == 0):
                        chunks.append((0, n))
                    chunks.append((1, n + 1))
                    E = {}
                    eset = (nit % 2) * 4
                    for (ci, kb) in chunks:
                        for hf in range(2):
                            psb = self.ps[sit % 4]
                            lt = lts[sit % 2]
                            sit += 1
                            rows = slice(64 * hf, 64 * hf + 64)
                            for h4 in range(4):
                                mm(P, (psb, psb[:, h4 * 128:(h4 + 1) * 128]), (kT2, kT2[rows, g, kb * 128:(kb + 1) * 128]),
                                   (qg, qg[rows, h4, n * 128:(n + 1) * 128]), True, True)
                            btv = bt[:, ci, :].rearrange("p (a b q) -> p a b q", a=4, b=2)[:, :, hf, :]
                            stt(P, (lt, lt.ap().rearrange("p (a q) -> p a q", a=4)),
                                (psb, psb.ap().rearrange("p (a q) -> p a q", a=4)), 0.125, (bt, btv), ALU.mult, ALU.add)
                            Et = Es[eset + ci * 2 + hf]
                            act(P, Et, lt, AF.Exp)
                            E[(ci, hf)] = Et
                    num = self.ps[4 + 2 * (nit % 2)]
                    den = self.ps[5 + 2 * (nit % 2)]
                    for pr in range(4):
                        cs = slice(pr * 128, (pr + 1) * 128)
                        firstmm = True
                        for (ci, kb) in chunks:
                            for hf in range(2):
                                Ea = (E[(ci, hf)], E[(ci, hf)][:, pr * 128:(pr + 1) * 128])
                                mm(P, (num, num[:, cs]), (vp, vp[:, kb, g, hf, :]), Ea, firstmm,
                                   (ci, kb) == chunks[-1] and hf == 1)
                                mm(P, (den, den[:, cs]), (onesp, onesp[:, hf, :]), Ea, firstmm, False)
                                firstmm = False
                        e0 = (g * 4 + pr) * 128
                        import os as _os
                        if not _os.environ.get("ATT_NOSINK"):
                            mm(P, (den, den[:, cs]), (es, es[0:1, e0:e0 + 128]), ones1, False, True)
                    rec = recs[nit % 2]
                    o16 = o16s[nit % 2]
                    P.op("dve", lambda e, rec=rec, den=den: e.reciprocal(out=rec.ap(), in_=den.ap()), reads=[den], writes=[rec])
                    tt(P, "dve", o16, num, rec, ALU.mult)
                    r0 = g * 512
                    P.dma("sp", self.uT, self.uT[r0:r0 + 512, n * 128:(n + 1) * 128].rearrange("(c p) q -> p c q", p=128),
                          o16, o16.ap().rearrange("p (c q) -> p c q", c=4))
                    nit += 1

    def phase_attn_out(self, l, xin, xout, t0):
        P = self.P
        j = l // 2
        wo = self.inp["wo%d" % j]
        with P.scope():
            o16 = P.sb("ao_o16", [128, KC, TT], BF16)
            wr = [P.sb("ao_w%d" % i, [128, 4096], BF16) for i in range(3)]
            mk = lambda nm, n=2: [P.sb("ao_%s%d" % (nm, i), [128, HT], F32) for i in range(n)]
            zts, sqs, xcs = mk("z"), mk("q"), mk("x")
            ov = []
            for k0 in range(0, KC, 8):
                v = o16.view(o16[:, k0:k0 + 8, :], "o16v%d" % k0)
                P.dma("sp", v, o16[:, k0:k0 + 8, :], self.uT,
                      self.uT[k0 * 128:(k0 + 8) * 128, :].rearrange("(j p) n -> p j n", p=128))
                ov.append(v)
            stats = [(self.ps[4], self.ps[5]), (self.ps[6], self.ps[7])]
            it = 0
            for oc in range(KC):
                w = wr[oc % 3]
                P.dma("pool", w, w.ap(), wo, wo[oc], max_dma_last_dim=8192)
                for half in range(2):
                    hs = slice(half * HT, (half + 1) * HT)
                    py = self.ps[it % 4]
                    for kc in range(KC):
                        mm(P, py, (w, w[:, kc * 128:(kc + 1) * 128]), (ov[kc // 8], o16[:, kc, hs]), kc == 0, kc == KC - 1)
                    self.z_epilogue(py, xin, t0, half, oc, l, 2, zts, sqs, xcs, stats[half], it)
                    it += 1
            for half in range(2):
                self.ln_finalize(xout, t0, half, stats[half][0], stats[half][1], l, 0)


def build_model(layers, ntok, plan):
    M = Model(layers, ntok)
    P = M.P
    M.setup()
    M.phase_mods()
    cur = M.xT
    pingpong = [M.xA, M.xB]
    for i, (l, kind) in enumerate(plan):
        last = i == len(plan) - 1
        nxt = M.outT if last else pingpong[i % 2]
        if kind == "ffn":
            for t in range(M.nt):
                M.phase_ffn_up(l, cur, t * TT)
                M.phase_ffn_down(l, cur, nxt, t * TT)
        elif l % 2 == 0:
            M.ssm_layer(l, cur, nxt)
        else:
            M.attn_layer(l, cur, nxt)
        cur = nxt
    P.finish()
    return M


FULL_PLAN = [(l, k) for l in range(DEPTH) for k in ("mix", "ffn")]
N_CORES_USED = 2
_CACHE = {}


def kernel(**inputs):
    x = np.asarray(inputs["x"])
    bsz, seq, _ = x.shape
    layers = list(range(DEPTH))
    key = ("full", seq)
    if key not in _CACHE:
        _CACHE[key] = build_model(layers, seq, FULL_PLAN)
    M = _CACHE[key]
    in_maps = []
    for b in range(bsz):
        im = prep_inputs(inputs, b, layers, seq)
        in_maps.append({k: v for k, v in im.items() if k in M.inp})
    res = run_bass_kernel_spmd(M.P.nc, in_maps, core_ids=list(range(bsz)))
    out = np.stack([np.ascontiguousarray(res.results[b]["outT"].T) for b in range(bsz)], axis=0)
    return out.astype(np.float32)
```

```python
import math
from contextlib import ExitStack, contextmanager

import numpy as np
import ml_dtypes

import concourse.bass as bass
import concourse.mybir as mybir
from concourse.bass_utils import run_bass_kernel_spmd

F32 = mybir.dt.float32
BF16 = mybir.dt.bfloat16
I32 = mybir.dt.int32
AF = mybir.ActivationFunctionType
ALU = mybir.AluOpType
AX = mybir.AxisListType

SEM_ROLL = 30000

D = 4096
KC = 32
DEPTH = 4
DFF = 11008
NJ = 86
TT = 1024
HT = 512
NHEAD = 64
NKV = 8
ALPHA = (2 * DEPTH) ** 0.25
LN_EPS = 1e-5
EPS_P = LN_EPS / (ALPHA * ALPHA)
PI_LO = 3.1415925
TWO_PI = 2.0 * math.pi
CW1 = 6.28125
CW2 = TWO_PI - 6.28125
MAGIC = 12582912.0


class Buf:
    def __init__(self, prog, name, base_ap, space, root=None):
        self.prog = prog
        self.name = name
        self.base = base_ap
        self.space = space
        self.w = {}
        self.r = {}
        self.sem = None
        self.ndma = 0
        self.root = root if root is not None else self

    def ap(self):
        return self.base

    def __getitem__(self, idx):
        return self.base[idx]

    def view(self, ap, name=None):
        return Buf(self.prog, name or self.name + "_v", ap, self.space, root=self.root)


class Prog:
    ENG = ("pe", "act", "dve", "pool", "sp")

    def __init__(self):
        self.nc = bass.Bass("TRN2", target_bir_lowering=False)
        self.es = ExitStack()
        self.rec = {e: [] for e in self.ENG}
        self.cnt = {e: 0 for e in self.ENG}
        self.esem = {}
        self.nsem = 0
        for e in ("pe", "act", "dve", "pool"):
            self.esem[e] = self._newsem(e)
        self.waited = {e: {} for e in self.ENG}
        self.dma_owner = {}
        self.free_dma_sems = []
        self.out_bufs = []
        self.ninst = 0
        self.scopes = []

    def _newsem(self, name):
        self.nsem += 1
        s = self.es.enter_context(self.nc.semaphore("s%d_%s" % (self.nsem, name)))
        return s

    def dram(self, name, shape, dtype, kind="Internal"):
        t = self.nc.dram_tensor(name, list(shape), dtype, kind=kind)
        b = Buf(self, name, t.ap(), "dram")
        if kind == "ExternalOutput":
            self.out_bufs.append(b)
        return b

    def sb(self, name, shape, dtype):
        st = self.scopes[-1][0] if self.scopes else self.es
        self.nsb = getattr(self, "nsb", 0) + 1
        name = "%s_%d" % (name, self.nsb)
        t = st.enter_context(self.nc.sbuf_tensor(name, list(shape), dtype))
        b = Buf(self, name, t[:], "sb")
        if self.scopes:
            self.scopes[-1][1].append(b)
        return b

    def ps(self, name, shape, dtype):
        t = self.es.enter_context(self.nc.psum_tensor(name, list(shape), dtype))
        return Buf(self, name, t[:], "ps")

    @contextmanager
    def scope(self):
        st = ExitStack()
        bufs = []
        self.scopes.append((st, bufs))
        try:
            yield
        finally:
            self.barrier()
            self.scopes.pop()
            for b in bufs:
                if b.sem is not None:
                    self.free_dma_sems.append((b.sem, b.ndma))
                    del self.dma_owner[id(b.sem)]
            st.close()

    def barrier(self):
        tok = {}
        for e in ("pe", "act", "dve", "pool"):
            tok[self.esem[e]] = self.cnt[e]
        for o in list(self.dma_owner.values()):
            tok[o.sem] = 16 * o.ndma
        for e in self.ENG:
            self._emit_waits(e, dict(tok), force_pe=True)

    def _emit_waits(self, eng, tokens, force_pe=False):
        for sem, val in tokens.items():
            owner = self.dma_owner.get(id(sem))
            if owner is not None:
                val = 16 * owner.ndma
            elif eng == "pe" and sem is self.esem["pe"] and not force_pe:
                continue
            if val <= 0:
                continue
            if self.waited[eng].get(id(sem), -1) >= val:
                continue
            self.waited[eng][id(sem)] = val
            self.rec[eng].append(("wait", sem, val))

    @staticmethod
    def _merge(tok, d):
        for s, v in d.items():
            if tok.get(s, -1) < v:
                tok[s] = v

    def op(self, eng, fn, reads=(), writes=()):
        tok = {}
        for b in reads:
            self._merge(tok, b.w)
        for b in writes:
            self._merge(tok, b.w)
            self._merge(tok, b.r)
        self._emit_waits(eng, tok)
        if self.cnt[eng] >= SEM_ROLL:
            self.esem[eng] = self._newsem(eng)
            self.cnt[eng] = 0
        self.cnt[eng] += 1
        sem = self.esem[eng]
        c = self.cnt[eng]
        self.rec[eng].append(("op", fn, sem, 1))
        self.ninst += 1
        for b in reads:
            b.r[sem] = c
        for b in writes:
            b.w = {sem: c}
            b.r = {}

    def _dma_sem(self, owner):
        owner = owner.root
        if owner.sem is None:
            if self.free_dma_sems:
                owner.sem, owner.ndma = self.free_dma_sems.pop()
            else:
                owner.sem = self._newsem("d_" + owner.name)
            self.dma_owner[id(owner.sem)] = owner
        return owner

    def dma(self, q, dst, dst_ap, src, src_ap, **kw):
        tok = {}
        self._merge(tok, src.w)
        self._merge(tok, dst.w)
        self._merge(tok, dst.r)
        self._emit_waits(q, tok)
        owner = dst if dst.space == "sb" else src
        owner = self._dma_sem(owner)
        owner.ndma += 1
        sem = owner.sem
        val = 16 * owner.ndma
        self.rec[q].append(("op", (lambda e: e.dma_start(out=dst_ap, in_=src_ap, **kw)), sem, 16))
        self.ninst += 1
        src.r[sem] = val
        dst.w = {sem: val}
        dst.r = {}

    def finish(self):
        self.barrier()
        engmap = {"pe": "tensor", "act": "scalar", "dve": "vector", "pool": "gpsimd", "sp": "sync"}
        with self.nc.Block() as block:
            for e in self.ENG:
                recs = self.rec[e]

                def body(engine, recs=recs):
                    for r in recs:
                        if r[0] == "wait":
                            engine.wait_ge(r[1], r[2])
                        else:
                            r[1](engine).then_inc(r[2], r[3])

                getattr(block, engmap[e])(body)
        return self.nc


def _ap(x):
    return x[1] if isinstance(x, tuple) else x.ap()


def _bf(x):
    return x[0] if isinstance(x, tuple) else x


def tt(P, eng, out, a, b, op):
    oa, aa, ba = _ap(out), _ap(a), _ap(b)
    P.op(eng, lambda e: e.tensor_tensor(out=oa, in0=aa, in1=ba, op=op), reads=[_bf(a), _bf(b)], writes=[_bf(out)])


def ts(P, eng, out, a, s1, s2, op0, op1=None, extra_reads=()):
    oa, aa = _ap(out), _ap(a)
    if eng == "pool" and op1 is None:
        if op0 == ALU.mult:
            op1, s2 = ALU.add, 0.0
        elif op0 == ALU.add:
            op1, s2 = ALU.mult, 1.0
    if op1 is None:
        P.op(eng, lambda e: e.tensor_scalar(out=oa, in0=aa, scalar1=s1, scalar2=None, op0=op0),
             reads=[_bf(a)] + list(extra_reads), writes=[_bf(out)])
    else:
        P.op(eng, lambda e: e.tensor_scalar(out=oa, in0=aa, scalar1=s1, scalar2=s2, op0=op0, op1=op1),
             reads=[_bf(a)] + list(extra_reads), writes=[_bf(out)])


def stt(P, out, a, s, b, op0, op1, extra_reads=()):
    oa, aa, ba = _ap(out), _ap(a), _ap(b)
    P.op("dve", lambda e: e.scalar_tensor_tensor(out=oa, in0=aa, scalar=s, in1=ba, op0=op0, op1=op1),
         reads=[_bf(a), _bf(b)] + list(extra_reads), writes=[_bf(out)])


def act(P, out, a, func, scale=None, bias=None, extra_reads=()):
    oa, aa = _ap(out), _ap(a)
    kw = {}
    if scale is not None:
        kw["scale"] = scale
    if bias is not None:
        kw["bias"] = bias
    P.op("act", lambda e: e.activation(out=oa, in_=aa, func=func, **kw), reads=[_bf(a)] + list(extra_reads),
         writes=[_bf(out)])


def cp(P, eng, out, a):
    oa, aa = _ap(out), _ap(a)
    if eng == "act":
        P.op("act", lambda e: e.activation(out=oa, in_=aa, func=AF.Copy), reads=[_bf(a)], writes=[_bf(out)])
    else:
        P.op(eng, lambda e: e.tensor_copy(out=oa, in_=aa), reads=[_bf(a)], writes=[_bf(out)])


def mm(P, out, lhsT, rhs, start, stop):
    oa, la, ra = _ap(out), _ap(lhsT), _ap(rhs)
    P.op("pe", lambda e: e.matmul(oa, lhsT=la, rhs=ra, start=start, stop=stop), reads=[_bf(lhsT), _bf(rhs)],
         writes=[_bf(out)])


def memset(P, eng, out, val):
    oa = _ap(out)
    P.op(eng, lambda e: e.memset(oa, val), writes=[_bf(out)])


def _fm(v):
    v = np.asarray(v, np.float32)
    return np.ascontiguousarray(v.reshape(-1, 128).T)


def _pieces(w, ncol=128):
    K, N = w.shape
    a = w.reshape(K // 128, 128, N // ncol, ncol)
    return np.ascontiguousarray(a.transpose(2, 1, 0, 3)).reshape(N // ncol, 128, (K // 128) * ncol)


class VecLayout:
    def __init__(self):
        self.off = {}
        self.n = 0

    def add(self, name, ncols):
        self.off[name] = self.n
        self.n += ncols


def vec_layout():
    L = VecLayout()
    L.add("c", 32)
    for m in range(6):
        L.add("bada%d" % m, 32)
    for l in range(DEPTH):
        for m in range(6):
            L.add("tab%d_%d" % (l, m), 32)
        for s in range(2):
            L.add("lng%d_%d" % (l, s), 32)
            L.add("lnb%d_%d" % (l, s), 32)
    for j in range(2):
        L.add("ssmd%d" % j, 32)
        L.add("bq%d" % j, 32)
        L.add("bk%d" % j, 8)
    return L


VL = vec_layout()


def _rel_bucket_np(dist):
    max_exact = 16
    d = np.maximum(dist, 0)
    log_ratio = np.log(np.maximum(d, 1).astype(np.float32) / max_exact) / math.log(128 / max_exact)
    large = max_exact + (log_ratio * (32 - max_exact)).astype(np.int32)
    large = np.minimum(large, 31)
    return np.where(d < max_exact, d, large)


def prep_inputs(inp, b, layers, ntok):
    m = {}
    m["xT"] = np.ascontiguousarray(np.asarray(inp["x"][b, :ntok], np.float32).T)
    vec = np.zeros((128, VL.n), np.float32)

    def put(name, v):
        a = _fm(v)
        vec[:, VL.off[name]:VL.off[name] + a.shape[1]] = a

    put("c", inp["c"][b])
    for mm_ in range(6):
        put("bada%d" % mm_, inp["b_ada"][mm_ * D:(mm_ + 1) * D])
    for l in range(DEPTH):
        for mm_ in range(6):
            put("tab%d_%d" % (l, mm_), inp["ada_table"][l, mm_])
        for s in range(2):
            put("lng%d_%d" % (l, s), inp["ln_g"][l, s])
            put("lnb%d_%d" % (l, s), inp["ln_b"][l, s])
    for j in range(2):
        put("ssmd%d" % j, inp["ssm_d"][j])
        bqkv = np.asarray(inp["attn_b_qkv"][j], np.float32)
        put("bq%d" % j, bqkv[:D])
        bk = bqkv[D:D + 512].reshape(8, 64)
        put("bk%d" % j, np.concatenate([bk, bk], axis=1).reshape(-1))
    m["vecs"] = vec
    m["wada"] = np.ascontiguousarray(np.asarray(inp["w_ada"], np.float32))
    for l in layers:
        m["win%d" % l] = _pieces(np.asarray(inp["ffn_w_in"][l], np.float32))
        wd = np.asarray(inp["ffn_w_down"][l], np.float32)
        a = wd.reshape(2, 43, 128, 32, 128).transpose(3, 0, 2, 1, 4)
        m["wdn%d" % l] = np.ascontiguousarray(a).reshape(64, 128, 43 * 128)
        j = l // 2
        if l % 2 == 0:
            m["wval%d" % j] = _pieces(np.asarray(inp["ssm_w_val"][j], np.float32))
            m["wgate%d" % j] = _pieces(np.asarray(inp["ssm_w_gate"][j], np.float32))
            for nm, key in (("bpr", "ssm_b_re"), ("bpi", "ssm_b_im")):
                bsrc = np.asarray(inp[key][j], np.float32)
                out = np.zeros((128, 128, 128), np.float32)
                for pair in range(128):
                    q = pair % 4
                    for g2 in range(2):
                        g = 2 * pair + g2
                        r0 = (2 * q + g2) * 16
                        out[pair, r0:r0 + 16, g2 * 64:(g2 + 1) * 64] = bsrc[g].T
                m["%s%d" % (nm, j)] = out
            for nm, key in (("cpr", "ssm_c_re"), ("cpi", "ssm_c_im")):
                csrc = np.asarray(inp[key][j], np.float32)
                out = np.zeros((128, 128, 128), np.float32)
                for pair in range(128):
                    q = pair % 4
                    for g2 in range(2):
                        g = 2 * pair + g2
                        c0 = (2 * q + g2) * 16
                        out[pair, g2 * 64:(g2 + 1) * 64, c0:c0 + 16] = csrc[g].T
                m["%s%d" % (nm, j)] = out
            are = np.asarray(inp["ssm_a_re"][j], np.float32).reshape(128, 128).T
            aim = np.asarray(inp["ssm_a_im"][j], np.float32).reshape(128, 128).T
            ldt = np.repeat(np.asarray(inp["ssm_log_dt"][j], np.float32).reshape(128, 2, 1), 64, axis=2)
            ldt = ldt.reshape(128, 128).T
            m["ssc%d" % j] = np.ascontiguousarray(np.concatenate([are, aim, ldt], axis=1))
        else:
            wqkv = np.asarray(inp["attn_w_qkv"][j], np.float32)
            m["wq%d" % j] = _pieces(wqkv[:, :D])
            wk = wqkv[:, D:D + 512].reshape(D, 8, 64)
            wk2 = np.concatenate([wk, wk], axis=2).reshape(D, 1024)
            m["wk%d" % j] = _pieces(wk2)
            m["wv%d" % j] = _pieces(wqkv[:, D + 512:D + 1024])
            m["wo%d" % j] = _pieces(np.asarray(inp["attn_w_o"][j], np.float32))
            bv = np.asarray(inp["attn_b_qkv"][j], np.float32)[D + 512:]
            m["bvrep%d" % j] = np.ascontiguousarray(np.broadcast_to(bv[None, :], (128, 512)))
            sk = np.asarray(inp["attn_sinks"][j], np.float32)
            m["sinkrep%d" % j] = np.ascontiguousarray(np.repeat(sk, 64)[None, :])
    if any(l % 2 == 1 for l in layers):
        rb = np.asarray(inp["rel_bias"], np.float32)
        qi = np.arange(128)[:, None]
        sj = np.arange(256)[None, :]
        dist = qi + 128 - sj
        band = (dist >= 0) & (dist < 128)
        bias = rb[_rel_bucket_np(dist)]
        bias = np.where(band[:, :, None], bias, np.float32(-1e30))
        bt = bias.transpose(1, 2, 0).reshape(2, 128, 8, 8, 128)
        m["biastab"] = np.ascontiguousarray(bt.transpose(2, 0, 1, 3, 4)).reshape(16, 128, 1024)
    return m


class Model:
    def __init__(self, layers, ntok, stop=None):
        self.layers = list(layers)
        self.ntok = ntok
        self.nt = ntok // TT
        self.stop = stop
        P = self.P = Prog()
        self.inp = {}
        self.xT = self.din("xT", [D, ntok], F32)
        self.vecs_d = self.din("vecs", [128, VL.n], F32)
        self.wada = self.din("wada", [D, 6 * D], F32)
        for l in self.layers:
            self.din("win%d" % l, [172, 128, 4096], F32)
            self.din("wdn%d" % l, [64, 128, 43 * 128], F32)
            j = l // 2
            if l % 2 == 0:
                for nm in ("wval", "wgate"):
                    self.din("%s%d" % (nm, j), [32, 128, 4096], F32)
                for nm in ("bpr", "bpi", "cpr", "cpi"):
                    self.din("%s%d" % (nm, j), [128, 128, 128], F32)
                self.din("ssc%d" % j, [128, 384], F32)
            else:
                self.din("wq%d" % j, [32, 128, 4096], F32)
                self.din("wk%d" % j, [8, 128, 4096], F32)
                self.din("wv%d" % j, [4, 128, 4096], F32)
                self.din("wo%d" % j, [32, 128, 4096], F32)
                self.din("bvrep%d" % j, [128, 512], F32)
                self.din("sinkrep%d" % j, [1, 4096], F32)
        if any(l % 2 == 1 for l in self.layers):
            self.din("biastab", [16, 128, 1024], F32)
        self.outT = P.dram("outT", [D, ntok], F32, kind="ExternalOutput")
        self.xA = P.dram("xA", [D, ntok], F32)
        self.xB = P.dram("xB", [D, ntok], F32)
        self.zT = P.dram("zT", [D, TT], F32)
        self.uT = P.dram("uT", [D, TT], BF16)
        self.hidT = P.dram("hidT", [DFF, TT], BF16)
        self.modsD = P.dram("modsD", [1, 6 * D], F32)
        self.tabs = P.dram("tabs", [128, 128, 2048], F32)
        self.cpp = P.dram("cpp", [128, 128, 384], BF16)
        self.qTd = P.dram("qTd", [D, TT], BF16)
        self.kTd = P.dram("kTd", [8, 128, TT + 128], BF16)
        self.vpd = P.dram("vpd", [128, 9 * 8 * 2 * 128], BF16)
        self.vecs = P.sb("vecs_sb", [128, VL.n], F32)
        self.mods = P.sb("mods_sb", [128, DEPTH * 6 * 32], F32)
        self.ones_ln = P.sb("ones_ln", [128, 128], F32)
        self.consts = P.sb("consts", [128, 8], F32)
        self.st_re = P.sb("st_re", [128, 128], F32)
        self.st_im = P.sb("st_im", [128, 128], F32)
        self.psall = P.ps("psall", [128, 4096], F32)
        self.ps = [self.psall.view(self.psall[:, i * 512:(i + 1) * 512], "psb%d" % i) for i in range(8)]
        self.psw = [self.psall.view(self.psall[:, i * 1024:(i + 1) * 1024], "psw%d" % i) for i in range(2)]

    def din(self, name, shape, dtype):
        b = self.P.dram(name, shape, dtype, kind="ExternalInput")
        self.inp[name] = b
        return b

    def vcol(self, name, i=0, n=1):
        o = VL.off[name] + i
        return self.vecs[:, o:o + n]

    def mcol(self, l, m, i=0, n=1):
        o = (l * 6 + m) * 32 + i
        return self.mods[:, o:o + n]

    def setup(self):
        P = self.P
        P.dma("sp", self.vecs, self.vecs.ap(), self.vecs_d, self.vecs_d.ap())
        memset(P, "dve", self.ones_ln, 1.0 / D)
        memset(P, "dve", (self.consts, self.consts[:, 0:1]), math.pi / 2)
        memset(P, "dve", (self.consts, self.consts[:, 1:2]), EPS_P)
        memset(P, "dve", (self.consts, self.consts[:, 2:3]), 0.0)

    def phase_mods(self):
        P = self.P
        with P.scope():
            scs = P.sb("scs", [128, 32], BF16)
            wts = [P.sb("wadat%d" % i, [128, 4096], BF16) for i in range(4)]
            row = P.sb("modrow", [1, 4096], F32)
            msh = P.sb("msh", [128, 192], F32)
            act(P, scs, (self.vecs, self.vcol("c", 0, 32)), AF.Silu)
            n = 0
            for cg in range(6):
                for kc in range(KC):
                    wt = wts[n % 4]
                    n += 1
                    P.dma("pool", wt, wt.ap(), self.wada, self.wada[kc * 128:(kc + 1) * 128, cg * D:(cg + 1) * D],
                          max_dma_last_dim=8192)
                    for b in range(8):
                        mm(P, (self.ps[b], self.ps[b][0:1, :]), (scs, scs[:, kc:kc + 1]),
                           (wt, wt[:, b * 512:(b + 1) * 512]), kc == 0, kc == KC - 1)
                for b in range(8):
                    cp(P, "act" if b % 2 else "dve", (row, row[0:1, b * 512:(b + 1) * 512]),
                       (self.ps[b], self.ps[b][0:1, :]))
                P.dma("sp", self.modsD, self.modsD[0:1, cg * D:(cg + 1) * D], row, row.ap())
            P.dma("sp", msh, msh.ap(), self.modsD, self.modsD.ap().rearrange("o (j p) -> p (o j)", p=128),
                  allow_slow_non_contiguous=True)
            o = VL.off["bada0"]
            tt(P, "dve", msh, msh, (self.vecs, self.vecs[:, o:o + 192]), ALU.add)
            for l in range(DEPTH):
                o = VL.off["tab%d_0" % l]
                ml = (self.mods, self.mods[:, l * 192:(l + 1) * 192])
                tt(P, "dve", ml, msh, (self.vecs, self.vecs[:, o:o + 192]), ALU.add)
                for m in (1, 4):
                    mc = (self.mods, self.mcol(l, m, 0, 32))
                    ts(P, "dve", mc, mc, 1.0, None, ALU.add)
                for m in (2, 5):
                    mc = (self.mods, self.mcol(l, m, 0, 32))
                    ts(P, "dve", mc, mc, 1.0, 1.0 / ALPHA, ALU.add, ALU.mult)

    def ln_finalize(self, xout, t0, half, st_mu, st_e2, l, s):
        P = self.P
        with P.scope():
            mu = P.sb("ln_mu", [128, HT], F32)
            rstd = P.sb("ln_rstd", [128, HT], F32)
            tmp = P.sb("ln_tmp", [128, HT], F32)
            zc = [P.sb("ln_z%d" % i, [128, HT], F32) for i in range(4)]
            oc_ = [P.sb("ln_o%d" % i, [128, HT], F32) for i in range(3)]
            c0 = t0 + half * HT

            def load(oc):
                z = zc[oc % 4]
                P.dma("sp", z, z.ap(), self.zT, self.zT[oc * 128:(oc + 1) * 128, half * HT:(half + 1) * HT])

            load(0)
            load(1)
            load(2)
            cp(P, "act", mu, st_mu)
            tt(P, "dve", tmp, mu, mu, ALU.mult)
            tt(P, "dve", tmp, st_e2, tmp, ALU.subtract)
            act(P, tmp, tmp, AF.Sqrt, bias=self.consts[:, 1:2], extra_reads=[self.consts])
            P.op("dve", lambda e: e.reciprocal(out=rstd.ap(), in_=tmp.ap()), reads=[tmp], writes=[rstd])
            for oc in range(KC):
                z = zc[oc % 4]
                o = oc_[oc % 3]
                tt(P, "dve", z, z, mu, ALU.subtract)
                tt(P, "dve", z, z, rstd, ALU.mult)
                if oc + 3 < KC:
                    load(oc + 3)
                act(P, o, z, AF.Identity, scale=self.vcol("lng%d_%d" % (l, s), oc), bias=self.vcol("lnb%d_%d" % (l, s), oc),
                    extra_reads=[self.vecs])
                P.dma("act", xout, xout[oc * 128:(oc + 1) * 128, c0:c0 + HT], o, o.ap())

    def z_epilogue(self, y, xin, t0, half, oc, l, gm, zts, sqs, xcs, stats, idx, pre=None):
        P = self.P
        zt = zts[idx % len(zts)]
        sq = sqs[idx % len(sqs)]
        xc = xcs[idx % len(xcs)]
        c0 = t0 + half * HT
        P.dma("sp", xc, xc.ap(), xin, xin[oc * 128:(oc + 1) * 128, c0:c0 + HT])
        stt(P, zt, y, self.mcol(l, gm, oc), xc, ALU.mult, ALU.add, extra_reads=[self.mods])
        act(P, sq, zt, AF.Square)
        mm(P, stats[0], self.ones_ln, zt, oc == 0, oc == KC - 1)
        mm(P, stats[1], self.ones_ln, sq, oc == 0, oc == KC - 1)
        P.dma("act", self.zT, self.zT[oc * 128:(oc + 1) * 128, half * HT:(half + 1) * HT], zt, zt.ap())

    def build_h(self, hT, xin, c_lo, ncol, l, m_shift, m_scale, xcs, zero_first=0):
        P = self.P
        views = []
        for kc in range(KC):
            xc = xcs[kc % len(xcs)]
            hv = hT.view(hT[:, kc, :], "h_kc%d" % kc)
            views.append(hv)
            if zero_first:
                memset(P, "pool", (hv, hT[:, kc, 0:zero_first]), 0.0)
            lo = c_lo + zero_first
            n = ncol - zero_first
            P.dma("sp", xc, xc[:, 0:n], xin, xin[kc * 128:(kc + 1) * 128, lo:lo + n])
            ts(P, "dve" if kc % 2 else "pool", (hv, hT[:, kc, zero_first:ncol]), (xc, xc[:, 0:n]),
               self.mcol(l, m_scale, kc), self.mcol(l, m_shift, kc), ALU.mult, ALU.add, extra_reads=[self.mods])
        return views

    def phase_ffn_up(self, l, xin, t0):
        P = self.P
        win = self.inp["win%d" % l]
        with P.scope():
            hT = P.sb("f1_hT", [128, KC, TT], BF16)
            xcs = [P.sb("f1_xc%d" % i, [128, TT], F32) for i in range(2)]
            wg = [P.sb("f1_wg%d" % i, [128, 4096], BF16) for i in range(3)]
            wu = [P.sb("f1_wu%d" % i, [128, 4096], BF16) for i in range(3)]
            sg = [P.sb("f1_sg%d" % i, [128, HT], F32) for i in range(2)]
            hid = [P.sb("f1_hid%d" % i, [128, HT], BF16) for i in range(3)]
            hv = self.build_h(hT, xin, t0, TT, l, 3, 4, xcs)
            it = 0
            for j in range(NJ):
                g_t = wg[j % 3]
                u_t = wu[j % 3]
                P.dma("pool", g_t, g_t.ap(), win, win[j], max_dma_last_dim=8192)
                P.dma("pool", u_t, u_t.ap(), win, win[NJ + j], max_dma_last_dim=8192)
                for half in range(2):
                    pg = self.ps[(it % 2) * 2]
                    pu = self.ps[(it % 2) * 2 + 1]
                    for kc in range(KC):
                        mm(P, pg, (g_t, g_t[:, kc * 128:(kc + 1) * 128]), (hv[kc], hT[:, kc, half * HT:(half + 1) * HT]),
                           kc == 0, kc == KC - 1)
                    for kc in range(KC):
                        mm(P, pu, (u_t, u_t[:, kc * 128:(kc + 1) * 128]), (hv[kc], hT[:, kc, half * HT:(half + 1) * HT]),
                           kc == 0, kc == KC - 1)
                    s_ = sg[it % 2]
                    h_ = hid[it % 3]
                    act(P, s_, pg, AF.Silu)
                    tt(P, "dve", h_, pu, s_, ALU.mult)
                    P.dma("act", self.hidT, self.hidT[j * 128:(j + 1) * 128, half * HT:(half + 1) * HT], h_, h_.ap())
                    it += 1

    def phase_ffn_down(self, l, xin, xout, t0):
        P = self.P
        wdn = self.inp["wdn%d" % l]
        for half in range(2):
            with P.scope():
                hid = P.sb("f2_hid", [128, NJ, HT], BF16)
                wp = [P.sb("f2_w%d" % i, [128, 43 * 128], BF16) for i in range(3)]
                zts = [P.sb("f2_z%d" % i, [128, HT], F32) for i in range(2)]
                sqs = [P.sb("f2_q%d" % i, [128, HT], F32) for i in range(2)]
                xcs = [P.sb("f2_x%d" % i, [128, HT], F32) for i in range(2)]
                hvs = []
                for j0 in range(0, NJ, 8):
                    j1 = min(NJ, j0 + 8)
                    v = hid.view(hid[:, j0:j1, :], "hidv%d" % j0)
                    P.dma("sp", v, hid[:, j0:j1, :], self.hidT,
                          self.hidT[j0 * 128:j1 * 128, half * HT:(half + 1) * HT].rearrange("(j p) n -> p j n", p=128))
                    hvs.append(v)
                stats = (self.ps[6], self.ps[7])
                for oc in range(KC):
                    py = self.ps[oc % 2]
                    for kh in range(2):
                        w = wp[(oc * 2 + kh) % 3]
                        P.dma("pool", w, w.ap(), wdn, wdn[oc * 2 + kh], max_dma_last_dim=8192)
                        for k in range(43):
                            j = kh * 43 + k
                            mm(P, py, (w, w[:, k * 128:(k + 1) * 128]), (hvs[j // 8], hid[:, j, :]),
                               j == 0, j == NJ - 1)
                    self.z_epilogue(py, xin, t0, half, oc, l, 5, zts, sqs, xcs, stats, oc)
                self.ln_finalize(xout, t0, half, stats[0], stats[1], l, 1)

    def _range_reduce(self, eng_ang, out, ang, kk):
        P = self.P
        ts(P, "dve", kk, ang, 1.0 / TWO_PI, MAGIC, ALU.mult, ALU.add)
        ts(P, "dve", kk, kk, -MAGIC, None, ALU.add)
        stt(P, out, kk, -CW1, ang, ALU.mult, ALU.add)
        stt(P, out, kk, -CW2, out, ALU.mult, ALU.add)
        ts(P, "pool", out, out, PI_LO, -PI_LO, ALU.min, ALU.max)

    def _sincos(self, sn, cs, r, tmp):
        P = self.P
        act(P, sn, r, AF.Sin)
        act(P, tmp, r, AF.Abs)
        act(P, cs, tmp, AF.Sin, scale=-1.0, bias=self.consts[:, 0:1], extra_reads=[self.consts])

    def ssm_layer(self, l, xin, xout):
        P = self.P
        j = l // 2
        with P.scope():
            rmag = P.sb("s_rmag", [128, 128], F32)
            with P.scope():
                ssc = P.sb("s_ssc", [128, 384], F32)
                P.dma("sp", ssc, ssc.ap(), self.inp["ssc%d" % j], self.inp["ssc%d" % j].ap())
                a_re = (ssc, ssc[:, 0:128])
                a_im = (ssc, ssc[:, 128:256])
                nm = ["dt", "th", "kk", "thr", "sn1", "cs1", "tmp", "abre", "abim", "den", "nr", "kre", "kim", "nkim",
                      "nkre", "t2"]
                T = {n: P.sb("s_" + n, [128, 128], F32) for n in nm}
                act(P, T["dt"], (ssc, ssc[:, 256:384]), AF.Exp)
                tt(P, "dve", T["tmp"], a_re, T["dt"], ALU.mult)
                act(P, rmag, T["tmp"], AF.Exp)
                tt(P, "dve", T["th"], a_im, T["dt"], ALU.mult)
                self._range_reduce("dve", T["thr"], T["th"], T["kk"])
                self._sincos(T["sn1"], T["cs1"], T["thr"], T["tmp"])
                tt(P, "dve", T["abre"], rmag, T["cs1"], ALU.mult)
                tt(P, "dve", T["abim"], rmag, T["sn1"], ALU.mult)
                tt(P, "dve", T["den"], a_re, a_re, ALU.mult)
                tt(P, "dve", T["tmp"], a_im, a_im, ALU.mult)
                tt(P, "dve", T["den"], T["den"], T["tmp"], ALU.add)
                P.op("dve", lambda e: e.reciprocal(out=T["den"].ap(), in_=T["den"].ap()), reads=[T["den"]],
                     writes=[T["den"]])
                ts(P, "dve", T["nr"], T["abre"], -1.0, None, ALU.add)
                tt(P, "dve", T["kre"], T["nr"], a_re, ALU.mult)
                tt(P, "dve", T["tmp"], T["abim"], a_im, ALU.mult)
                tt(P, "dve", T["kre"], T["kre"], T["tmp"], ALU.add)
                tt(P, "dve", T["kre"], T["kre"], T["den"], ALU.mult)
                tt(P, "dve", T["kim"], T["abim"], a_re, ALU.mult)
                tt(P, "dve", T["tmp"], T["nr"], a_im, ALU.mult)
                tt(P, "dve", T["kim"], T["kim"], T["tmp"], ALU.subtract)
                tt(P, "dve", T["kim"], T["kim"], T["den"], ALU.mult)
                ts(P, "dve", T["nkim"], T["kim"], -1.0, None, ALU.mult)
                ts(P, "dve", T["nkre"], T["kre"], -1.0, None, ALU.mult)
                cr = [P.sb("s_cr%d" % i, [128, 128], F32) for i in range(2)]
                ci = [P.sb("s_ci%d" % i, [128, 128], F32) for i in range(2)]
                c1 = [P.sb("s_c1%d" % i, [128, 128], F32) for i in range(2)]
                cb = [P.sb("s_cb%d" % i, [128, 384], BF16) for i in range(2)]
                cpr = self.inp["cpr%d" % j]
                cpi = self.inp["cpi%d" % j]
                for pair in range(128):
                    a, b, c_, o = cr[pair % 2], ci[pair % 2], c1[pair % 2], cb[pair % 2]
                    P.dma("sp", a, a.ap(), cpr, cpr[pair])
                    P.dma("sp", b, b.ap(), cpi, cpi[pair])
                    pc = slice(pair, pair + 1)
                    ts(P, "pool", c_, a, T["kre"][:, pc], None, ALU.mult, extra_reads=[T["kre"]])
                    stt(P, (o, o[:, 0:128]), b, T["nkim"][:, pc], c_, ALU.mult, ALU.add, extra_reads=[T["nkim"]])
                    ts(P, "pool", c_, a, T["nkim"][:, pc], None, ALU.mult, extra_reads=[T["nkim"]])
                    stt(P, (o, o[:, 128:256]), b, T["nkre"][:, pc], c_, ALU.mult, ALU.add, extra_reads=[T["nkre"]])
                    ts(P, "dve", (o, o[:, 256:384]), (o, o[:, 0:128]), -1.0, None, ALU.mult)
                    P.dma("sp", self.cpp, self.cpp[pair], o, o.ap())
                io_i = P.sb("s_ioi", [128, TT], I32)
                io_f = P.sb("s_iof", [128, TT], F32)
                P.op("pool", lambda e: e.iota(io_i.ap(), pattern=[[1, TT]], base=1, channel_multiplier=0), writes=[io_i])
                cp(P, "dve", io_f, io_i)
                ang = [P.sb("s_ang%d" % i, [128, TT], F32) for i in range(2)]
                kk = [P.sb("s_kk%d" % i, [128, TT], F32) for i in range(2)]
                rr = [P.sb("s_rr%d" % i, [128, TT], F32) for i in range(2)]
                tb = [P.sb("s_tb%d" % i, [128, 2, TT], F32) for i in range(2)]
                for pair in range(128):
                    a, k_, r_, t_ = ang[pair % 2], kk[pair % 2], rr[pair % 2], tb[pair % 2]
                    ts(P, "pool", a, io_f, T["thr"][:, pair:pair + 1], None, ALU.mult, extra_reads=[T["thr"]])
                    self._range_reduce("dve", r_, a, k_)
                    act(P, (t_, t_[:, 1, :]), r_, AF.Sin)
                    act(P, a, r_, AF.Abs)
                    act(P, (t_, t_[:, 0, :]), a, AF.Sin, scale=-1.0, bias=self.consts[:, 0:1], extra_reads=[self.consts])
                    P.dma("sp", self.tabs, self.tabs[pair], t_, t_.ap().rearrange("p a n -> p (a n)"))
            memset(P, "dve", self.st_re, 0.0)
            memset(P, "dve", self.st_im, 0.0)
            for t in range(self.nt):
                self.phase_ssm_tile(l, xin, t * TT, rmag)
                self.phase_ssm_proj(l, xin, xout, t * TT)

    def phase_ssm_tile(self, l, xin, t0, rmag):
        P = self.P
        j = l // 2
        bpr = self.inp["bpr%d" % j]
        bpi = self.inp["bpi%d" % j]
        with P.scope():
            xcs = [P.sb("st_xc%d" % i, [128, TT], F32) for i in range(2)]
            h32s = [P.sb("st_h32%d" % i, [128, TT], F32) for i in range(2)]
            h16s = [P.sb("st_h16%d" % i, [128, TT], BF16) for i in range(2)]
            bps = [P.sb("st_bp%d" % i, [128, 8, 128], BF16) for i in range(2)]
            cpts = [P.sb("st_cp%d" % i, [128, 4, 384], BF16) for i in range(2)]
            tabs = [P.sb("st_tab%d" % i, [128, 2, TT], F32) for i in range(2)]
            rmats = [P.sb("st_rm%d" % i, [128, TT], F32) for i in range(2)]
            ones = P.sb("st_ones", [128, TT], F32)
            memset(P, "dve", ones, 1.0)
            mk = lambda nm, dt=F32, n=2, w=TT: [P.sb("st_%s%d" % (nm, i), [128, w], dt) for i in range(n)]
            t1, t2, t3, t4 = mk("t1", n=1)[0], mk("t2", n=1)[0], mk("t3", n=1)[0], mk("t4", n=1)[0]
            dsets = [[P.sb("st_dd%d_%d" % (a, b), [128, TT], BF16) for b in range(4)] for a in range(2)]
            utr, uti = mk("utr"), mk("uti")
            srs, sis = mk("sr"), mk("si")
            yv, ga, gb = mk("yv", w=HT), mk("ga", w=HT), mk("gb", w=HT)
            u16 = mk("u16", BF16, 3, HT)
            sm = P.sb("st_sm", [128, 8], F32)
            ure, uim = self.psw[0], self.psw[1]

            def prep(dc):
                xc, h32, h16 = xcs[dc % 2], h32s[dc % 2], h16s[dc % 2]
                bp, cpt = bps[dc % 2], cpts[dc % 2]
                P.dma("sp", xc, xc.ap(), xin, xin[dc * 128:(dc + 1) * 128, t0:t0 + TT])
                ts(P, "dve", h32, xc, self.mcol(l, 1, dc), self.mcol(l, 0, dc), ALU.mult, ALU.add, extra_reads=[self.mods])
                cp(P, "act", h16, h32)
                P.dma("pool", bp, bp[:, 0:4, :], bpr, bpr[dc * 4:(dc + 1) * 4].rearrange("q p c -> p q c"))
                P.dma("pool", bp, bp[:, 4:8, :], bpi, bpi[dc * 4:(dc + 1) * 4].rearrange("q p c -> p q c"))
                P.dma("sp", cpt, cpt.ap(), self.cpp, self.cpp[dc * 4:(dc + 1) * 4].rearrange("q p c -> p q c"))

            def load_tab(pair):
                tb = tabs[pair % 2]
                P.dma("sp", tb, tb.ap().rearrange("p a n -> p (a n)"), self.tabs, self.tabs[pair])
                act(P, rmats[pair % 2], ones, AF.Copy, scale=rmag[:, pair:pair + 1], extra_reads=[rmag])

            def bmm(pair):
                dc, q = pair // 4, pair % 4
                bp, h16 = bps[dc % 2], h16s[dc % 2]
                for half in range(2):
                    hs = slice(half * HT, (half + 1) * HT)
                    mm(P, (ure, ure[:, hs]), (bp, bp[:, q, :]), (h16, h16[:, hs]), True, True)
                    mm(P, (uim, uim[:, hs]), (bp, bp[:, 4 + q, :]), (h16, h16[:, hs]), True, True)

            prep(0)
            load_tab(0)
            bmm(0)
            for pair in range(128):
                dc, q = pair // 4, pair % 4
                i2 = pair % 2
                tb, rm = tabs[i2], rmats[i2]
                cpt = cpts[dc % 2]
                yb = (self.ps[4 + 2 * (dc % 2)], self.ps[5 + 2 * (dc % 2)])
                c_ = (tb, tb[:, 0, :])
                s_ = (tb, tb[:, 1, :])
                if q == 0 and dc + 1 < KC:
                    prep(dc + 1)
                tt(P, "dve", t1, ure, c_, ALU.mult)
                tt(P, "dve", t2, uim, s_, ALU.mult)
                tt(P, "dve", t3, uim, c_, ALU.mult)
                tt(P, "dve", t4, ure, s_, ALU.mult)
                if pair + 1 < 128:
                    load_tab(pair + 1)
                    bmm(pair + 1)
                tt(P, "dve", utr[i2], t1, t2, ALU.add)
                tt(P, "dve", uti[i2], t3, t4, ALU.subtract)
                sr, si = srs[i2], sis[i2]
                ini_r, ini_i = self.st_re[:, pair:pair + 1], self.st_im[:, pair:pair + 1]
                P.op("dve", lambda e, sr=sr, rm=rm, u=utr[i2], ini=ini_r: e.tensor_tensor_scan(
                    out=sr.ap(), data0=rm.ap(), data1=u.ap(), initial=ini, op0=ALU.mult, op1=ALU.add),
                    reads=[rm, utr[i2], self.st_re], writes=[sr])
                P.op("dve", lambda e, si=si, rm=rm, u=uti[i2], ini=ini_i: e.tensor_tensor_scan(
                    out=si.ap(), data0=rm.ap(), data1=u.ap(), initial=ini, op0=ALU.mult, op1=ALU.add),
                    reads=[rm, uti[i2], self.st_im], writes=[si])
                dd = dsets[i2]
                tt(P, "pool", dd[0], sr, c_, ALU.mult)
                tt(P, "pool", dd[1], si, s_, ALU.mult)
                tt(P, "pool", dd[2], sr, s_, ALU.mult)
                tt(P, "dve" if pair % 2 else "pool", dd[3], si, c_, ALU.mult)
                for half in range(2):
                    hs = slice(half * HT, (half + 1) * HT)
                    mm(P, yb[half], (cpt, cpt[:, q, 0:128]), (dd[0], dd[0][:, hs]), q == 0, False)
                    mm(P, yb[half], (cpt, cpt[:, q, 256:384]), (dd[1], dd[1][:, hs]), False, False)
                    mm(P, yb[half], (cpt, cpt[:, q, 128:256]), (dd[2], dd[2][:, hs]), False, False)
                    mm(P, yb[half], (cpt, cpt[:, q, 128:256]), (dd[3], dd[3][:, hs]), False, q == 3)
                L = slice(TT - 1, TT)
                smv = lambda k: (sm, sm[:, k:k + 1])
                act(P, smv(0), (sr, sr[:, L]), AF.Copy, scale=tb[:, 0, L], extra_reads=[tb])
                act(P, smv(1), (si, si[:, L]), AF.Copy, scale=tb[:, 1, L], extra_reads=[tb])
                act(P, smv(2), (sr, sr[:, L]), AF.Copy, scale=tb[:, 1, L], extra_reads=[tb])
                act(P, smv(3), (si, si[:, L]), AF.Copy, scale=tb[:, 0, L], extra_reads=[tb])
                act(P, (self.st_re, self.st_re[:, pair:pair + 1]), smv(1), AF.Identity, scale=-1.0, bias=sm[:, 0:1],
                    extra_reads=[sm])
                act(P, (self.st_im, self.st_im[:, pair:pair + 1]), smv(3), AF.Identity, bias=sm[:, 2:3], extra_reads=[sm])
                if q == 3:
                    h32 = h32s[dc % 2]
                    for half in range(2):
                        hs = slice(half * HT, (half + 1) * HT)
                        y_, a_, b_ = yv[half], ga[half], gb[half]
                        uo = u16[(dc * 2 + half) % 3]
                        stt(P, y_, (h32, h32[:, hs]), self.vcol("ssmd%d" % j, dc), yb[half], ALU.mult, ALU.add,
                            extra_reads=[self.vecs])
                        act(P, a_, y_, AF.Square)
                        ts(P, "dve", a_, a_, 0.044715, 1.0, ALU.mult, ALU.add)
                        tt(P, "dve", a_, a_, y_, ALU.mult)
                        act(P, b_, a_, AF.Sigmoid, scale=1.5957691216057308)
                        tt(P, "dve", uo, y_, b_, ALU.mult)
                        P.dma("sp", self.uT, self.uT[dc * 128:(dc + 1) * 128, hs], uo, uo.ap())

    def phase_ssm_proj(self, l, xin, xout, t0):
        P = self.P
        j = l // 2
        wv = self.inp["wval%d" % j]
        wg = self.inp["wgate%d" % j]
        with P.scope():
            u16 = P.sb("sp_u16", [128, KC, TT], BF16)
            wvt = [P.sb("sp_wv%d" % i, [128, 4096], BF16) for i in range(3)]
            wgt = [P.sb("sp_wg%d" % i, [128, 4096], BF16) for i in range(3)]
            mk = lambda nm, n=2: [P.sb("sp_%s%d" % (nm, i), [128, HT], F32) for i in range(n)]
            sgs, tmps, zts, sqs, xcs = mk("sg"), mk("tm"), mk("z"), mk("q"), mk("x")
            uv = []
            for k0 in range(0, KC, 8):
                v = u16.view(u16[:, k0:k0 + 8, :], "u16v%d" % k0)
                P.dma("sp", v, u16[:, k0:k0 + 8, :], self.uT,
                      self.uT[k0 * 128:(k0 + 8) * 128, :].rearrange("(j p) n -> p j n", p=128))
                uv.append(v)
            stats = [(self.ps[4], self.ps[5]), (self.ps[6], self.ps[7])]
            it = 0
            for oc in range(KC):
                a, b = wvt[oc % 3], wgt[oc % 3]
                P.dma("pool", a, a.ap(), wv, wv[oc], max_dma_last_dim=8192)
                P.dma("pool", b, b.ap(), wg, wg[oc], max_dma_last_dim=8192)
                for half in range(2):
                    hs = slice(half * HT, (half + 1) * HT)
                    pv, pg = self.ps[(it % 2) * 2], self.ps[(it % 2) * 2 + 1]
                    for kc in range(KC):
                        mm(P, pv, (a, a[:, kc * 128:(kc + 1) * 128]), (uv[kc // 8], u16[:, kc, hs]), kc == 0, kc == KC - 1)
                    for kc in range(KC):
                        mm(P, pg, (b, b[:, kc * 128:(kc + 1) * 128]), (uv[kc // 8], u16[:, kc, hs]), kc == 0, kc == KC - 1)
                    sg, tm = sgs[it % 2], tmps[it % 2]
                    act(P, sg, pg, AF.Sigmoid)
                    tt(P, "dve", tm, pv, sg, ALU.mult)
                    self.z_epilogue(tm, xin, t0, half, oc, l, 2, zts, sqs, xcs, stats[half], it)
                    it += 1
            for half in range(2):
                self.ln_finalize(xout, t0, half, stats[half][0], stats[half][1], l, 0)

    def attn_layer(self, l, xin, xout):
        for t in range(self.nt):
            import os as _os
            st_ = _os.environ.get("ATT_STOP", "")
            self.phase_attn_qkv(l, xin, t * TT, t == 0)
            if st_ == "qkv":
                continue
            self.phase_attn_core(l, t * TT, t == 0)
            if st_ == "core":
                continue
            self.phase_attn_out(l, xin, xout, t * TT)

    def phase_attn_qkv(self, l, xin, t0, first):
        P = self.P
        j = l // 2
        wq, wk, wv = self.inp["wq%d" % j], self.inp["wk%d" % j], self.inp["wv%d" % j]
        NC = TT + 128
        with P.scope():
            hT = P.sb("aq_hT", [128, KC, NC], BF16)
            xcs = [P.sb("aq_xc%d" % i, [128, NC], F32) for i in range(2)]
            wr = [P.sb("aq_w%d" % i, [128, 4096], BF16) for i in range(3)]
            wvs = [P.sb("aq_wv%d" % i, [128, 4096], BF16) for i in range(4)]
            q16 = [P.sb("aq_q%d" % i, [128, HT], BF16) for i in range(3)]
            vt = [P.sb("aq_vt%d" % i, [128, 512], F32) for i in range(2)]
            vpad = [P.sb("aq_vp%d" % i, [128, 8, 2, 128], BF16) for i in range(2)]
            bv = P.sb("aq_bv", [128, 512], F32)
            P.dma("sp", bv, bv.ap(), self.inp["bvrep%d" % j], self.inp["bvrep%d" % j].ap())
            for v_ in vpad:
                memset(P, "pool", v_, 0.0)
            hv = self.build_h(hT, xin, t0 - 128, NC, l, 0, 1, xcs, zero_first=128 if first else 0)
            for vp_ in range(4):
                P.dma("pool", wvs[vp_], wvs[vp_].ap(), wv, wv[vp_], max_dma_last_dim=8192)
            it = 0
            for oc in range(KC):
                w = wr[oc % 3]
                P.dma("pool", w, w.ap(), wq, wq[oc], max_dma_last_dim=8192)
                for half in range(2):
                    pb = self.ps[it % 4]
                    c0 = 128 + half * HT
                    for kc in range(KC):
                        mm(P, pb, (w, w[:, kc * 128:(kc + 1) * 128]), (hv[kc], hT[:, kc, c0:c0 + HT]), kc == 0, kc == KC - 1)
                    qo = q16[it % 3]
                    act(P, qo, pb, AF.Identity, bias=self.vcol("bq%d" % j, oc), extra_reads=[self.vecs])
                    P.dma("sp", self.qTd, self.qTd[oc * 128:(oc + 1) * 128, half * HT:(half + 1) * HT], qo, qo.ap())
                    it += 1
            for g in range(NKV):
                w = wr[(KC + g) % 3]
                P.dma("pool", w, w.ap(), wk, wk[g], max_dma_last_dim=8192)
                for (c0, n) in ((0, 512), (512, 512), (1024, 128)):
                    pb = self.ps[it % 4]
                    for kc in range(KC):
                        mm(P, (pb, pb[:, 0:n]), (w, w[:, kc * 128:(kc + 1) * 128]), (hv[kc], hT[:, kc, c0:c0 + n]),
                           kc == 0, kc == KC - 1)
                    ko = q16[it % 3]
                    act(P, (ko, ko[:, 0:n]), (pb, pb[:, 0:n]), AF.Identity, bias=self.vcol("bk%d" % j, g),
                        extra_reads=[self.vecs])
                    P.dma("sp", self.kTd, self.kTd[g, :, c0:c0 + n], ko, ko[:, 0:n])
                    it += 1
            for blk in range(9):
                pb = self.ps[4 + blk % 4]
                for vp_ in range(4):
                    for kc in range(KC):
                        mm(P, (pb, pb[:, vp_ * 128:(vp_ + 1) * 128]), (hv[kc], hT[:, kc, blk * 128:(blk + 1) * 128]),
                           (wvs[vp_], wvs[vp_][:, kc * 128:(kc + 1) * 128]), kc == 0, kc == KC - 1)
                v32 = vt[blk % 2]
                vp16 = vpad[blk % 2]
                tt(P, "dve", v32, pb, bv, ALU.add)
                src = v32.ap().rearrange("p (g d) -> p g d", g=8)
                cp(P, "act", (vp16, vp16[:, :, 0, 0:64]), (v32, src))
                cp(P, "dve", (vp16, vp16[:, :, 1, 64:128]), (v32, src))
                P.dma("sp", self.vpd, self.vpd[:, blk * 2048:(blk + 1) * 2048], vp16,
                      vp16.ap().rearrange("p g h c -> p (g h c)"))

    def phase_attn_core(self, l, t0, first):
        P = self.P
        j = l // 2
        NC = TT + 128
        bias_d = self.inp["biastab"]
        with P.scope():
            kT2 = P.sb("ac_k", [128, 8, NC], BF16)
            vp = P.sb("ac_v", [128, 9, 8, 2, 128], BF16)
            es = P.sb("ac_es", [1, 4096], F32)
            ones1 = P.sb("ac_one", [1, 128], F32)
            onesp = P.sb("ac_onep", [128, 2, 128], BF16)
            qgs = [P.sb("ac_q%d" % i, [128, 4, TT], BF16) for i in range(2)]
            bts = [P.sb("ac_b%d" % i, [128, 2, TT], F32) for i in range(2)]
            lts = [P.sb("ac_l%d" % i, [128, 512], F32) for i in range(2)]
            Es = [P.sb("ac_E%d" % i, [128, 512], BF16) for i in range(8)]
            recs = [P.sb("ac_r%d" % i, [128, 512], F32) for i in range(2)]
            o16s = [P.sb("ac_o%d" % i, [128, 512], BF16) for i in range(2)]
            P.dma("sp", kT2, kT2.ap(), self.kTd, self.kTd.ap().rearrange("g p n -> p g n"))
            P.dma("sp", vp, vp.ap().rearrange("p b g h c -> p (b g h c)"), self.vpd, self.vpd.ap())
            P.dma("sp", es, es.ap(), self.inp["sinkrep%d" % j], self.inp["sinkrep%d" % j].ap())
            act(P, es, es, AF.Exp)
            memset(P, "dve", ones1, 1.0)
            memset(P, "dve", onesp, 0.0)
            memset(P, "dve", (onesp, onesp[:, 0, 0:64]), 1.0)
            memset(P, "dve", (onesp, onesp[:, 1, 64:128]), 1.0)
            sit = 0
            nit = 0
            esbs = [P.sb("ac_esb%d" % i, [128, 512], F32) for i in range(2)]
            dens = [P.sb("ac_den%d" % i, [128, 512], F32) for i in range(2)]
            for g in range(NKV):
                qg, bt = qgs[g % 2], bts[g % 2]
                P.dma("sp", qg, qg.ap(), self.qTd, self.qTd[g * 512:(g + 1) * 512, :].rearrange("(c p) n -> p c n", p=128))
                P.dma("sp", bt, bt.ap(), bias_d, bias_d[2 * g:2 * g + 2].rearrange("c p n -> p c n"))
                esb = esbs[g % 2]
                psb = self.ps[sit % 4]
                sit += 1
                for pr in range(4):
                    e0 = (g * 4 + pr) * 128
                    mm(P, (psb, psb[:, pr * 128:(pr + 1) * 128]), (es, es[0:1, e0:e0 + 128]), ones1, True, True)
                cp(P, "dve", esb, psb)
                for n in range(8):
                    chunks = []
                    if not (first and n == 0):
                        chunks.append((0, n))
                    chunks.append((1, n + 1))
                    E = {}
                    eset = (nit % 2) * 4
                    for (ci, kb) in chunks:
                        for hf in range(2):
                            psb = self.ps[sit % 4]
                            lt = lts[sit % 2]
                            sit += 1
                            rows = slice(64 * hf, 64 * hf + 64)
                            mm(P, (psb, psb.ap().rearrange("p (a q) -> p a q", a=4)),
                               (kT2, kT2[rows, g, kb * 128:(kb + 1) * 128]),
                               (qg, qg[rows, :, n * 128:(n + 1) * 128]), True, True)
                            btv = bt[:, ci, :].rearrange("p (a b q) -> p a b q", a=4, b=2)[:, :, hf, :]
                            stt(P, (lt, lt.ap().rearrange("p (a q) -> p a q", a=4)),
                                (psb, psb.ap().rearrange("p (a q) -> p a q", a=4)), 0.125, (bt, btv), ALU.mult, ALU.add)
                            Et = Es[eset + ci * 2 + hf]
                            act(P, Et, lt, AF.Exp)
                            E[(ci, hf)] = Et
                    num = self.ps[4 + 2 * (nit % 2)]
                    den = self.ps[5 + 2 * (nit % 2)]
                    combos = [(ci, kb, hf) for (ci, kb) in chunks for hf in range(2)]
                    for k_, (ci, kb, hf) in enumerate(combos):
                        mm(P, num, (vp, vp[:, kb, g, hf, :]), E[(ci, hf)], k_ == 0, k_ == len(combos) - 1)
                    for k_, (ci, kb, hf) in enumerate(combos):
                        mm(P, den, (onesp, onesp[:, hf, :]), E[(ci, hf)], k_ == 0, k_ == len(combos) - 1)
                    rec = recs[nit % 2]
                    dn = dens[nit % 2]
                    o16 = o16s[nit % 2]
                    tt(P, "dve", dn, den, esb, ALU.add)
                    P.op("dve", lambda e, rec=rec, dn=dn: e.reciprocal(out=rec.ap(), in_=dn.ap()), reads=[dn], writes=[rec])
                    tt(P, "dve", o16, num, rec, ALU.mult)
                    r0 = g * 512
                    P.dma("act", self.uT, self.uT[r0:r0 + 512, n * 128:(n + 1) * 128].rearrange("(c p) q -> p c q", p=128),
                          o16, o16.ap().rearrange("p (c q) -> p c q", c=4))
                    nit += 1

    def phase_attn_out(self, l, xin, xout, t0):
        P = self.P
        j = l // 2
        wo = self.inp["wo%d" % j]
        with P.scope():
            o16 = P.sb("ao_o16", [128, KC, TT], BF16)
            wr = [P.sb("ao_w%d" % i, [128, 4096], BF16) for i in range(3)]
            mk = lambda nm, n=2: [P.sb("ao_%s%d" % (nm, i), [128, HT], F32) for i in range(n)]
            zts, sqs, xcs = mk("z"), mk("q"), mk("x")
            ov = []
            for k0 in range(0, KC, 8):
                v = o16.view(o16[:, k0:k0 + 8, :], "o16v%d" % k0)
                P.dma("sp", v, o16[:, k0:k0 + 8, :], self.uT,
                      self.uT[k0 * 128:(k0 + 8) * 128, :].rearrange("(j p) n -> p j n", p=128))
                ov.append(v)
            stats = [(self.ps[4], self.ps[5]), (self.ps[6], self.ps[7])]
            it = 0
            for oc in range(KC):
                w = wr[oc % 3]
                P.dma("pool", w, w.ap(), wo, wo[oc], max_dma_last_dim=8192)
                for half in range(2):
                    hs = slice(half * HT, (half + 1) * HT)
                    py = self.ps[it % 4]
                    for kc in range(KC):
                        mm(P, py, (w, w[:, kc * 128:(kc + 1) * 128]), (ov[kc // 8], o16[:, kc, hs]), kc == 0, kc == KC - 1)
                    self.z_epilogue(py, xin, t0, half, oc, l, 2, zts, sqs, xcs, stats[half], it)
                    it += 1
            for half in range(2):
                self.ln_finalize(xout, t0, half, stats[half][0], stats[half][1], l, 0)


def build_model(layers, ntok, plan):
    M = Model(layers, ntok)
    P = M.P
    M.setup()
    M.phase_mods()
    cur = M.xT
    pingpong = [M.xA, M.xB]
    for i, (l, kind) in enumerate(plan):
        last = i == len(plan) - 1
        nxt = M.outT if last else pingpong[i % 2]
        if kind == "ffn":
            for t in range(M.nt):
                M.phase_ffn_up(l, cur, t * TT)
                M.phase_ffn_down(l, cur, nxt, t * TT)
        elif l % 2 == 0:
            M.ssm_layer(l, cur, nxt)
        else:
            M.attn_layer(l, cur, nxt)
        cur = nxt
    P.finish()
    return M


FULL_PLAN = [(l, k) for l in range(DEPTH) for k in ("mix", "ffn")]
N_CORES_USED = 2
_CACHE = {}


def kernel(**inputs):
    x = np.asarray(inputs["x"])
    bsz, seq, _ = x.shape
    layers = list(range(DEPTH))
    key = ("full", seq)
    if key not in _CACHE:
        _CACHE[key] = build_model(layers, seq, FULL_PLAN)
    M = _CACHE[key]
    in_maps = []
    for b in range(bsz):
        im = prep_inputs(inputs, b, layers, seq)
        in_maps.append({k: v for k, v in im.items() if k in M.inp})
    res = run_bass_kernel_spmd(M.P.nc, in_maps, core_ids=list(range(bsz)))
    out = np.stack([np.ascontiguousarray(res.results[b]["outT"].T) for b in range(bsz)], axis=0)
    return out.astype(np.float32)
```

```python
import math
from contextlib import ExitStack, contextmanager

import numpy as np
import ml_dtypes

import concourse.bass as bass
import concourse.mybir as mybir
from concourse.bass_utils import run_bass_kernel_spmd

F32 = mybir.dt.float32
BF16 = mybir.dt.bfloat16
I32 = mybir.dt.int32
AF = mybir.ActivationFunctionType
ALU = mybir.AluOpType
AX = mybir.AxisListType

SEM_ROLL = 30000

D = 4096
KC = 32
DEPTH = 4
DFF = 11008
NJ = 86
TT = 1024
HT = 512
NHEAD = 64
NKV = 8
ALPHA = (2 * DEPTH) ** 0.25
LN_EPS = 1e-5
EPS_P = LN_EPS / (ALPHA * ALPHA)
PI_LO = 3.1415925
TWO_PI = 2.0 * math.pi
CW1 = 6.28125
CW2 = TWO_PI - 6.28125
MAGIC = 12582912.0


class Buf:
    def __init__(self, prog, name, base_ap, space, root=None):
        self.prog = prog
        self.name = name
        self.base = base_ap
        self.space = space
        self.w = {}
        self.r = {}
        self.sem = None
        self.ndma = 0
        self.root = root if root is not None else self

    def ap(self):
        return self.base

    def __getitem__(self, idx):
        return self.base[idx]

    def view(self, ap, name=None):
        return Buf(self.prog, name or self.name + "_v", ap, self.space, root=self.root)


class Prog:
    ENG = ("pe", "act", "dve", "pool", "sp")

    def __init__(self):
        self.nc = bass.Bass("TRN2", target_bir_lowering=False)
        self.es = ExitStack()
        self.rec = {e: [] for e in self.ENG}
        self.cnt = {e: 0 for e in self.ENG}
        self.esem = {}
        self.nsem = 0
        for e in ("pe", "act", "dve", "pool"):
            self.esem[e] = self._newsem(e)
        self.waited = {e: {} for e in self.ENG}
        self.dma_owner = {}
        self.free_dma_sems = []
        self.out_bufs = []
        self.ninst = 0
        self.scopes = []

    def _newsem(self, name):
        self.nsem += 1
        s = self.es.enter_context(self.nc.semaphore("s%d_%s" % (self.nsem, name)))
        return s

    def dram(self, name, shape, dtype, kind="Internal"):
        t = self.nc.dram_tensor(name, list(shape), dtype, kind=kind)
        b = Buf(self, name, t.ap(), "dram")
        if kind == "ExternalOutput":
            self.out_bufs.append(b)
        return b

    def sb(self, name, shape, dtype):
        st = self.scopes[-1][0] if self.scopes else self.es
        self.nsb = getattr(self, "nsb", 0) + 1
        name = "%s_%d" % (name, self.nsb)
        t = st.enter_context(self.nc.sbuf_tensor(name, list(shape), dtype))
        b = Buf(self, name, t[:], "sb")
        if self.scopes:
            self.scopes[-1][1].append(b)
        return b

    def ps(self, name, shape, dtype):
        t = self.es.enter_context(self.nc.psum_tensor(name, list(shape), dtype))
        return Buf(self, name, t[:], "ps")

    @contextmanager
    def scope(self):
        st = ExitStack()
        bufs = []
        self.scopes.append((st, bufs))
        try:
            yield
        finally:
            self.barrier()
            self.scopes.pop()
            for b in bufs:
                if b.sem is not None:
                    self.free_dma_sems.append((b.sem, b.ndma))
                    del self.dma_owner[id(b.sem)]
            st.close()

    def barrier(self):
        tok = {}
        for e in ("pe", "act", "dve", "pool"):
            tok[self.esem[e]] = self.cnt[e]
        for o in list(self.dma_owner.values()):
            tok[o.sem] = 16 * o.ndma
        for e in self.ENG:
            self._emit_waits(e, dict(tok), force_pe=True)

    def _emit_waits(self, eng, tokens, force_pe=False):
        for sem, val in tokens.items():
            owner = self.dma_owner.get(id(sem))
            if owner is not None:
                val = 16 * owner.ndma
            elif eng == "pe" and sem is self.esem["pe"] and not force_pe:
                continue
            if val <= 0:
                continue
            if self.waited[eng].get(id(sem), -1) >= val:
                continue
            self.waited[eng][id(sem)] = val
            self.rec[eng].append(("wait", sem, val))

    @staticmethod
    def _merge(tok, d):
        for s, v in d.items():
            if tok.get(s, -1) < v:
                tok[s] = v

    def op(self, eng, fn, reads=(), writes=()):
        tok = {}
        for b in reads:
            self._merge(tok, b.w)
        for b in writes:
            self._merge(tok, b.w)
            self._merge(tok, b.r)
        self._emit_waits(eng, tok)
        if self.cnt[eng] >= SEM_ROLL:
            self.esem[eng] = self._newsem(eng)
            self.cnt[eng] = 0
        self.cnt[eng] += 1
        sem = self.esem[eng]
        c = self.cnt[eng]
        self.rec[eng].append(("op", fn, sem, 1))
        self.ninst += 1
        for b in reads:
            b.r[sem] = c
        for b in writes:
            b.w = {sem: c}
            b.r = {}

    def _dma_sem(self, owner):
        owner = owner.root
        if owner.sem is None:
            if self.free_dma_sems:
                owner.sem, owner.ndma = self.free_dma_sems.pop()
            else:
                owner.sem = self._newsem("d_" + owner.name)
            self.dma_owner[id(owner.sem)] = owner
        return owner

    def dma(self, q, dst, dst_ap, src, src_ap, **kw):
        tok = {}
        self._merge(tok, src.w)
        self._merge(tok, dst.w)
        self._merge(tok, dst.r)
        self._emit_waits(q, tok)
        owner = dst if dst.space == "sb" else src
        owner = self._dma_sem(owner)
        owner.ndma += 1
        sem = owner.sem
        val = 16 * owner.ndma
        self.rec[q].append(("op", (lambda e: e.dma_start(out=dst_ap, in_=src_ap, **kw)), sem, 16))
        self.ninst += 1
        src.r[sem] = val
        dst.w = {sem: val}
        dst.r = {}

    def finish(self):
        self.barrier()
        engmap = {"pe": "tensor", "act": "scalar", "dve": "vector", "pool": "gpsimd", "sp": "sync"}
        with self.nc.Block() as block:
            for e in self.ENG:
                recs = self.rec[e]

                def body(engine, recs=recs):
                    for r in recs:
                        if r[0] == "wait":
                            engine.wait_ge(r[1], r[2])
                        else:
                            r[1](engine).then_inc(r[2], r[3])

                getattr(block, engmap[e])(body)
        return self.nc


def _ap(x):
    return x[1] if isinstance(x, tuple) else x.ap()


def _bf(x):
    return x[0] if isinstance(x, tuple) else x


def tt(P, eng, out, a, b, op):
    oa, aa, ba = _ap(out), _ap(a), _ap(b)
    P.op(eng, lambda e: e.tensor_tensor(out=oa, in0=aa, in1=ba, op=op), reads=[_bf(a), _bf(b)], writes=[_bf(out)])


def ts(P, eng, out, a, s1, s2, op0, op1=None, extra_reads=()):
    oa, aa = _ap(out), _ap(a)
    if eng == "pool" and op1 is None:
        if op0 == ALU.mult:
            op1, s2 = ALU.add, 0.0
        elif op0 == ALU.add:
            op1, s2 = ALU.mult, 1.0
    if op1 is None:
        P.op(eng, lambda e: e.tensor_scalar(out=oa, in0=aa, scalar1=s1, scalar2=None, op0=op0),
             reads=[_bf(a)] + list(extra_reads), writes=[_bf(out)])
    else:
        P.op(eng, lambda e: e.tensor_scalar(out=oa, in0=aa, scalar1=s1, scalar2=s2, op0=op0, op1=op1),
             reads=[_bf(a)] + list(extra_reads), writes=[_bf(out)])


def stt(P, out, a, s, b, op0, op1, extra_reads=()):
    oa, aa, ba = _ap(out), _ap(a), _ap(b)
    P.op("dve", lambda e: e.scalar_tensor_tensor(out=oa, in0=aa, scalar=s, in1=ba, op0=op0, op1=op1),
         reads=[_bf(a), _bf(b)] + list(extra_reads), writes=[_bf(out)])


def act(P, out, a, func, scale=None, bias=None, extra_reads=()):
    oa, aa = _ap(out), _ap(a)
    kw = {}
    if scale is not None:
        kw["scale"] = scale
    if bias is not None:
        kw["bias"] = bias
    P.op("act", lambda e: e.activation(out=oa, in_=aa, func=func, **kw), reads=[_bf(a)] + list(extra_reads),
         writes=[_bf(out)])


def cp(P, eng, out, a):
    oa, aa = _ap(out), _ap(a)
    if eng == "act":
        P.op("act", lambda e: e.activation(out=oa, in_=aa, func=AF.Copy), reads=[_bf(a)], writes=[_bf(out)])
    else:
        P.op(eng, lambda e: e.tensor_copy(out=oa, in_=aa), reads=[_bf(a)], writes=[_bf(out)])


def mm(P, out, lhsT, rhs, start, stop):
    oa, la, ra = _ap(out), _ap(lhsT), _ap(rhs)
    P.op("pe", lambda e: e.matmul(oa, lhsT=la, rhs=ra, start=start, stop=stop), reads=[_bf(lhsT), _bf(rhs)],
         writes=[_bf(out)])


def memset(P, eng, out, val):
    oa = _ap(out)
    P.op(eng, lambda e: e.memset(oa, val), writes=[_bf(out)])


def _fm(v):
    v = np.asarray(v, np.float32)
    return np.ascontiguousarray(v.reshape(-1, 128).T)


def _pieces(w, ncol=128):
    K, N = w.shape
    a = w.reshape(K // 128, 128, N // ncol, ncol)
    return np.ascontiguousarray(a.transpose(2, 1, 0, 3)).reshape(N // ncol, 128, (K // 128) * ncol)


class VecLayout:
    def __init__(self):
        self.off = {}
        self.n = 0

    def add(self, name, ncols):
        self.off[name] = self.n
        self.n += ncols


def vec_layout():
    L = VecLayout()
    L.add("c", 32)
    for m in range(6):
        L.add("bada%d" % m, 32)
    for l in range(DEPTH):
        for m in range(6):
            L.add("tab%d_%d" % (l, m), 32)
        for s in range(2):
            L.add("lng%d_%d" % (l, s), 32)
            L.add("lnb%d_%d" % (l, s), 32)
    for j in range(2):
        L.add("ssmd%d" % j, 32)
        L.add("bq%d" % j, 32)
        L.add("bk%d" % j, 8)
    return L


VL = vec_layout()


def _rel_bucket_np(dist):
    max_exact = 16
    d = np.maximum(dist, 0)
    log_ratio = np.log(np.maximum(d, 1).astype(np.float32) / max_exact) / math.log(128 / max_exact)
    large = max_exact + (log_ratio * (32 - max_exact)).astype(np.int32)
    large = np.minimum(large, 31)
    return np.where(d < max_exact, d, large)


def prep_inputs(inp, b, layers, ntok):
    m = {}
    m["xT"] = np.ascontiguousarray(np.asarray(inp["x"][b, :ntok], np.float32).T)
    vec = np.zeros((128, VL.n), np.float32)

    def put(name, v):
        a = _fm(v)
        vec[:, VL.off[name]:VL.off[name] + a.shape[1]] = a

    put("c", inp["c"][b])
    for mm_ in range(6):
        put("bada%d" % mm_, inp["b_ada"][mm_ * D:(mm_ + 1) * D])
    for l in range(DEPTH):
        for mm_ in range(6):
            put("tab%d_%d" % (l, mm_), inp["ada_table"][l, mm_])
        for s in range(2):
            put("lng%d_%d" % (l, s), inp["ln_g"][l, s])
            put("lnb%d_%d" % (l, s), inp["ln_b"][l, s])
    for j in range(2):
        put("ssmd%d" % j, inp["ssm_d"][j])
        bqkv = np.asarray(inp["attn_b_qkv"][j], np.float32)
        put("bq%d" % j, bqkv[:D])
        bk = bqkv[D:D + 512].reshape(8, 64)
        put("bk%d" % j, np.concatenate([bk, bk], axis=1).reshape(-1))
    m["vecs"] = vec
    m["wada"] = np.ascontiguousarray(np.asarray(inp["w_ada"], np.float32))
    for l in layers:
        m["win%d" % l] = _pieces(np.asarray(inp["ffn_w_in"][l], np.float32))
        wd = np.asarray(inp["ffn_w_down"][l], np.float32)
        a = wd.reshape(2, 43, 128, 32, 128).transpose(3, 0, 2, 1, 4)
        m["wdn%d" % l] = np.ascontiguousarray(a).reshape(64, 128, 43 * 128)
        j = l // 2
        if l % 2 == 0:
            m["wval%d" % j] = _pieces(np.asarray(inp["ssm_w_val"][j], np.float32))
            m["wgate%d" % j] = _pieces(np.asarray(inp["ssm_w_gate"][j], np.float32))
            for nm, key in (("bpr", "ssm_b_re"), ("bpi", "ssm_b_im")):
                bsrc = np.asarray(inp[key][j], np.float32)
                out = np.zeros((128, 128, 128), np.float32)
                for pair in range(128):
                    q = pair % 4
                    for g2 in range(2):
                        g = 2 * pair + g2
                        r0 = (2 * q + g2) * 16
                        out[pair, r0:r0 + 16, g2 * 64:(g2 + 1) * 64] = bsrc[g].T
                m["%s%d" % (nm, j)] = out
            for nm, key in (("cpr", "ssm_c_re"), ("cpi", "ssm_c_im")):
                csrc = np.asarray(inp[key][j], np.float32)
                out = np.zeros((128, 128, 128), np.float32)
                for pair in range(128):
                    q = pair % 4
                    for g2 in range(2):
                        g = 2 * pair + g2
                        c0 = (2 * q + g2) * 16
                        out[pair, g2 * 64:(g2 + 1) * 64, c0:c0 + 16] = csrc[g].T
                m["%s%d" % (nm, j)] = out
            are = np.asarray(inp["ssm_a_re"][j], np.float32).reshape(128, 128).T
            aim = np.asarray(inp["ssm_a_im"][j], np.float32).reshape(128, 128).T
            ldt = np.repeat(np.asarray(inp["ssm_log_dt"][j], np.float32).reshape(128, 2, 1), 64, axis=2)
            ldt = ldt.reshape(128, 128).T
            m["ssc%d" % j] = np.ascontiguousarray(np.concatenate([are, aim, ldt], axis=1))
        else:
            wqkv = np.asarray(inp["attn_w_qkv"][j], np.float32)
            m["wq%d" % j] = _pieces(wqkv[:, :D])
            wk = wqkv[:, D:D + 512].reshape(D, 8, 64)
            wk2 = np.concatenate([wk, wk], axis=2).reshape(D, 1024)
            m["wk%d" % j] = _pieces(wk2)
            m["wv%d" % j] = _pieces(wqkv[:, D + 512:D + 1024])
            m["wo%d" % j] = _pieces(np.asarray(inp["attn_w_o"][j], np.float32))
            bv = np.asarray(inp["attn_b_qkv"][j], np.float32)[D + 512:]
            m["bvrep%d" % j] = np.ascontiguousarray(np.broadcast_to(bv[None, :], (128, 512)))
            sk = np.asarray(inp["attn_sinks"][j], np.float32)
            m["sinkrep%d" % j] = np.ascontiguousarray(np.repeat(sk, 64)[None, :])
    if any(l % 2 == 1 for l in layers):
        rb = np.asarray(inp["rel_bias"], np.float32)
        qi = np.arange(128)[:, None]
        sj = np.arange(256)[None, :]
        dist = qi + 128 - sj
        band = (dist >= 0) & (dist < 128)
        bias = rb[_rel_bucket_np(dist)]
        bias = np.where(band[:, :, None], bias, np.float32(-1e30))
        bt = bias.transpose(1, 2, 0).reshape(2, 128, 8, 8, 128)
        m["biastab"] = np.ascontiguousarray(bt.transpose(2, 0, 1, 3, 4)).reshape(16, 128, 1024)
    return m


class Model:
    def __init__(self, layers, ntok, stop=None):
        self.layers = list(layers)
        self.ntok = ntok
        self.nt = ntok // TT
        self.stop = stop
        P = self.P = Prog()
        self.inp = {}
        self.xT = self.din("xT", [D, ntok], F32)
        self.vecs_d = self.din("vecs", [128, VL.n], F32)
        self.wada = self.din("wada", [D, 6 * D], F32)
        for l in self.layers:
            self.din("win%d" % l, [172, 128, 4096], F32)
            self.din("wdn%d" % l, [64, 128, 43 * 128], F32)
            j = l // 2
            if l % 2 == 0:
                for nm in ("wval", "wgate"):
                    self.din("%s%d" % (nm, j), [32, 128, 4096], F32)
                for nm in ("bpr", "bpi", "cpr", "cpi"):
                    self.din("%s%d" % (nm, j), [128, 128, 128], F32)
                self.din("ssc%d" % j, [128, 384], F32)
            else:
                self.din("wq%d" % j, [32, 128, 4096], F32)
                self.din("wk%d" % j, [8, 128, 4096], F32)
                self.din("wv%d" % j, [4, 128, 4096], F32)
                self.din("wo%d" % j, [32, 128, 4096], F32)
                self.din("bvrep%d" % j, [128, 512], F32)
                self.din("sinkrep%d" % j, [1, 4096], F32)
        if any(l % 2 == 1 for l in self.layers):
            self.din("biastab", [16, 128, 1024], F32)
        self.outT = P.dram("outT", [D, ntok], F32, kind="ExternalOutput")
        self.xA = P.dram("xA", [D, ntok], F32)
        self.xB = P.dram("xB", [D, ntok], F32)
        self.zT = P.dram("zT", [D, TT], F32)
        self.uT = P.dram("uT", [D, TT], BF16)
        self.hidT = P.dram("hidT", [DFF, TT], BF16)
        self.modsD = P.dram("modsD", [1, 6 * D], F32)
        self.tabs = P.dram("tabs", [128, 128, 2048], F32)
        self.cpp = P.dram("cpp", [128, 128, 384], BF16)
        self.qTd = P.dram("qTd", [D, TT], BF16)
        self.kTd = P.dram("kTd", [8, 128, TT + 128], BF16)
        self.vpd = P.dram("vpd", [128, 9 * 8 * 2 * 128], BF16)
        self.vecs = P.sb("vecs_sb", [128, VL.n], F32)
        self.mods = P.sb("mods_sb", [128, DEPTH * 6 * 32], F32)
        self.ones_ln = P.sb("ones_ln", [128, 128], F32)
        self.consts = P.sb("consts", [128, 8], F32)
        self.st_re = P.sb("st_re", [128, 128], F32)
        self.st_im = P.sb("st_im", [128, 128], F32)
        self.psall = P.ps("psall", [128, 4096], F32)
        self.ps = [self.psall.view(self.psall[:, i * 512:(i + 1) * 512], "psb%d" % i) for i in range(8)]
        self.psw = [self.psall.view(self.psall[:, i * 1024:(i + 1) * 1024], "psw%d" % i) for i in range(2)]

    def din(self, name, shape, dtype):
        b = self.P.dram(name, shape, dtype, kind="ExternalInput")
        self.inp[name] = b
        return b

    def vcol(self, name, i=0, n=1):
        o = VL.off[name] + i
        return self.vecs[:, o:o + n]

    def mcol(self, l, m, i=0, n=1):
        o = (l * 6 + m) * 32 + i
        return self.mods[:, o:o + n]

    def setup(self):
        P = self.P
        P.dma("sp", self.vecs, self.vecs.ap(), self.vecs_d, self.vecs_d.ap())
        memset(P, "dve", self.ones_ln, 1.0 / D)
        memset(P, "dve", (self.consts, self.consts[:, 0:1]), math.pi / 2)
        memset(P, "dve", (self.consts, self.consts[:, 1:2]), EPS_P)
        memset(P, "dve", (self.consts, self.consts[:, 2:3]), 0.0)

    def phase_mods(self):
        P = self.P
        with P.scope():
            scs = P.sb("scs", [128, 32], BF16)
            wts = [P.sb("wadat%d" % i, [128, 4096], BF16) for i in range(4)]
            row = P.sb("modrow", [1, 4096], F32)
            msh = P.sb("msh", [128, 192], F32)
            act(P, scs, (self.vecs, self.vcol("c", 0, 32)), AF.Silu)
            n = 0
            for cg in range(6):
                for kc in range(KC):
                    wt = wts[n % 4]
                    n += 1
                    P.dma("pool", wt, wt.ap(), self.wada, self.wada[kc * 128:(kc + 1) * 128, cg * D:(cg + 1) * D],
                          max_dma_last_dim=8192)
                    for b in range(8):
                        mm(P, (self.ps[b], self.ps[b][0:1, :]), (scs, scs[:, kc:kc + 1]),
                           (wt, wt[:, b * 512:(b + 1) * 512]), kc == 0, kc == KC - 1)
                for b in range(8):
                    cp(P, "act" if b % 2 else "dve", (row, row[0:1, b * 512:(b + 1) * 512]),
                       (self.ps[b], self.ps[b][0:1, :]))
                P.dma("sp", self.modsD, self.modsD[0:1, cg * D:(cg + 1) * D], row, row.ap())
            P.dma("sp", msh, msh.ap(), self.modsD, self.modsD.ap().rearrange("o (j p) -> p (o j)", p=128),
                  allow_slow_non_contiguous=True)
            o = VL.off["bada0"]
            tt(P, "dve", msh, msh, (self.vecs, self.vecs[:, o:o + 192]), ALU.add)
            for l in range(DEPTH):
                o = VL.off["tab%d_0" % l]
                ml = (self.mods, self.mods[:, l * 192:(l + 1) * 192])
                tt(P, "dve", ml, msh, (self.vecs, self.vecs[:, o:o + 192]), ALU.add)
                for m in (1, 4):
                    mc = (self.mods, self.mcol(l, m, 0, 32))
                    ts(P, "dve", mc, mc, 1.0, None, ALU.add)
                for m in (2, 5):
                    mc = (self.mods, self.mcol(l, m, 0, 32))
                    ts(P, "dve", mc, mc, 1.0, 1.0 / ALPHA, ALU.add, ALU.mult)

    def ln_finalize(self, xout, t0, half, st_mu, st_e2, l, s):
        P = self.P
        with P.scope():
            mu = P.sb("ln_mu", [128, HT], F32)
            rstd = P.sb("ln_rstd", [128, HT], F32)
            tmp = P.sb("ln_tmp", [128, HT], F32)
            zc = [P.sb("ln_z%d" % i, [128, HT], F32) for i in range(4)]
            oc_ = [P.sb("ln_o%d" % i, [128, HT], F32) for i in range(3)]
            c0 = t0 + half * HT

            def load(oc):
                z = zc[oc % 4]
                P.dma("sp", z, z.ap(), self.zT, self.zT[oc * 128:(oc + 1) * 128, half * HT:(half + 1) * HT])

            load(0)
            load(1)
            load(2)
            cp(P, "act", mu, st_mu)
            tt(P, "dve", tmp, mu, mu, ALU.mult)
            tt(P, "dve", tmp, st_e2, tmp, ALU.subtract)
            act(P, tmp, tmp, AF.Sqrt, bias=self.consts[:, 1:2], extra_reads=[self.consts])
            P.op("dve", lambda e: e.reciprocal(out=rstd.ap(), in_=tmp.ap()), reads=[tmp], writes=[rstd])
            for oc in range(KC):
                z = zc[oc % 4]
                o = oc_[oc % 3]
                tt(P, "dve", z, z, mu, ALU.subtract)
                tt(P, "dve", z, z, rstd, ALU.mult)
                if oc + 3 < KC:
                    load(oc + 3)
                act(P, o, z, AF.Identity, scale=self.vcol("lng%d_%d" % (l, s), oc), bias=self.vcol("lnb%d_%d" % (l, s), oc),
                    extra_reads=[self.vecs])
                P.dma("act", xout, xout[oc * 128:(oc + 1) * 128, c0:c0 + HT], o, o.ap())

    def z_epilogue(self, y, xin, t0, half, oc, l, gm, zts, sqs, xcs, stats, idx, pre=None):
        P = self.P
        zt = zts[idx % len(zts)]
        sq = sqs[idx % len(sqs)]
        xc = xcs[idx % len(xcs)]
        c0 = t0 + half * HT
        P.dma("sp", xc, xc.ap(), xin, xin[oc * 128:(oc + 1) * 128, c0:c0 + HT])
        stt(P, zt, y, self.mcol(l, gm, oc), xc, ALU.mult, ALU.add, extra_reads=[self.mods])
        act(P, sq, zt, AF.Square)
        mm(P, stats[0], self.ones_ln, zt, oc == 0, oc == KC - 1)
        mm(P, stats[1], self.ones_ln, sq, oc == 0, oc == KC - 1)
        P.dma("act", self.zT, self.zT[oc * 128:(oc + 1) * 128, half * HT:(half + 1) * HT], zt, zt.ap())

    def build_h(self, hT, xin, c_lo, ncol, l, m_shift, m_scale, xcs, zero_first=0):
        P = self.P
        views = []
        for kc in range(KC):
            xc = xcs[kc % len(xcs)]
            hv = hT.view(hT[:, kc, :], "h_kc%d" % kc)
            views.append(hv)
            if zero_first:
                memset(P, "pool", (hv, hT[:, kc, 0:zero_first]), 0.0)
            lo = c_lo + zero_first
            n = ncol - zero_first
            P.dma("sp", xc, xc[:, 0:n], xin, xin[kc * 128:(kc + 1) * 128, lo:lo + n])
            ts(P, "pool" if kc % 4 == 3 else "dve", (hv, hT[:, kc, zero_first:ncol]), (xc, xc[:, 0:n]),
               self.mcol(l, m_scale, kc), self.mcol(l, m_shift, kc), ALU.mult, ALU.add, extra_reads=[self.mods])
        return views

    def phase_ffn_up(self, l, xin, t0):
        P = self.P
        win = self.inp["win%d" % l]
        with P.scope():
            hT = P.sb("f1_hT", [128, KC, TT], BF16)
            xcs = [P.sb("f1_xc%d" % i, [128, TT], F32) for i in range(4)]
            wg = [P.sb("f1_wg%d" % i, [128, 4096], BF16) for i in range(3)]
            wu = [P.sb("f1_wu%d" % i, [128, 4096], BF16) for i in range(3)]
            sg = [P.sb("f1_sg%d" % i, [128, HT], F32) for i in range(2)]
            hid = [P.sb("f1_hid%d" % i, [128, HT], BF16) for i in range(3)]
            hv = self.build_h(hT, xin, t0, TT, l, 3, 4, xcs)
            it = 0
            for j in range(NJ):
                g_t = wg[j % 3]
                u_t = wu[j % 3]
                P.dma("pool", g_t, g_t.ap(), win, win[j], max_dma_last_dim=8192)
                P.dma("pool", u_t, u_t.ap(), win, win[NJ + j], max_dma_last_dim=8192)
                for half in range(2):
                    pg = self.ps[(it % 2) * 2]
                    pu = self.ps[(it % 2) * 2 + 1]
                    for kc in range(KC):
                        mm(P, pg, (g_t, g_t[:, kc * 128:(kc + 1) * 128]), (hv[kc], hT[:, kc, half * HT:(half + 1) * HT]),
                           kc == 0, kc == KC - 1)
                    for kc in range(KC):
                        mm(P, pu, (u_t, u_t[:, kc * 128:(kc + 1) * 128]), (hv[kc], hT[:, kc, half * HT:(half + 1) * HT]),
                           kc == 0, kc == KC - 1)
                    s_ = sg[it % 2]
                    h_ = hid[it % 3]
                    act(P, s_, pg, AF.Silu)
                    tt(P, "dve", h_, pu, s_, ALU.mult)
                    P.dma("act", self.hidT, self.hidT[j * 128:(j + 1) * 128, half * HT:(half + 1) * HT], h_, h_.ap())
                    it += 1

    def phase_ffn_down(self, l, xin, xout, t0):
        P = self.P
        wdn = self.inp["wdn%d" % l]
        for half in range(2):
            with P.scope():
                hid = P.sb("f2_hid", [128, NJ, HT], BF16)
                wp = [P.sb("f2_w%d" % i, [128, 43 * 128], BF16) for i in range(3)]
                zts = [P.sb("f2_z%d" % i, [128, HT], F32) for i in range(2)]
                sqs = [P.sb("f2_q%d" % i, [128, HT], F32) for i in range(2)]
                xcs = [P.sb("f2_x%d" % i, [128, HT], F32) for i in range(2)]
                hvs = []
                for j0 in range(0, NJ, 8):
                    j1 = min(NJ, j0 + 8)
                    v = hid.view(hid[:, j0:j1, :], "hidv%d" % j0)
                    P.dma("sp", v, hid[:, j0:j1, :], self.hidT,
                          self.hidT[j0 * 128:j1 * 128, half * HT:(half + 1) * HT].rearrange("(j p) n -> p j n", p=128))
                    hvs.append(v)
                stats = (self.ps[6], self.ps[7])
                for oc in range(KC):
                    py = self.ps[oc % 2]
                    for kh in range(2):
                        w = wp[(oc * 2 + kh) % 3]
                        P.dma("pool", w, w.ap(), wdn, wdn[oc * 2 + kh], max_dma_last_dim=8192)
                        for k in range(43):
                            j = kh * 43 + k
                            mm(P, py, (w, w[:, k * 128:(k + 1) * 128]), (hvs[j // 8], hid[:, j, :]),
                               j == 0, j == NJ - 1)
                    self.z_epilogue(py, xin, t0, half, oc, l, 5, zts, sqs, xcs, stats, oc)
                self.ln_finalize(xout, t0, half, stats[0], stats[1], l, 1)

    def _range_reduce(self, eng_ang, out, ang, kk):
        P = self.P
        ts(P, "dve", kk, ang, 1.0 / TWO_PI, MAGIC, ALU.mult, ALU.add)
        ts(P, "dve", kk, kk, -MAGIC, None, ALU.add)
        stt(P, out, kk, -CW1, ang, ALU.mult, ALU.add)
        stt(P, out, kk, -CW2, out, ALU.mult, ALU.add)
        ts(P, "pool", out, out, PI_LO, -PI_LO, ALU.min, ALU.max)

    def _sincos(self, sn, cs, r, tmp):
        P = self.P
        act(P, sn, r, AF.Sin)
        act(P, tmp, r, AF.Abs)
        act(P, cs, tmp, AF.Sin, scale=-1.0, bias=self.consts[:, 0:1], extra_reads=[self.consts])

    def ssm_layer(self, l, xin, xout):
        P = self.P
        j = l // 2
        with P.scope():
            rmag = P.sb("s_rmag", [128, 128], F32)
            with P.scope():
                ssc = P.sb("s_ssc", [128, 384], F32)
                P.dma("sp", ssc, ssc.ap(), self.inp["ssc%d" % j], self.inp["ssc%d" % j].ap())
                a_re = (ssc, ssc[:, 0:128])
                a_im = (ssc, ssc[:, 128:256])
                nm = ["dt", "th", "kk", "thr", "sn1", "cs1", "tmp", "abre", "abim", "den", "nr", "kre", "kim", "nkim",
                      "nkre", "t2"]
                T = {n: P.sb("s_" + n, [128, 128], F32) for n in nm}
                act(P, T["dt"], (ssc, ssc[:, 256:384]), AF.Exp)
                tt(P, "dve", T["tmp"], a_re, T["dt"], ALU.mult)
                act(P, rmag, T["tmp"], AF.Exp)
                tt(P, "dve", T["th"], a_im, T["dt"], ALU.mult)
                self._range_reduce("dve", T["thr"], T["th"], T["kk"])
                self._sincos(T["sn1"], T["cs1"], T["thr"], T["tmp"])
                tt(P, "dve", T["abre"], rmag, T["cs1"], ALU.mult)
                tt(P, "dve", T["abim"], rmag, T["sn1"], ALU.mult)
                tt(P, "dve", T["den"], a_re, a_re, ALU.mult)
                tt(P, "dve", T["tmp"], a_im, a_im, ALU.mult)
                tt(P, "dve", T["den"], T["den"], T["tmp"], ALU.add)
                P.op("dve", lambda e: e.reciprocal(out=T["den"].ap(), in_=T["den"].ap()), reads=[T["den"]],
                     writes=[T["den"]])
                ts(P, "dve", T["nr"], T["abre"], -1.0, None, ALU.add)
                tt(P, "dve", T["kre"], T["nr"], a_re, ALU.mult)
                tt(P, "dve", T["tmp"], T["abim"], a_im, ALU.mult)
                tt(P, "dve", T["kre"], T["kre"], T["tmp"], ALU.add)
                tt(P, "dve", T["kre"], T["kre"], T["den"], ALU.mult)
                tt(P, "dve", T["kim"], T["abim"], a_re, ALU.mult)
                tt(P, "dve", T["tmp"], T["nr"], a_im, ALU.mult)
                tt(P, "dve", T["kim"], T["kim"], T["tmp"], ALU.subtract)
                tt(P, "dve", T["kim"], T["kim"], T["den"], ALU.mult)
                ts(P, "dve", T["nkim"], T["kim"], -1.0, None, ALU.mult)
                ts(P, "dve", T["nkre"], T["kre"], -1.0, None, ALU.mult)
                cr = [P.sb("s_cr%d" % i, [128, 128], F32) for i in range(2)]
                ci = [P.sb("s_ci%d" % i, [128, 128], F32) for i in range(2)]
                c1 = [P.sb("s_c1%d" % i, [128, 128], F32) for i in range(2)]
                cb = [P.sb("s_cb%d" % i, [128, 384], BF16) for i in range(2)]
                cpr = self.inp["cpr%d" % j]
                cpi = self.inp["cpi%d" % j]
                for pair in range(128):
                    a, b, c_, o = cr[pair % 2], ci[pair % 2], c1[pair % 2], cb[pair % 2]
                    P.dma("sp", a, a.ap(), cpr, cpr[pair])
                    P.dma("sp", b, b.ap(), cpi, cpi[pair])
                    pc = slice(pair, pair + 1)
                    ts(P, "pool", c_, a, T["kre"][:, pc], None, ALU.mult, extra_reads=[T["kre"]])
                    stt(P, (o, o[:, 0:128]), b, T["nkim"][:, pc], c_, ALU.mult, ALU.add, extra_reads=[T["nkim"]])
                    ts(P, "pool", c_, a, T["nkim"][:, pc], None, ALU.mult, extra_reads=[T["nkim"]])
                    stt(P, (o, o[:, 128:256]), b, T["nkre"][:, pc], c_, ALU.mult, ALU.add, extra_reads=[T["nkre"]])
                    ts(P, "dve", (o, o[:, 256:384]), (o, o[:, 0:128]), -1.0, None, ALU.mult)
                    P.dma("sp", self.cpp, self.cpp[pair], o, o.ap())
                io_i = P.sb("s_ioi", [128, TT], I32)
                io_f = P.sb("s_iof", [128, TT], F32)
                P.op("pool", lambda e: e.iota(io_i.ap(), pattern=[[1, TT]], base=1, channel_multiplier=0), writes=[io_i])
                cp(P, "dve", io_f, io_i)
                ang = [P.sb("s_ang%d" % i, [128, TT], F32) for i in range(2)]
                kk = [P.sb("s_kk%d" % i, [128, TT], F32) for i in range(2)]
                rr = [P.sb("s_rr%d" % i, [128, TT], F32) for i in range(2)]
                tb = [P.sb("s_tb%d" % i, [128, 2, TT], F32) for i in range(2)]
                for pair in range(128):
                    a, k_, r_, t_ = ang[pair % 2], kk[pair % 2], rr[pair % 2], tb[pair % 2]
                    ts(P, "pool", a, io_f, T["thr"][:, pair:pair + 1], None, ALU.mult, extra_reads=[T["thr"]])
                    self._range_reduce("dve", r_, a, k_)
                    act(P, (t_, t_[:, 1, :]), r_, AF.Sin)
                    act(P, a, r_, AF.Abs)
                    act(P, (t_, t_[:, 0, :]), a, AF.Sin, scale=-1.0, bias=self.consts[:, 0:1], extra_reads=[self.consts])
                    P.dma("sp", self.tabs, self.tabs[pair], t_, t_.ap().rearrange("p a n -> p (a n)"))
            memset(P, "dve", self.st_re, 0.0)
            memset(P, "dve", self.st_im, 0.0)
            for t in range(self.nt):
                self.phase_ssm_tile(l, xin, t * TT, rmag)
                self.phase_ssm_proj(l, xin, xout, t * TT)

    def phase_ssm_tile(self, l, xin, t0, rmag):
        P = self.P
        j = l // 2
        bpr = self.inp["bpr%d" % j]
        bpi = self.inp["bpi%d" % j]
        with P.scope():
            xcs = [P.sb("st_xc%d" % i, [128, TT], F32) for i in range(2)]
            h32s = [P.sb("st_h32%d" % i, [128, TT], F32) for i in range(2)]
            h16s = [P.sb("st_h16%d" % i, [128, TT], BF16) for i in range(2)]
            bps = [P.sb("st_bp%d" % i, [128, 8, 128], BF16) for i in range(2)]
            cpts = [P.sb("st_cp%d" % i, [128, 4, 384], BF16) for i in range(2)]
            tabs = [P.sb("st_tab%d" % i, [128, 2, TT], F32) for i in range(2)]
            rmats = [P.sb("st_rm%d" % i, [128, TT], F32) for i in range(2)]
            ones = P.sb("st_ones", [128, TT], F32)
            memset(P, "dve", ones, 1.0)
            mk = lambda nm, dt=F32, n=2, w=TT: [P.sb("st_%s%d" % (nm, i), [128, w], dt) for i in range(n)]
            t1, t2, t3, t4 = mk("t1", n=1)[0], mk("t2", n=1)[0], mk("t3", n=1)[0], mk("t4", n=1)[0]
            dsets = [[P.sb("st_dd%d_%d" % (a, b), [128, TT], BF16) for b in range(4)] for a in range(2)]
            utr, uti = mk("utr"), mk("uti")
            srs, sis = mk("sr"), mk("si")
            yv, ga, gb = mk("yv", w=HT), mk("ga", w=HT), mk("gb", w=HT)
            u16 = mk("u16", BF16, 3, HT)
            sm = P.sb("st_sm", [128, 8], F32)
            ure, uim = self.psw[0], self.psw[1]

            def prep2(dc):
                xc, h32, h16 = xcs[dc % 2], h32s[dc % 2], h16s[dc % 2]
                ts(P, "dve", h32, xc, self.mcol(l, 1, dc), self.mcol(l, 0, dc), ALU.mult, ALU.add, extra_reads=[self.mods])
                cp(P, "act", h16, h32)

            def prep(dc):
                xc = xcs[dc % 2]
                bp, cpt = bps[dc % 2], cpts[dc % 2]
                P.dma("sp", xc, xc.ap(), xin, xin[dc * 128:(dc + 1) * 128, t0:t0 + TT])
                P.dma("pool", bp, bp[:, 0:4, :], bpr, bpr[dc * 4:(dc + 1) * 4].rearrange("q p c -> p q c"))
                P.dma("pool", bp, bp[:, 4:8, :], bpi, bpi[dc * 4:(dc + 1) * 4].rearrange("q p c -> p q c"))
                P.dma("sp", cpt, cpt.ap(), self.cpp, self.cpp[dc * 4:(dc + 1) * 4].rearrange("q p c -> p q c"))

            def load_tab(pair):
                tb = tabs[pair % 2]
                P.dma("sp", tb, tb.ap().rearrange("p a n -> p (a n)"), self.tabs, self.tabs[pair])
                act(P, rmats[pair % 2], ones, AF.Copy, scale=rmag[:, pair:pair + 1], extra_reads=[rmag])

            def bmm(pair):
                dc, q = pair // 4, pair % 4
                bp, h16 = bps[dc % 2], h16s[dc % 2]
                for half in range(2):
                    hs = slice(half * HT, (half + 1) * HT)
                    mm(P, (ure, ure[:, hs]), (bp, bp[:, q, :]), (h16, h16[:, hs]), True, True)
                    mm(P, (uim, uim[:, hs]), (bp, bp[:, 4 + q, :]), (h16, h16[:, hs]), True, True)

            prep(0)
            prep2(0)
            load_tab(0)
            bmm(0)
            for pair in range(128):
                dc, q = pair // 4, pair % 4
                i2 = pair % 2
                tb, rm = tabs[i2], rmats[i2]
                cpt = cpts[dc % 2]
                yb = (self.ps[4 + 2 * (dc % 2)], self.ps[5 + 2 * (dc % 2)])
                c_ = (tb, tb[:, 0, :])
                s_ = (tb, tb[:, 1, :])
                if q == 0 and dc + 1 < KC:
                    prep(dc + 1)
                tt(P, "dve", t1, ure, c_, ALU.mult)
                tt(P, "dve", t2, uim, s_, ALU.mult)
                tt(P, "dve", t3, uim, c_, ALU.mult)
                tt(P, "dve", t4, ure, s_, ALU.mult)
                if pair + 1 < 128:
                    load_tab(pair + 1)
                    bmm(pair + 1)
                if q == 1 and dc + 1 < KC:
                    prep2(dc + 1)
                tt(P, "dve", utr[i2], t1, t2, ALU.add)
                tt(P, "dve", uti[i2], t3, t4, ALU.subtract)
                sr, si = srs[i2], sis[i2]
                ini_r, ini_i = self.st_re[:, pair:pair + 1], self.st_im[:, pair:pair + 1]
                P.op("dve", lambda e, sr=sr, rm=rm, u=utr[i2], ini=ini_r: e.tensor_tensor_scan(
                    out=sr.ap(), data0=rm.ap(), data1=u.ap(), initial=ini, op0=ALU.mult, op1=ALU.add),
                    reads=[rm, utr[i2], self.st_re], writes=[sr])
                P.op("dve", lambda e, si=si, rm=rm, u=uti[i2], ini=ini_i: e.tensor_tensor_scan(
                    out=si.ap(), data0=rm.ap(), data1=u.ap(), initial=ini, op0=ALU.mult, op1=ALU.add),
                    reads=[rm, uti[i2], self.st_im], writes=[si])
                dd = dsets[i2]
                tt(P, "pool", dd[0], sr, c_, ALU.mult)
                tt(P, "pool", dd[1], si, s_, ALU.mult)
                tt(P, "pool", dd[2], sr, s_, ALU.mult)
                tt(P, "dve" if pair % 2 else "pool", dd[3], si, c_, ALU.mult)
                for half in range(2):
                    hs = slice(half * HT, (half + 1) * HT)
                    mm(P, yb[half], (cpt, cpt[:, q, 0:128]), (dd[0], dd[0][:, hs]), q == 0, False)
                    mm(P, yb[half], (cpt, cpt[:, q, 256:384]), (dd[1], dd[1][:, hs]), False, False)
                    mm(P, yb[half], (cpt, cpt[:, q, 128:256]), (dd[2], dd[2][:, hs]), False, False)
                    mm(P, yb[half], (cpt, cpt[:, q, 128:256]), (dd[3], dd[3][:, hs]), False, q == 3)
                L = slice(TT - 1, TT)
                smv = lambda k: (sm, sm[:, k:k + 1])
                act(P, smv(0), (sr, sr[:, L]), AF.Copy, scale=tb[:, 0, L], extra_reads=[tb])
                act(P, smv(1), (si, si[:, L]), AF.Copy, scale=tb[:, 1, L], extra_reads=[tb])
                act(P, smv(2), (sr, sr[:, L]), AF.Copy, scale=tb[:, 1, L], extra_reads=[tb])
                act(P, smv(3), (si, si[:, L]), AF.Copy, scale=tb[:, 0, L], extra_reads=[tb])
                act(P, (self.st_re, self.st_re[:, pair:pair + 1]), smv(1), AF.Identity, scale=-1.0, bias=sm[:, 0:1],
                    extra_reads=[sm])
                act(P, (self.st_im, self.st_im[:, pair:pair + 1]), smv(3), AF.Identity, bias=sm[:, 2:3], extra_reads=[sm])
                if q == 3:
                    h32 = h32s[dc % 2]
                    for half in range(2):
                        hs = slice(half * HT, (half + 1) * HT)
                        y_, a_, b_ = yv[half], ga[half], gb[half]
                        uo = u16[(dc * 2 + half) % 3]
                        stt(P, y_, (h32, h32[:, hs]), self.vcol("ssmd%d" % j, dc), yb[half], ALU.mult, ALU.add,
                            extra_reads=[self.vecs])
                        act(P, a_, y_, AF.Square)
                        ts(P, "dve", a_, a_, 0.044715, 1.0, ALU.mult, ALU.add)
                        tt(P, "dve", a_, a_, y_, ALU.mult)
                        act(P, b_, a_, AF.Sigmoid, scale=1.5957691216057308)
                        tt(P, "dve", uo, y_, b_, ALU.mult)
                        P.dma("sp", self.uT, self.uT[dc * 128:(dc + 1) * 128, hs], uo, uo.ap())

    def phase_ssm_proj(self, l, xin, xout, t0):
        P = self.P
        j = l // 2
        wv = self.inp["wval%d" % j]
        wg = self.inp["wgate%d" % j]
        with P.scope():
            u16 = P.sb("sp_u16", [128, KC, TT], BF16)
            wvt = [P.sb("sp_wv%d" % i, [128, 4096], BF16) for i in range(3)]
            wgt = [P.sb("sp_wg%d" % i, [128, 4096], BF16) for i in range(3)]
            mk = lambda nm, n=2: [P.sb("sp_%s%d" % (nm, i), [128, HT], F32) for i in range(n)]
            sgs, tmps, zts, sqs, xcs = mk("sg"), mk("tm"), mk("z"), mk("q"), mk("x")
            uv = []
            for k0 in range(0, KC, 8):
                v = u16.view(u16[:, k0:k0 + 8, :], "u16v%d" % k0)
                P.dma("sp", v, u16[:, k0:k0 + 8, :], self.uT,
                      self.uT[k0 * 128:(k0 + 8) * 128, :].rearrange("(j p) n -> p j n", p=128))
                uv.append(v)
            stats = [(self.ps[4], self.ps[5]), (self.ps[6], self.ps[7])]
            it = 0
            for oc in range(KC):
                a, b = wvt[oc % 3], wgt[oc % 3]
                P.dma("pool", a, a.ap(), wv, wv[oc], max_dma_last_dim=8192)
                P.dma("pool", b, b.ap(), wg, wg[oc], max_dma_last_dim=8192)
                for half in range(2):
                    hs = slice(half * HT, (half + 1) * HT)
                    pv, pg = self.ps[(it % 2) * 2], self.ps[(it % 2) * 2 + 1]
                    for kc in range(KC):
                        mm(P, pv, (a, a[:, kc * 128:(kc + 1) * 128]), (uv[kc // 8], u16[:, kc, hs]), kc == 0, kc == KC - 1)
                    for kc in range(KC):
                        mm(P, pg, (b, b[:, kc * 128:(kc + 1) * 128]), (uv[kc // 8], u16[:, kc, hs]), kc == 0, kc == KC - 1)
                    sg, tm = sgs[it % 2], tmps[it % 2]
                    act(P, sg, pg, AF.Sigmoid)
                    tt(P, "dve", tm, pv, sg, ALU.mult)
                    self.z_epilogue(tm, xin, t0, half, oc, l, 2, zts, sqs, xcs, stats[half], it)
                    it += 1
            for half in range(2):
                self.ln_finalize(xout, t0, half, stats[half][0], stats[half][1], l, 0)

    def attn_layer(self, l, xin, xout):
        for t in range(self.nt):
            import os as _os
            st_ = _os.environ.get("ATT_STOP", "")
            self.phase_attn_qkv(l, xin, t * TT, t == 0)
            if st_ == "qkv":
                continue
            self.phase_attn_core(l, t * TT, t == 0)
            if st_ == "core":
                continue
            self.phase_attn_out(l, xin, xout, t * TT)

    def phase_attn_qkv(self, l, xin, t0, first):
        P = self.P
        j = l // 2
        wq, wk, wv = self.inp["wq%d" % j], self.inp["wk%d" % j], self.inp["wv%d" % j]
        NC = TT + 128
        with P.scope():
            hT = P.sb("aq_hT", [128, KC, NC], BF16)
            xcs = [P.sb("aq_xc%d" % i, [128, NC], F32) for i in range(3)]
            wr = [P.sb("aq_w%d" % i, [128, 4096], BF16) for i in range(3)]
            wvs = [P.sb("aq_wv%d" % i, [128, 4096], BF16) for i in range(4)]
            q16 = [P.sb("aq_q%d" % i, [128, HT], BF16) for i in range(3)]
            vt = [P.sb("aq_vt%d" % i, [128, 512], F32) for i in range(2)]
            vpad = [P.sb("aq_vp%d" % i, [128, 8, 2, 128], BF16) for i in range(2)]
            bv = P.sb("aq_bv", [128, 512], F32)
            P.dma("sp", bv, bv.ap(), self.inp["bvrep%d" % j], self.inp["bvrep%d" % j].ap())
            for v_ in vpad:
                memset(P, "pool", v_, 0.0)
            hv = self.build_h(hT, xin, t0 - 128, NC, l, 0, 1, xcs, zero_first=128 if first else 0)
            for vp_ in range(4):
                P.dma("pool", wvs[vp_], wvs[vp_].ap(), wv, wv[vp_], max_dma_last_dim=8192)
            it = 0
            for oc in range(KC):
                w = wr[oc % 3]
                P.dma("pool", w, w.ap(), wq, wq[oc], max_dma_last_dim=8192)
                for half in range(2):
                    pb = self.ps[it % 4]
                    c0 = 128 + half * HT
                    for kc in range(KC):
                        mm(P, pb, (w, w[:, kc * 128:(kc + 1) * 128]), (hv[kc], hT[:, kc, c0:c0 + HT]), kc == 0, kc == KC - 1)
                    qo = q16[it % 3]
                    act(P, qo, pb, AF.Identity, bias=self.vcol("bq%d" % j, oc), extra_reads=[self.vecs])
                    P.dma("sp", self.qTd, self.qTd[oc * 128:(oc + 1) * 128, half * HT:(half + 1) * HT], qo, qo.ap())
                    it += 1
            for g in range(NKV):
                w = wr[(KC + g) % 3]
                P.dma("pool", w, w.ap(), wk, wk[g], max_dma_last_dim=8192)
                for (c0, n) in ((0, 512), (512, 512), (1024, 128)):
                    pb = self.ps[it % 4]
                    for kc in range(KC):
                        mm(P, (pb, pb[:, 0:n]), (w, w[:, kc * 128:(kc + 1) * 128]), (hv[kc], hT[:, kc, c0:c0 + n]),
                           kc == 0, kc == KC - 1)
                    ko = q16[it % 3]
                    act(P, (ko, ko[:, 0:n]), (pb, pb[:, 0:n]), AF.Identity, bias=self.vcol("bk%d" % j, g),
                        extra_reads=[self.vecs])
                    P.dma("sp", self.kTd, self.kTd[g, :, c0:c0 + n], ko, ko[:, 0:n])
                    it += 1
            for blk in range(9):
                pb = self.ps[4 + blk % 4]
                for vp_ in range(4):
                    for kc in range(KC):
                        mm(P, (pb, pb[:, vp_ * 128:(vp_ + 1) * 128]), (hv[kc], hT[:, kc, blk * 128:(blk + 1) * 128]),
                           (wvs[vp_], wvs[vp_][:, kc * 128:(kc + 1) * 128]), kc == 0, kc == KC - 1)
                v32 = vt[blk % 2]
                vp16 = vpad[blk % 2]
                tt(P, "dve", v32, pb, bv, ALU.add)
                src = v32.ap().rearrange("p (g d) -> p g d", g=8)
                cp(P, "act", (vp16, vp16[:, :, 0, 0:64]), (v32, src))
                cp(P, "dve", (vp16, vp16[:, :, 1, 64:128]), (v32, src))
                P.dma("sp", self.vpd, self.vpd[:, blk * 2048:(blk + 1) * 2048], vp16,
                      vp16.ap().rearrange("p g h c -> p (g h c)"))

    def phase_attn_core(self, l, t0, first):
        P = self.P
        j = l // 2
        NC = TT + 128
        bias_d = self.inp["biastab"]
        with P.scope():
            kT2 = P.sb("ac_k", [128, 8, NC], BF16)
            vp = P.sb("ac_v", [128, 9, 8, 2, 128], BF16)
            es = P.sb("ac_es", [1, 4096], F32)
            ones1 = P.sb("ac_one", [1, 128], F32)
            onesp = P.sb("ac_onep", [128, 2, 128], BF16)
            qgs = [P.sb("ac_q%d" % i, [128, 4, TT], BF16) for i in range(2)]
            bts = [P.sb("ac_b%d" % i, [128, 2, TT], F32) for i in range(2)]
            lts = [P.sb("ac_l%d" % i, [128, 512], F32) for i in range(2)]
            Es = [P.sb("ac_E%d" % i, [128, 512], BF16) for i in range(8)]
            recs = [P.sb("ac_r%d" % i, [128, 512], F32) for i in range(2)]
            o16s = [P.sb("ac_o%d" % i, [128, 512], BF16) for i in range(2)]
            P.dma("sp", kT2, kT2.ap(), self.kTd, self.kTd.ap().rearrange("g p n -> p g n"))
            P.dma("sp", vp, vp.ap().rearrange("p b g h c -> p (b g h c)"), self.vpd, self.vpd.ap())
            P.dma("sp", es, es.ap(), self.inp["sinkrep%d" % j], self.inp["sinkrep%d" % j].ap())
            act(P, es, es, AF.Exp)
            memset(P, "dve", ones1, 1.0)
            memset(P, "dve", onesp, 0.0)
            memset(P, "dve", (onesp, onesp[:, 0, 0:64]), 1.0)
            memset(P, "dve", (onesp, onesp[:, 1, 64:128]), 1.0)
            sit = 0
            nit = 0
            esbs = [P.sb("ac_esb%d" % i, [128, 512], F32) for i in range(2)]
            dens = [P.sb("ac_den%d" % i, [128, 512], F32) for i in range(2)]
            for g in range(NKV):
                qg, bt = qgs[g % 2], bts[g % 2]
                P.dma("sp", qg, qg.ap(), self.qTd, self.qTd[g * 512:(g + 1) * 512, :].rearrange("(c p) n -> p c n", p=128))
                P.dma("sp", bt, bt.ap(), bias_d, bias_d[2 * g:2 * g + 2].rearrange("c p n -> p c n"))
                esb = esbs[g % 2]
                psb = self.ps[sit % 4]
                sit += 1
                for pr in range(4):
                    e0 = (g * 4 + pr) * 128
                    mm(P, (psb, psb[:, pr * 128:(pr + 1) * 128]), (es, es[0:1, e0:e0 + 128]), ones1, True, True)
                cp(P, "dve", esb, psb)
                for n in range(8):
                    chunks = []
                    if not (first and n == 0):
                        chunks.append((0, n))
                    chunks.append((1, n + 1))
                    E = {}
                    eset = (nit % 2) * 4
                    for (ci, kb) in chunks:
                        for hf in range(2):
                            psb = self.ps[sit % 4]
                            lt = lts[sit % 2]
                            sit += 1
                            rows = slice(64 * hf, 64 * hf + 64)
                            mm(P, (psb, psb.ap().rearrange("p (a q) -> p a q", a=4)),
                               (kT2, kT2[rows, g, kb * 128:(kb + 1) * 128]),
                               (qg, qg[rows, :, n * 128:(n + 1) * 128]), True, True)
                            btv = bt[:, ci, :].rearrange("p (a b q) -> p a b q", a=4, b=2)[:, :, hf, :]
                            stt(P, (lt, lt.ap().rearrange("p (a q) -> p a q", a=4)),
                                (psb, psb.ap().rearrange("p (a q) -> p a q", a=4)), 0.125, (bt, btv), ALU.mult, ALU.add)
                            Et = Es[eset + ci * 2 + hf]
                            act(P, Et, lt, AF.Exp)
                            E[(ci, hf)] = Et
                    num = self.ps[4 + 2 * (nit % 2)]
                    den = self.ps[5 + 2 * (nit % 2)]
                    combos = [(ci, kb, hf) for (ci, kb) in chunks for hf in range(2)]
                    for k_, (ci, kb, hf) in enumerate(combos):
                        mm(P, num, (vp, vp[:, kb, g, hf, :]), E[(ci, hf)], k_ == 0, k_ == len(combos) - 1)
                    for k_, (ci, kb, hf) in enumerate(combos):
                        mm(P, den, (onesp, onesp[:, hf, :]), E[(ci, hf)], k_ == 0, k_ == len(combos) - 1)
                    rec = recs[nit % 2]
                    dn = dens[nit % 2]
                    o16 = o16s[nit % 2]
                    tt(P, "dve", dn, den, esb, ALU.add)
                    P.op("dve", lambda e, rec=rec, dn=dn: e.reciprocal(out=rec.ap(), in_=dn.ap()), reads=[dn], writes=[rec])
                    tt(P, "dve", o16, num, rec, ALU.mult)
                    r0 = g * 512
                    P.dma("act", self.uT, self.uT[r0:r0 + 512, n * 128:(n + 1) * 128].rearrange("(c p) q -> p c q", p=128),
                          o16, o16.ap().rearrange("p (c q) -> p c q", c=4))
                    nit += 1

    def phase_attn_out(self, l, xin, xout, t0):
        P = self.P
        j = l // 2
        wo = self.inp["wo%d" % j]
        with P.scope():
            o16 = P.sb("ao_o16", [128, KC, TT], BF16)
            wr = [P.sb("ao_w%d" % i, [128, 4096], BF16) for i in range(3)]
            mk = lambda nm, n=2: [P.sb("ao_%s%d" % (nm, i), [128, HT], F32) for i in range(n)]
            zts, sqs, xcs = mk("z"), mk("q"), mk("x")
            ov = []
            for k0 in range(0, KC, 8):
                v = o16.view(o16[:, k0:k0 + 8, :], "o16v%d" % k0)
                P.dma("sp", v, o16[:, k0:k0 + 8, :], self.uT,
                      self.uT[k0 * 128:(k0 + 8) * 128, :].rearrange("(j p) n -> p j n", p=128))
                ov.append(v)
            stats = [(self.ps[4], self.ps[5]), (self.ps[6], self.ps[7])]
            it = 0
            for oc in range(KC):
                w = wr[oc % 3]
                P.dma("pool", w, w.ap(), wo, wo[oc], max_dma_last_dim=8192)
                for half in range(2):
                    hs = slice(half * HT, (half + 1) * HT)
                    py = self.ps[it % 4]
                    for kc in range(KC):
                        mm(P, py, (w, w[:, kc * 128:(kc + 1) * 128]), (ov[kc // 8], o16[:, kc, hs]), kc == 0, kc == KC - 1)
                    self.z_epilogue(py, xin, t0, half, oc, l, 2, zts, sqs, xcs, stats[half], it)
                    it += 1
            for half in range(2):
                self.ln_finalize(xout, t0, half, stats[half][0], stats[half][1], l, 0)


def build_model(layers, ntok, plan):
    M = Model(layers, ntok)
    P = M.P
    M.setup()
    M.phase_mods()
    cur = M.xT
    pingpong = [M.xA, M.xB]
    for i, (l, kind) in enumerate(plan):
        last = i == len(plan) - 1
        nxt = M.outT if last else pingpong[i % 2]
        if kind == "ffn":
            for t in range(M.nt):
                M.phase_ffn_up(l, cur, t * TT)
                M.phase_ffn_down(l, cur, nxt, t * TT)
        elif l % 2 == 0:
            M.ssm_layer(l, cur, nxt)
        else:
            M.attn_layer(l, cur, nxt)
        cur = nxt
    P.finish()
    return M


FULL_PLAN = [(l, k) for l in range(DEPTH) for k in ("mix", "ffn")]
N_CORES_USED = 2
_CACHE = {}


def kernel(**inputs):
    x = np.asarray(inputs["x"])
    bsz, seq, _ = x.shape
    layers = list(range(DEPTH))
    key = ("full", seq)
    if key not in _CACHE:
        _CACHE[key] = build_model(layers, seq, FULL_PLAN)
    M = _CACHE[key]
    in_maps = []
    for b in range(bsz):
        im = prep_inputs(inputs, b, layers, seq)
        in_maps.append({k: v for k, v in im.items() if k in M.inp})
    res = run_bass_kernel_spmd(M.P.nc, in_maps, core_ids=list(range(bsz)))
    out = np.stack([np.ascontiguousarray(res.results[b]["outT"].T) for b in range(bsz)], axis=0)
    return out.astype(np.float32)
```

```python
import math
from contextlib import ExitStack, contextmanager

import numpy as np
import ml_dtypes

import concourse.bass as bass
import concourse.mybir as mybir
from concourse.bass_utils import run_bass_kernel_spmd

F32 = mybir.dt.float32
BF16 = mybir.dt.bfloat16
I32 = mybir.dt.int32
AF = mybir.ActivationFunctionType
ALU = mybir.AluOpType
AX = mybir.AxisListType

SEM_ROLL = 30000

D = 4096
KC = 32
DEPTH = 4
DFF = 11008
NJ = 86
TT = 1024
HT = 512
NHEAD = 64
NKV = 8
ALPHA = (2 * DEPTH) ** 0.25
LN_EPS = 1e-5
EPS_P = LN_EPS / (ALPHA * ALPHA)
PI_LO = 3.1415925
TWO_PI = 2.0 * math.pi
CW1 = 6.28125
CW2 = TWO_PI - 6.28125
MAGIC = 12582912.0


class Buf:
    def __init__(self, prog, name, base_ap, space, root=None):
        self.prog = prog
        self.name = name
        self.base = base_ap
        self.space = space
        self.w = {}
        self.r = {}
        self.sem = None
        self.ndma = 0
        self.root = root if root is not None else self

    def ap(self):
        return self.base

    def __getitem__(self, idx):
        return self.base[idx]

    def view(self, ap, name=None):
        return Buf(self.prog, name or self.name + "_v", ap, self.space, root=self.root)


class Prog:
    ENG = ("pe", "act", "dve", "pool", "sp")

    def __init__(self):
        self.nc = bass.Bass("TRN2", target_bir_lowering=False)
        self.es = ExitStack()
        self.rec = {e: [] for e in self.ENG}
        self.cnt = {e: 0 for e in self.ENG}
        self.esem = {}
        self.nsem = 0
        for e in ("pe", "act", "dve", "pool"):
            self.esem[e] = self._newsem(e)
        self.waited = {e: {} for e in self.ENG}
        self.dma_owner = {}
        self.free_dma_sems = []
        self.out_bufs = []
        self.ninst = 0
        self.scopes = []

    def _newsem(self, name):
        self.nsem += 1
        s = self.es.enter_context(self.nc.semaphore("s%d_%s" % (self.nsem, name)))
        return s

    def dram(self, name, shape, dtype, kind="Internal"):
        t = self.nc.dram_tensor(name, list(shape), dtype, kind=kind)
        b = Buf(self, name, t.ap(), "dram")
        if kind == "ExternalOutput":
            self.out_bufs.append(b)
        return b

    def sb(self, name, shape, dtype):
        st = self.scopes[-1][0] if self.scopes else self.es
        self.nsb = getattr(self, "nsb", 0) + 1
        name = "%s_%d" % (name, self.nsb)
        t = st.enter_context(self.nc.sbuf_tensor(name, list(shape), dtype))
        b = Buf(self, name, t[:], "sb")
        if self.scopes:
            self.scopes[-1][1].append(b)
        return b

    def ps(self, name, shape, dtype):
        t = self.es.enter_context(self.nc.psum_tensor(name, list(shape), dtype))
        return Buf(self, name, t[:], "ps")

    @contextmanager
    def scope(self):
        st = ExitStack()
        bufs = []
        self.scopes.append((st, bufs))
        try:
            yield
        finally:
            self.barrier()
            self.scopes.pop()
            for b in bufs:
                if b.sem is not None:
                    self.free_dma_sems.append((b.sem, b.ndma))
                    del self.dma_owner[id(b.sem)]
            st.close()

    def barrier(self):
        tok = {}
        for e in ("pe", "act", "dve", "pool"):
            tok[self.esem[e]] = self.cnt[e]
        for o in list(self.dma_owner.values()):
            tok[o.sem] = 16 * o.ndma
        for e in self.ENG:
            self._emit_waits(e, dict(tok), force_pe=True)

    def _emit_waits(self, eng, tokens, force_pe=False):
        for sem, val in tokens.items():
            owner = self.dma_owner.get(id(sem))
            if owner is not None:
                val = 16 * owner.ndma
            elif eng == "pe" and sem is self.esem["pe"] and not force_pe:
                continue
            if val <= 0:
                continue
            if self.waited[eng].get(id(sem), -1) >= val:
                continue
            self.waited[eng][id(sem)] = val
            self.rec[eng].append(("wait", sem, val))

    @staticmethod
    def _merge(tok, d):
        for s, v in d.items():
            if tok.get(s, -1) < v:
                tok[s] = v

    def op(self, eng, fn, reads=(), writes=()):
        tok = {}
        for b in reads:
            self._merge(tok, b.w)
        for b in writes:
            self._merge(tok, b.w)
            self._merge(tok, b.r)
        self._emit_waits(eng, tok)
        if self.cnt[eng] >= SEM_ROLL:
            self.esem[eng] = self._newsem(eng)
            self.cnt[eng] = 0
        self.cnt[eng] += 1
        sem = self.esem[eng]
        c = self.cnt[eng]
        self.rec[eng].append(("op", fn, sem, 1))
        self.ninst += 1
        for b in reads:
            b.r[sem] = c
        for b in writes:
            b.w = {sem: c}
            b.r = {}

    def _dma_sem(self, owner):
        owner = owner.root
        if owner.sem is None:
            if self.free_dma_sems:
                owner.sem, owner.ndma = self.free_dma_sems.pop()
            else:
                owner.sem = self._newsem("d_" + owner.name)
            self.dma_owner[id(owner.sem)] = owner
        return owner

    def dma(self, q, dst, dst_ap, src, src_ap, **kw):
        tok = {}
        self._merge(tok, src.w)
        self._merge(tok, dst.w)
        self._merge(tok, dst.r)
        self._emit_waits(q, tok)
        owner = dst if dst.space == "sb" else src
        owner = self._dma_sem(owner)
        owner.ndma += 1
        sem = owner.sem
        val = 16 * owner.ndma
        self.rec[q].append(("op", (lambda e: e.dma_start(out=dst_ap, in_=src_ap, **kw)), sem, 16))
        self.ninst += 1
        src.r[sem] = val
        dst.w = {sem: val}
        dst.r = {}

    def finish(self):
        self.barrier()
        engmap = {"pe": "tensor", "act": "scalar", "dve": "vector", "pool": "gpsimd", "sp": "sync"}
        with self.nc.Block() as block:
            for e in self.ENG:
                recs = self.rec[e]

                def body(engine, recs=recs):
                    for r in recs:
                        if r[0] == "wait":
                            engine.wait_ge(r[1], r[2])
                        else:
                            r[1](engine).then_inc(r[2], r[3])

                getattr(block, engmap[e])(body)
        return self.nc


def _ap(x):
    return x[1] if isinstance(x, tuple) else x.ap()


def _bf(x):
    return x[0] if isinstance(x, tuple) else x


def tt(P, eng, out, a, b, op):
    oa, aa, ba = _ap(out), _ap(a), _ap(b)
    P.op(eng, lambda e: e.tensor_tensor(out=oa, in0=aa, in1=ba, op=op), reads=[_bf(a), _bf(b)], writes=[_bf(out)])


def ts(P, eng, out, a, s1, s2, op0, op1=None, extra_reads=()):
    oa, aa = _ap(out), _ap(a)
    if eng == "pool" and op1 is None:
        if op0 == ALU.mult:
            op1, s2 = ALU.add, 0.0
        elif op0 == ALU.add:
            op1, s2 = ALU.mult, 1.0
    if op1 is None:
        P.op(eng, lambda e: e.tensor_scalar(out=oa, in0=aa, scalar1=s1, scalar2=None, op0=op0),
             reads=[_bf(a)] + list(extra_reads), writes=[_bf(out)])
    else:
        P.op(eng, lambda e: e.tensor_scalar(out=oa, in0=aa, scalar1=s1, scalar2=s2, op0=op0, op1=op1),
             reads=[_bf(a)] + list(extra_reads), writes=[_bf(out)])


def stt(P, out, a, s, b, op0, op1, extra_reads=()):
    oa, aa, ba = _ap(out), _ap(a), _ap(b)
    P.op("dve", lambda e: e.scalar_tensor_tensor(out=oa, in0=aa, scalar=s, in1=ba, op0=op0, op1=op1),
         reads=[_bf(a), _bf(b)] + list(extra_reads), writes=[_bf(out)])


def act(P, out, a, func, scale=None, bias=None, extra_reads=()):
    oa, aa = _ap(out), _ap(a)
    kw = {}
    if scale is not None:
        kw["scale"] = scale
    if bias is not None:
        kw["bias"] = bias
    P.op("act", lambda e: e.activation(out=oa, in_=aa, func=func, **kw), reads=[_bf(a)] + list(extra_reads),
         writes=[_bf(out)])


def cp(P, eng, out, a):
    oa, aa = _ap(out), _ap(a)
    if eng == "act":
        P.op("act", lambda e: e.activation(out=oa, in_=aa, func=AF.Copy), reads=[_bf(a)], writes=[_bf(out)])
    else:
        P.op(eng, lambda e: e.tensor_copy(out=oa, in_=aa), reads=[_bf(a)], writes=[_bf(out)])


def mm(P, out, lhsT, rhs, start, stop):
    oa, la, ra = _ap(out), _ap(lhsT), _ap(rhs)
    P.op("pe", lambda e: e.matmul(oa, lhsT=la, rhs=ra, start=start, stop=stop), reads=[_bf(lhsT), _bf(rhs)],
         writes=[_bf(out)])


def memset(P, eng, out, val):
    oa = _ap(out)
    P.op(eng, lambda e: e.memset(oa, val), writes=[_bf(out)])


def _fm(v):
    v = np.asarray(v, np.float32)
    return np.ascontiguousarray(v.reshape(-1, 128).T)


def _pieces(w, ncol=128):
    K, N = w.shape
    a = w.reshape(K // 128, 128, N // ncol, ncol)
    return np.ascontiguousarray(a.transpose(2, 1, 0, 3)).reshape(N // ncol, 128, (K // 128) * ncol)


class VecLayout:
    def __init__(self):
        self.off = {}
        self.n = 0

    def add(self, name, ncols):
        self.off[name] = self.n
        self.n += ncols


def vec_layout():
    L = VecLayout()
    L.add("c", 32)
    for m in range(6):
        L.add("bada%d" % m, 32)
    for l in range(DEPTH):
        for m in range(6):
            L.add("tab%d_%d" % (l, m), 32)
        for s in range(2):
            L.add("lng%d_%d" % (l, s), 32)
            L.add("lnb%d_%d" % (l, s), 32)
    for j in range(2):
        L.add("ssmd%d" % j, 32)
        L.add("bq%d" % j, 32)
        L.add("bk%d" % j, 8)
    return L


VL = vec_layout()


def _rel_bucket_np(dist):
    max_exact = 16
    d = np.maximum(dist, 0)
    log_ratio = np.log(np.maximum(d, 1).astype(np.float32) / max_exact) / math.log(128 / max_exact)
    large = max_exact + (log_ratio * (32 - max_exact)).astype(np.int32)
    large = np.minimum(large, 31)
    return np.where(d < max_exact, d, large)


def prep_inputs(inp, b, layers, ntok):
    m = {}
    m["xT"] = np.ascontiguousarray(np.asarray(inp["x"][b, :ntok], np.float32).T)
    vec = np.zeros((128, VL.n), np.float32)

    def put(name, v):
        a = _fm(v)
        vec[:, VL.off[name]:VL.off[name] + a.shape[1]] = a

    put("c", inp["c"][b])
    for mm_ in range(6):
        put("bada%d" % mm_, inp["b_ada"][mm_ * D:(mm_ + 1) * D])
    for l in range(DEPTH):
        for mm_ in range(6):
            put("tab%d_%d" % (l, mm_), inp["ada_table"][l, mm_])
        for s in range(2):
            put("lng%d_%d" % (l, s), inp["ln_g"][l, s])
            put("lnb%d_%d" % (l, s), inp["ln_b"][l, s])
    for j in range(2):
        put("ssmd%d" % j, inp["ssm_d"][j])
        bqkv = np.asarray(inp["attn_b_qkv"][j], np.float32)
        put("bq%d" % j, bqkv[:D])
        bk = bqkv[D:D + 512].reshape(8, 64)
        put("bk%d" % j, np.concatenate([bk, bk], axis=1).reshape(-1))
    m["vecs"] = vec
    m["wada"] = np.ascontiguousarray(np.asarray(inp["w_ada"], np.float32))
    for l in layers:
        m["win%d" % l] = _pieces(np.asarray(inp["ffn_w_in"][l], np.float32))
        wd = np.asarray(inp["ffn_w_down"][l], np.float32)
        a = wd.reshape(2, 43, 128, 32, 128).transpose(3, 0, 2, 1, 4)
        m["wdn%d" % l] = np.ascontiguousarray(a).reshape(64, 128, 43 * 128)
        j = l // 2
        if l % 2 == 0:
            m["wval%d" % j] = _pieces(np.asarray(inp["ssm_w_val"][j], np.float32))
            m["wgate%d" % j] = _pieces(np.asarray(inp["ssm_w_gate"][j], np.float32))
            for nm, key in (("bpr", "ssm_b_re"), ("bpi", "ssm_b_im")):
                bsrc = np.asarray(inp[key][j], np.float32)
                out = np.zeros((128, 128, 128), np.float32)
                for pair in range(128):
                    q = pair % 4
                    for g2 in range(2):
                        g = 2 * pair + g2
                        r0 = (2 * q + g2) * 16
                        out[pair, r0:r0 + 16, g2 * 64:(g2 + 1) * 64] = bsrc[g].T
                m["%s%d" % (nm, j)] = out
            for nm, key in (("cpr", "ssm_c_re"), ("cpi", "ssm_c_im")):
                csrc = np.asarray(inp[key][j], np.float32)
                out = np.zeros((128, 128, 128), np.float32)
                for pair in range(128):
                    q = pair % 4
                    for g2 in range(2):
                        g = 2 * pair + g2
                        c0 = (2 * q + g2) * 16
                        out[pair, g2 * 64:(g2 + 1) * 64, c0:c0 + 16] = csrc[g].T
                m["%s%d" % (nm, j)] = out
            are = np.asarray(inp["ssm_a_re"][j], np.float32).reshape(128, 128).T
            aim = np.asarray(inp["ssm_a_im"][j], np.float32).reshape(128, 128).T
            ldt = np.repeat(np.asarray(inp["ssm_log_dt"][j], np.float32).reshape(128, 2, 1), 64, axis=2)
            ldt = ldt.reshape(128, 128).T
            m["ssc%d" % j] = np.ascontiguousarray(np.concatenate([are, aim, ldt], axis=1))
        else:
            wqkv = np.asarray(inp["attn_w_qkv"][j], np.float32)
            m["wq%d" % j] = _pieces(wqkv[:, :D])
            wk = wqkv[:, D:D + 512].reshape(D, 8, 64)
            wk2 = np.concatenate([wk, wk], axis=2).reshape(D, 1024)
            m["wk%d" % j] = _pieces(wk2)
            m["wv%d" % j] = _pieces(wqkv[:, D + 512:D + 1024])
            m["wo%d" % j] = _pieces(np.asarray(inp["attn_w_o"][j], np.float32))
            bv = np.asarray(inp["attn_b_qkv"][j], np.float32)[D + 512:]
            m["bvrep%d" % j] = np.ascontiguousarray(np.broadcast_to(bv[None, :], (128, 512)))
            sk = np.asarray(inp["attn_sinks"][j], np.float32)
            m["sinkrep%d" % j] = np.ascontiguousarray(np.repeat(sk, 64)[None, :])
    if any(l % 2 == 1 for l in layers):
        rb = np.asarray(inp["rel_bias"], np.float32)
        qi = np.arange(128)[:, None]
        sj = np.arange(256)[None, :]
        dist = qi + 128 - sj
        band = (dist >= 0) & (dist < 128)
        bias = rb[_rel_bucket_np(dist)]
        bias = np.where(band[:, :, None], bias, np.float32(-1e30))
        bt = bias.transpose(1, 2, 0).reshape(2, 128, 8, 8, 128)
        m["biastab"] = np.ascontiguousarray(bt.transpose(2, 0, 1, 3, 4)).reshape(16, 128, 1024)
    return m


class Model:
    def __init__(self, layers, ntok, stop=None):
        self.layers = list(layers)
        self.ntok = ntok
        self.nt = ntok // TT
        self.stop = stop
        P = self.P = Prog()
        self.inp = {}
        self.xT = self.din("xT", [D, ntok], F32)
        self.vecs_d = self.din("vecs", [128, VL.n], F32)
        self.wada = self.din("wada", [D, 6 * D], F32)
        for l in self.layers:
            self.din("win%d" % l, [172, 128, 4096], F32)
            self.din("wdn%d" % l, [64, 128, 43 * 128], F32)
            j = l // 2
            if l % 2 == 0:
                for nm in ("wval", "wgate"):
                    self.din("%s%d" % (nm, j), [32, 128, 4096], F32)
                for nm in ("bpr", "bpi", "cpr", "cpi"):
                    self.din("%s%d" % (nm, j), [128, 128, 128], F32)
                self.din("ssc%d" % j, [128, 384], F32)
            else:
                self.din("wq%d" % j, [32, 128, 4096], F32)
                self.din("wk%d" % j, [8, 128, 4096], F32)
                self.din("wv%d" % j, [4, 128, 4096], F32)
                self.din("wo%d" % j, [32, 128, 4096], F32)
                self.din("bvrep%d" % j, [128, 512], F32)
                self.din("sinkrep%d" % j, [1, 4096], F32)
        if any(l % 2 == 1 for l in self.layers):
            self.din("biastab", [16, 128, 1024], F32)
        self.outT = P.dram("outT", [D, ntok], F32, kind="ExternalOutput")
        self.xA = P.dram("xA", [D, ntok], F32)
        self.xB = P.dram("xB", [D, ntok], F32)
        self.zT = P.dram("zT", [D, TT], F32)
        self.uT = P.dram("uT", [D, TT], BF16)
        self.hidT = P.dram("hidT", [DFF, TT], BF16)
        self.modsD = P.dram("modsD", [1, 6 * D], F32)
        self.tabs = P.dram("tabs", [128, 128, 2048], F32)
        self.cpp = P.dram("cpp", [128, 128, 384], BF16)
        self.qTd = P.dram("qTd", [D, TT], BF16)
        self.kTd = P.dram("kTd", [8, 128, TT + 128], BF16)
        self.vpd = P.dram("vpd", [128, 9 * 8 * 2 * 128], BF16)
        self.vecs = P.sb("vecs_sb", [128, VL.n], F32)
        self.mods = P.sb("mods_sb", [128, DEPTH * 6 * 32], F32)
        self.ones_ln = P.sb("ones_ln", [128, 128], F32)
        self.consts = P.sb("consts", [128, 8], F32)
        self.st_re = P.sb("st_re", [128, 128], F32)
        self.st_im = P.sb("st_im", [128, 128], F32)
        self.psall = P.ps("psall", [128, 4096], F32)
        self.ps = [self.psall.view(self.psall[:, i * 512:(i + 1) * 512], "psb%d" % i) for i in range(8)]
        self.psw = [self.psall.view(self.psall[:, i * 1024:(i + 1) * 1024], "psw%d" % i) for i in range(2)]

    def din(self, name, shape, dtype):
        b = self.P.dram(name, shape, dtype, kind="ExternalInput")
        self.inp[name] = b
        return b

    def vcol(self, name, i=0, n=1):
        o = VL.off[name] + i
        return self.vecs[:, o:o + n]

    def mcol(self, l, m, i=0, n=1):
        o = (l * 6 + m) * 32 + i
        return self.mods[:, o:o + n]

    def setup(self):
        P = self.P
        P.dma("sp", self.vecs, self.vecs.ap(), self.vecs_d, self.vecs_d.ap())
        memset(P, "dve", self.ones_ln, 1.0 / D)
        memset(P, "dve", (self.consts, self.consts[:, 0:1]), math.pi / 2)
        memset(P, "dve", (self.consts, self.consts[:, 1:2]), EPS_P)
        memset(P, "dve", (self.consts, self.consts[:, 2:3]), 0.0)

    def phase_mods(self):
        P = self.P
        with P.scope():
            scs = P.sb("scs", [128, 32], BF16)
            wts = [P.sb("wadat%d" % i, [128, 4096], BF16) for i in range(4)]
            row = P.sb("modrow", [1, 4096], F32)
            msh = P.sb("msh", [128, 192], F32)
            act(P, scs, (self.vecs, self.vcol("c", 0, 32)), AF.Silu)
            n = 0
            for cg in range(6):
                for kc in range(KC):
                    wt = wts[n % 4]
                    n += 1
                    P.dma("pool", wt, wt.ap(), self.wada, self.wada[kc * 128:(kc + 1) * 128, cg * D:(cg + 1) * D],
                          max_dma_last_dim=8192)
                    for b in range(8):
                        mm(P, (self.ps[b], self.ps[b][0:1, :]), (scs, scs[:, kc:kc + 1]),
                           (wt, wt[:, b * 512:(b + 1) * 512]), kc == 0, kc == KC - 1)
                for b in range(8):
                    cp(P, "act" if b % 2 else "dve", (row, row[0:1, b * 512:(b + 1) * 512]),
                       (self.ps[b], self.ps[b][0:1, :]))
                P.dma("sp", self.modsD, self.modsD[0:1, cg * D:(cg + 1) * D], row, row.ap())
            P.dma("sp", msh, msh.ap(), self.modsD, self.modsD.ap().rearrange("o (j p) -> p (o j)", p=128),
                  allow_slow_non_contiguous=True)
            o = VL.off["bada0"]
            tt(P, "dve", msh, msh, (self.vecs, self.vecs[:, o:o + 192]), ALU.add)
            for l in range(DEPTH):
                o = VL.off["tab%d_0" % l]
                ml = (self.mods, self.mods[:, l * 192:(l + 1) * 192])
                tt(P, "dve", ml, msh, (self.vecs, self.vecs[:, o:o + 192]), ALU.add)
                for m in (1, 4):
                    mc = (self.mods, self.mcol(l, m, 0, 32))
                    ts(P, "dve", mc, mc, 1.0, None, ALU.add)
                for m in (2, 5):
                    mc = (self.mods, self.mcol(l, m, 0, 32))
                    ts(P, "dve", mc, mc, 1.0, 1.0 / ALPHA, ALU.add, ALU.mult)

    def ln_finalize(self, xout, t0, half, st_mu, st_e2, l, s):
        P = self.P
        with P.scope():
            mu = P.sb("ln_mu", [128, HT], F32)
            rstd = P.sb("ln_rstd", [128, HT], F32)
            tmp = P.sb("ln_tmp", [128, HT], F32)
            zc = [P.sb("ln_z%d" % i, [128, HT], F32) for i in range(4)]
            oc_ = [P.sb("ln_o%d" % i, [128, HT], F32) for i in range(3)]
            c0 = t0 + half * HT

            def load(oc):
                z = zc[oc % 4]
                P.dma("sp", z, z.ap(), self.zT, self.zT[oc * 128:(oc + 1) * 128, half * HT:(half + 1) * HT])

            load(0)
            load(1)
            load(2)
            cp(P, "act", mu, st_mu)
            tt(P, "dve", tmp, mu, mu, ALU.mult)
            tt(P, "dve", tmp, st_e2, tmp, ALU.subtract)
            act(P, tmp, tmp, AF.Sqrt, bias=self.consts[:, 1:2], extra_reads=[self.consts])
            P.op("dve", lambda e: e.reciprocal(out=rstd.ap(), in_=tmp.ap()), reads=[tmp], writes=[rstd])
            for oc in range(KC):
                z = zc[oc % 4]
                o = oc_[oc % 3]
                tt(P, "dve", z, z, mu, ALU.subtract)
                tt(P, "dve", z, z, rstd, ALU.mult)
                if oc + 3 < KC:
                    load(oc + 3)
                act(P, o, z, AF.Identity, scale=self.vcol("lng%d_%d" % (l, s), oc), bias=self.vcol("lnb%d_%d" % (l, s), oc),
                    extra_reads=[self.vecs])
                P.dma("act", xout, xout[oc * 128:(oc + 1) * 128, c0:c0 + HT], o, o.ap())

    def z_prefetch(self, xin, t0, half, oc, xcs, idx):
        P = self.P
        xc = xcs[idx % len(xcs)]
        c0 = t0 + half * HT
        P.dma("sp", xc, xc.ap(), xin, xin[oc * 128:(oc + 1) * 128, c0:c0 + HT])

    def z_epilogue(self, y, xin, t0, half, oc, l, gm, zts, sqs, xcs, stats, idx, pre=None):
        P = self.P
        zt = zts[idx % len(zts)]
        sq = sqs[idx % len(sqs)]
        xc = xcs[idx % len(xcs)]
        stt(P, zt, y, self.mcol(l, gm, oc), xc, ALU.mult, ALU.add, extra_reads=[self.mods])
        act(P, sq, zt, AF.Square)
        P.dma("act", self.zT, self.zT[oc * 128:(oc + 1) * 128, half * HT:(half + 1) * HT], zt, zt.ap())

        def deferred():
            mm(P, stats[0], self.ones_ln, zt, oc == 0, oc == KC - 1)
            mm(P, stats[1], self.ones_ln, sq, oc == 0, oc == KC - 1)
        return deferred

    def build_h(self, hT, xin, c_lo, ncol, l, m_shift, m_scale, xcs, zero_first=0):
        P = self.P
        views = []
        for kc in range(KC):
            xc = xcs[kc % len(xcs)]
            hv = hT.view(hT[:, kc, :], "h_kc%d" % kc)
            views.append(hv)
            if zero_first:
                memset(P, "pool", (hv, hT[:, kc, 0:zero_first]), 0.0)
            lo = c_lo + zero_first
            n = ncol - zero_first
            P.dma("sp", xc, xc[:, 0:n], xin, xin[kc * 128:(kc + 1) * 128, lo:lo + n])
            ts(P, "pool" if kc % 4 == 3 else "dve", (hv, hT[:, kc, zero_first:ncol]), (xc, xc[:, 0:n]),
               self.mcol(l, m_scale, kc), self.mcol(l, m_shift, kc), ALU.mult, ALU.add, extra_reads=[self.mods])
        return views

    def phase_ffn_up(self, l, xin, t0):
        P = self.P
        win = self.inp["win%d" % l]
        with P.scope():
            hT = P.sb("f1_hT", [128, KC, TT], BF16)
            xcs = [P.sb("f1_xc%d" % i, [128, TT], F32) for i in range(4)]
            wg = [P.sb("f1_wg%d" % i, [128, 4096], BF16) for i in range(3)]
            wu = [P.sb("f1_wu%d" % i, [128, 4096], BF16) for i in range(3)]
            sg = [P.sb("f1_sg%d" % i, [128, HT], F32) for i in range(2)]
            hid = [P.sb("f1_hid%d" % i, [128, HT], BF16) for i in range(3)]
            hv = self.build_h(hT, xin, t0, TT, l, 3, 4, xcs)
            it = 0
            for j in range(NJ):
                g_t = wg[j % 3]
                u_t = wu[j % 3]
                P.dma("pool", g_t, g_t.ap(), win, win[j], max_dma_last_dim=8192)
                P.dma("pool", u_t, u_t.ap(), win, win[NJ + j], max_dma_last_dim=8192)
                for half in range(2):
                    pg = self.ps[(it % 2) * 2]
                    pu = self.ps[(it % 2) * 2 + 1]
                    for kc in range(KC):
                        mm(P, pg, (g_t, g_t[:, kc * 128:(kc + 1) * 128]), (hv[kc], hT[:, kc, half * HT:(half + 1) * HT]),
                           kc == 0, kc == KC - 1)
                    for kc in range(KC):
                        mm(P, pu, (u_t, u_t[:, kc * 128:(kc + 1) * 128]), (hv[kc], hT[:, kc, half * HT:(half + 1) * HT]),
                           kc == 0, kc == KC - 1)
                    s_ = sg[it % 2]
                    h_ = hid[it % 3]
                    act(P, s_, pg, AF.Silu)
                    tt(P, "dve", h_, pu, s_, ALU.mult)
                    P.dma("act", self.hidT, self.hidT[j * 128:(j + 1) * 128, half * HT:(half + 1) * HT], h_, h_.ap())
                    it += 1

    def phase_ffn_down(self, l, xin, xout, t0):
        P = self.P
        wdn = self.inp["wdn%d" % l]
        for half in range(2):
            with P.scope():
                hid = P.sb("f2_hid", [128, NJ, HT], BF16)
                wp = [P.sb("f2_w%d" % i, [128, 43 * 128], BF16) for i in range(3)]
                zts = [P.sb("f2_z%d" % i, [128, HT], F32) for i in range(2)]
                sqs = [P.sb("f2_q%d" % i, [128, HT], F32) for i in range(2)]
                xcs = [P.sb("f2_x%d" % i, [128, HT], F32) for i in range(2)]
                hvs = []
                for j0 in range(0, NJ, 8):
                    j1 = min(NJ, j0 + 8)
                    v = hid.view(hid[:, j0:j1, :], "hidv%d" % j0)
                    P.dma("sp", v, hid[:, j0:j1, :], self.hidT,
                          self.hidT[j0 * 128:j1 * 128, half * HT:(half + 1) * HT].rearrange("(j p) n -> p j n", p=128))
                    hvs.append(v)
                stats = (self.ps[6], self.ps[7])
                pend = None
                for oc in range(KC):
                    py = self.ps[oc % 2]
                    self.z_prefetch(xin, t0, half, oc, xcs, oc)
                    for kh in range(2):
                        w = wp[(oc * 2 + kh) % 3]
                        P.dma("pool", w, w.ap(), wdn, wdn[oc * 2 + kh], max_dma_last_dim=8192)
                        for k in range(43):
                            j = kh * 43 + k
                            mm(P, py, (w, w[:, k * 128:(k + 1) * 128]), (hvs[j // 8], hid[:, j, :]),
                               j == 0, j == NJ - 1)
                    if pend is not None:
                        pend()
                    pend = self.z_epilogue(py, xin, t0, half, oc, l, 5, zts, sqs, xcs, stats, oc)
                pend()
                self.ln_finalize(xout, t0, half, stats[0], stats[1], l, 1)

    def _range_reduce(self, eng_ang, out, ang, kk):
        P = self.P
        ts(P, "dve", kk, ang, 1.0 / TWO_PI, MAGIC, ALU.mult, ALU.add)
        ts(P, "dve", kk, kk, -MAGIC, None, ALU.add)
        stt(P, out, kk, -CW1, ang, ALU.mult, ALU.add)
        stt(P, out, kk, -CW2, out, ALU.mult, ALU.add)
        ts(P, "pool", out, out, PI_LO, -PI_LO, ALU.min, ALU.max)

    def _sincos(self, sn, cs, r, tmp):
        P = self.P
        act(P, sn, r, AF.Sin)
        act(P, tmp, r, AF.Abs)
        act(P, cs, tmp, AF.Sin, scale=-1.0, bias=self.consts[:, 0:1], extra_reads=[self.consts])

    def ssm_layer(self, l, xin, xout):
        P = self.P
        j = l // 2
        with P.scope():
            rmag = P.sb("s_rmag", [128, 128], F32)
            with P.scope():
                ssc = P.sb("s_ssc", [128, 384], F32)
                P.dma("sp", ssc, ssc.ap(), self.inp["ssc%d" % j], self.inp["ssc%d" % j].ap())
                a_re = (ssc, ssc[:, 0:128])
                a_im = (ssc, ssc[:, 128:256])
                nm = ["dt", "th", "kk", "thr", "sn1", "cs1", "tmp", "abre", "abim", "den", "nr", "kre", "kim", "nkim",
                      "nkre", "t2"]
                T = {n: P.sb("s_" + n, [128, 128], F32) for n in nm}
                act(P, T["dt"], (ssc, ssc[:, 256:384]), AF.Exp)
                tt(P, "dve", T["tmp"], a_re, T["dt"], ALU.mult)
                act(P, rmag, T["tmp"], AF.Exp)
                tt(P, "dve", T["th"], a_im, T["dt"], ALU.mult)
                self._range_reduce("dve", T["thr"], T["th"], T["kk"])
                self._sincos(T["sn1"], T["cs1"], T["thr"], T["tmp"])
                tt(P, "dve", T["abre"], rmag, T["cs1"], ALU.mult)
                tt(P, "dve", T["abim"], rmag, T["sn1"], ALU.mult)
                tt(P, "dve", T["den"], a_re, a_re, ALU.mult)
                tt(P, "dve", T["tmp"], a_im, a_im, ALU.mult)
                tt(P, "dve", T["den"], T["den"], T["tmp"], ALU.add)
                P.op("dve", lambda e: e.reciprocal(out=T["den"].ap(), in_=T["den"].ap()), reads=[T["den"]],
                     writes=[T["den"]])
                ts(P, "dve", T["nr"], T["abre"], -1.0, None, ALU.add)
                tt(P, "dve", T["kre"], T["nr"], a_re, ALU.mult)
                tt(P, "dve", T["tmp"], T["abim"], a_im, ALU.mult)
                tt(P, "dve", T["kre"], T["kre"], T["tmp"], ALU.add)
                tt(P, "dve", T["kre"], T["kre"], T["den"], ALU.mult)
                tt(P, "dve", T["kim"], T["abim"], a_re, ALU.mult)
                tt(P, "dve", T["tmp"], T["nr"], a_im, ALU.mult)
                tt(P, "dve", T["kim"], T["kim"], T["tmp"], ALU.subtract)
                tt(P, "dve", T["kim"], T["kim"], T["den"], ALU.mult)
                ts(P, "dve", T["nkim"], T["kim"], -1.0, None, ALU.mult)
                ts(P, "dve", T["nkre"], T["kre"], -1.0, None, ALU.mult)
                cr = [P.sb("s_cr%d" % i, [128, 128], F32) for i in range(2)]
                ci = [P.sb("s_ci%d" % i, [128, 128], F32) for i in range(2)]
                c1 = [P.sb("s_c1%d" % i, [128, 128], F32) for i in range(2)]
                cb = [P.sb("s_cb%d" % i, [128, 384], BF16) for i in range(2)]
                cpr = self.inp["cpr%d" % j]
                cpi = self.inp["cpi%d" % j]
                for pair in range(128):
                    a, b, c_, o = cr[pair % 2], ci[pair % 2], c1[pair % 2], cb[pair % 2]
                    P.dma("sp", a, a.ap(), cpr, cpr[pair])
                    P.dma("sp", b, b.ap(), cpi, cpi[pair])
                    pc = slice(pair, pair + 1)
                    ts(P, "pool", c_, a, T["kre"][:, pc], None, ALU.mult, extra_reads=[T["kre"]])
                    stt(P, (o, o[:, 0:128]), b, T["nkim"][:, pc], c_, ALU.mult, ALU.add, extra_reads=[T["nkim"]])
                    ts(P, "pool", c_, a, T["nkim"][:, pc], None, ALU.mult, extra_reads=[T["nkim"]])
                    stt(P, (o, o[:, 128:256]), b, T["nkre"][:, pc], c_, ALU.mult, ALU.add, extra_reads=[T["nkre"]])
                    ts(P, "dve", (o, o[:, 256:384]), (o, o[:, 0:128]), -1.0, None, ALU.mult)
                    P.dma("sp", self.cpp, self.cpp[pair], o, o.ap())
                io_i = P.sb("s_ioi", [128, TT], I32)
                io_f = P.sb("s_iof", [128, TT], F32)
                P.op("pool", lambda e: e.iota(io_i.ap(), pattern=[[1, TT]], base=1, channel_multiplier=0), writes=[io_i])
                cp(P, "dve", io_f, io_i)
                ang = [P.sb("s_ang%d" % i, [128, TT], F32) for i in range(2)]
                kk = [P.sb("s_kk%d" % i, [128, TT], F32) for i in range(2)]
                rr = [P.sb("s_rr%d" % i, [128, TT], F32) for i in range(2)]
                tb = [P.sb("s_tb%d" % i, [128, 2, TT], F32) for i in range(2)]
                for pair in range(128):
                    a, k_, r_, t_ = ang[pair % 2], kk[pair % 2], rr[pair % 2], tb[pair % 2]
                    ts(P, "pool", a, io_f, T["thr"][:, pair:pair + 1], None, ALU.mult, extra_reads=[T["thr"]])
                    self._range_reduce("dve", r_, a, k_)
                    act(P, (t_, t_[:, 1, :]), r_, AF.Sin)
                    act(P, a, r_, AF.Abs)
                    act(P, (t_, t_[:, 0, :]), a, AF.Sin, scale=-1.0, bias=self.consts[:, 0:1], extra_reads=[self.consts])
                    P.dma("sp", self.tabs, self.tabs[pair], t_, t_.ap().rearrange("p a n -> p (a n)"))
            memset(P, "dve", self.st_re, 0.0)
            memset(P, "dve", self.st_im, 0.0)
            for t in range(self.nt):
                self.phase_ssm_tile(l, xin, t * TT, rmag)
                self.phase_ssm_proj(l, xin, xout, t * TT)

    def phase_ssm_tile(self, l, xin, t0, rmag):
        P = self.P
        j = l // 2
        bpr = self.inp["bpr%d" % j]
        bpi = self.inp["bpi%d" % j]
        with P.scope():
            xcs = [P.sb("st_xc%d" % i, [128, TT], F32) for i in range(2)]
            h32s = [P.sb("st_h32%d" % i, [128, TT], F32) for i in range(2)]
            h16s = [P.sb("st_h16%d" % i, [128, TT], BF16) for i in range(2)]
            bps = [P.sb("st_bp%d" % i, [128, 8, 128], BF16) for i in range(2)]
            cpts = [P.sb("st_cp%d" % i, [128, 4, 384], BF16) for i in range(2)]
            tabs = [P.sb("st_tab%d" % i, [128, 2, TT], F32) for i in range(2)]
            rmats = [P.sb("st_rm%d" % i, [128, TT], F32) for i in range(2)]
            ones = P.sb("st_ones", [128, TT], F32)
            memset(P, "dve", ones, 1.0)
            mk = lambda nm, dt=F32, n=2, w=TT: [P.sb("st_%s%d" % (nm, i), [128, w], dt) for i in range(n)]
            t1, t2, t3, t4 = mk("t1", n=1)[0], mk("t2", n=1)[0], mk("t3", n=1)[0], mk("t4", n=1)[0]
            dsets = [[P.sb("st_dd%d_%d" % (a, b), [128, TT], BF16) for b in range(4)] for a in range(2)]
            utr, uti = mk("utr"), mk("uti")
            srs, sis = mk("sr"), mk("si")
            yv, ga, gb = mk("yv", w=HT), mk("ga", w=HT), mk("gb", w=HT)
            u16 = mk("u16", BF16, 3, HT)
            sm = P.sb("st_sm", [128, 8], F32)
            ure, uim = self.psw[0], self.psw[1]

            def prep2(dc):
                xc, h32, h16 = xcs[dc % 2], h32s[dc % 2], h16s[dc % 2]
                ts(P, "dve", h32, xc, self.mcol(l, 1, dc), self.mcol(l, 0, dc), ALU.mult, ALU.add, extra_reads=[self.mods])
                cp(P, "act", h16, h32)

            def prep(dc):
                xc = xcs[dc % 2]
                bp, cpt = bps[dc % 2], cpts[dc % 2]
                P.dma("sp", xc, xc.ap(), xin, xin[dc * 128:(dc + 1) * 128, t0:t0 + TT])
                P.dma("pool", bp, bp[:, 0:4, :], bpr, bpr[dc * 4:(dc + 1) * 4].rearrange("q p c -> p q c"))
                P.dma("pool", bp, bp[:, 4:8, :], bpi, bpi[dc * 4:(dc + 1) * 4].rearrange("q p c -> p q c"))
                P.dma("sp", cpt, cpt.ap(), self.cpp, self.cpp[dc * 4:(dc + 1) * 4].rearrange("q p c -> p q c"))

            def load_tab(pair):
                tb = tabs[pair % 2]
                P.dma("sp", tb, tb.ap().rearrange("p a n -> p (a n)"), self.tabs, self.tabs[pair])
                act(P, rmats[pair % 2], ones, AF.Copy, scale=rmag[:, pair:pair + 1], extra_reads=[rmag])

            def bmm(pair):
                dc, q = pair // 4, pair % 4
                bp, h16 = bps[dc % 2], h16s[dc % 2]
                for half in range(2):
                    hs = slice(half * HT, (half + 1) * HT)
                    mm(P, (ure, ure[:, hs]), (bp, bp[:, q, :]), (h16, h16[:, hs]), True, True)
                    mm(P, (uim, uim[:, hs]), (bp, bp[:, 4 + q, :]), (h16, h16[:, hs]), True, True)

            prep(0)
            prep2(0)
            load_tab(0)
            bmm(0)
            for pair in range(128):
                dc, q = pair // 4, pair % 4
                i2 = pair % 2
                tb, rm = tabs[i2], rmats[i2]
                cpt = cpts[dc % 2]
                yb = (self.ps[4 + 2 * (dc % 2)], self.ps[5 + 2 * (dc % 2)])
                c_ = (tb, tb[:, 0, :])
                s_ = (tb, tb[:, 1, :])
                if q == 0 and dc + 1 < KC:
                    prep(dc + 1)
                tt(P, "dve", t1, ure, c_, ALU.mult)
                tt(P, "dve", t2, uim, s_, ALU.mult)
                tt(P, "dve", t3, uim, c_, ALU.mult)
                tt(P, "dve", t4, ure, s_, ALU.mult)
                if pair + 1 < 128:
                    load_tab(pair + 1)
                    bmm(pair + 1)
                if q == 1 and dc + 1 < KC:
                    prep2(dc + 1)
                tt(P, "dve", utr[i2], t1, t2, ALU.add)
                tt(P, "dve", uti[i2], t3, t4, ALU.subtract)
                sr, si = srs[i2], sis[i2]
                ini_r, ini_i = self.st_re[:, pair:pair + 1], self.st_im[:, pair:pair + 1]
                P.op("dve", lambda e, sr=sr, rm=rm, u=utr[i2], ini=ini_r: e.tensor_tensor_scan(
                    out=sr.ap(), data0=rm.ap(), data1=u.ap(), initial=ini, op0=ALU.mult, op1=ALU.add),
                    reads=[rm, utr[i2], self.st_re], writes=[sr])
                P.op("dve", lambda e, si=si, rm=rm, u=uti[i2], ini=ini_i: e.tensor_tensor_scan(
                    out=si.ap(), data0=rm.ap(), data1=u.ap(), initial=ini, op0=ALU.mult, op1=ALU.add),
                    reads=[rm, uti[i2], self.st_im], writes=[si])
                dd = dsets[i2]
                tt(P, "pool", dd[0], sr, c_, ALU.mult)
                tt(P, "pool", dd[1], si, s_, ALU.mult)
                tt(P, "pool", dd[2], sr, s_, ALU.mult)
                tt(P, "dve" if pair % 2 else "pool", dd[3], si, c_, ALU.mult)
                for half in range(2):
                    hs = slice(half * HT, (half + 1) * HT)
                    mm(P, yb[half], (cpt, cpt[:, q, 0:128]), (dd[0], dd[0][:, hs]), q == 0, False)
                    mm(P, yb[half], (cpt, cpt[:, q, 256:384]), (dd[1], dd[1][:, hs]), False, False)
                    mm(P, yb[half], (cpt, cpt[:, q, 128:256]), (dd[2], dd[2][:, hs]), False, False)
                    mm(P, yb[half], (cpt, cpt[:, q, 128:256]), (dd[3], dd[3][:, hs]), False, q == 3)
                L = slice(TT - 1, TT)
                smv = lambda k: (sm, sm[:, k:k + 1])
                act(P, smv(0), (sr, sr[:, L]), AF.Copy, scale=tb[:, 0, L], extra_reads=[tb])
                act(P, smv(1), (si, si[:, L]), AF.Copy, scale=tb[:, 1, L], extra_reads=[tb])
                act(P, smv(2), (sr, sr[:, L]), AF.Copy, scale=tb[:, 1, L], extra_reads=[tb])
                act(P, smv(3), (si, si[:, L]), AF.Copy, scale=tb[:, 0, L], extra_reads=[tb])
                act(P, (self.st_re, self.st_re[:, pair:pair + 1]), smv(1), AF.Identity, scale=-1.0, bias=sm[:, 0:1],
                    extra_reads=[sm])
                act(P, (self.st_im, self.st_im[:, pair:pair + 1]), smv(3), AF.Identity, bias=sm[:, 2:3], extra_reads=[sm])
                if q == 3:
                    h32 = h32s[dc % 2]
                    for half in range(2):
                        hs = slice(half * HT, (half + 1) * HT)
                        y_, a_, b_ = yv[half], ga[half], gb[half]
                        uo = u16[(dc * 2 + half) % 3]
                        stt(P, y_, (h32, h32[:, hs]), self.vcol("ssmd%d" % j, dc), yb[half], ALU.mult, ALU.add,
                            extra_reads=[self.vecs])
                        act(P, a_, y_, AF.Square)
                        ts(P, "dve", a_, a_, 0.044715, 1.0, ALU.mult, ALU.add)
                        tt(P, "dve", a_, a_, y_, ALU.mult)
                        act(P, b_, a_, AF.Sigmoid, scale=1.5957691216057308)
                        tt(P, "dve", uo, y_, b_, ALU.mult)
                        P.dma("sp", self.uT, self.uT[dc * 128:(dc + 1) * 128, hs], uo, uo.ap())

    def phase_ssm_proj(self, l, xin, xout, t0):
        P = self.P
        j = l // 2
        wv = self.inp["wval%d" % j]
        wg = self.inp["wgate%d" % j]
        with P.scope():
            u16 = P.sb("sp_u16", [128, KC, TT], BF16)
            wvt = [P.sb("sp_wv%d" % i, [128, 4096], BF16) for i in range(3)]
            wgt = [P.sb("sp_wg%d" % i, [128, 4096], BF16) for i in range(3)]
            mk = lambda nm, n=2: [P.sb("sp_%s%d" % (nm, i), [128, HT], F32) for i in range(n)]
            sgs, tmps, zts, sqs, xcs = mk("sg"), mk("tm"), mk("z"), mk("q"), mk("x")
            uv = []
            for k0 in range(0, KC, 8):
                v = u16.view(u16[:, k0:k0 + 8, :], "u16v%d" % k0)
                P.dma("sp", v, u16[:, k0:k0 + 8, :], self.uT,
                      self.uT[k0 * 128:(k0 + 8) * 128, :].rearrange("(j p) n -> p j n", p=128))
                uv.append(v)
            stats = [(self.ps[4], self.ps[5]), (self.ps[6], self.ps[7])]
            it = 0
            pend = []
            for oc in range(KC):
                a, b = wvt[oc % 3], wgt[oc % 3]
                P.dma("pool", a, a.ap(), wv, wv[oc], max_dma_last_dim=8192)
                P.dma("pool", b, b.ap(), wg, wg[oc], max_dma_last_dim=8192)
                for half in range(2):
                    hs = slice(half * HT, (half + 1) * HT)
                    pv, pg = self.ps[(it % 2) * 2], self.ps[(it % 2) * 2 + 1]
                    self.z_prefetch(xin, t0, half, oc, xcs, it)
                    for kc in range(KC):
                        mm(P, pv, (a, a[:, kc * 128:(kc + 1) * 128]), (uv[kc // 8], u16[:, kc, hs]), kc == 0, kc == KC - 1)
                    for kc in range(KC):
                        mm(P, pg, (b, b[:, kc * 128:(kc + 1) * 128]), (uv[kc // 8], u16[:, kc, hs]), kc == 0, kc == KC - 1)
                    sg, tm = sgs[it % 2], tmps[it % 2]
                    act(P, sg, pg, AF.Sigmoid)
                    tt(P, "dve", tm, pv, sg, ALU.mult)
                    for p_ in pend:
                        p_()
                    pend = [self.z_epilogue(tm, xin, t0, half, oc, l, 2, zts, sqs, xcs, stats[half], it)]
                    it += 1
            for p_ in pend:
                p_()
            for half in range(2):
                self.ln_finalize(xout, t0, half, stats[half][0], stats[half][1], l, 0)

    def attn_layer(self, l, xin, xout):
        for t in range(self.nt):
            import os as _os
            st_ = _os.environ.get("ATT_STOP", "")
            self.phase_attn_qkv(l, xin, t * TT, t == 0)
            if st_ == "qkv":
                continue
            self.phase_attn_core(l, t * TT, t == 0)
            if st_ == "core":
                continue
            self.phase_attn_out(l, xin, xout, t * TT)

    def phase_attn_qkv(self, l, xin, t0, first):
        P = self.P
        j = l // 2
        wq, wk, wv = self.inp["wq%d" % j], self.inp["wk%d" % j], self.inp["wv%d" % j]
        NC = TT + 128
        with P.scope():
            hT = P.sb("aq_hT", [128, KC, NC], BF16)
            xcs = [P.sb("aq_xc%d" % i, [128, NC], F32) for i in range(3)]
            wr = [P.sb("aq_w%d" % i, [128, 4096], BF16) for i in range(3)]
            wvs = [P.sb("aq_wv%d" % i, [128, 4096], BF16) for i in range(4)]
            q16 = [P.sb("aq_q%d" % i, [128, HT], BF16) for i in range(3)]
            vt = [P.sb("aq_vt%d" % i, [128, 512], F32) for i in range(2)]
            vpad = [P.sb("aq_vp%d" % i, [128, 8, 2, 128], BF16) for i in range(2)]
            bv = P.sb("aq_bv", [128, 512], F32)
            P.dma("sp", bv, bv.ap(), self.inp["bvrep%d" % j], self.inp["bvrep%d" % j].ap())
            for v_ in vpad:
                memset(P, "pool", v_, 0.0)
            hv = self.build_h(hT, xin, t0 - 128, NC, l, 0, 1, xcs, zero_first=128 if first else 0)
            for vp_ in range(4):
                P.dma("pool", wvs[vp_], wvs[vp_].ap(), wv, wv[vp_], max_dma_last_dim=8192)
            it = 0
            for oc in range(KC):
                w = wr[oc % 3]
                P.dma("pool", w, w.ap(), wq, wq[oc], max_dma_last_dim=8192)
                for half in range(2):
                    pb = self.ps[it % 4]
                    c0 = 128 + half * HT
                    for kc in range(KC):
                        mm(P, pb, (w, w[:, kc * 128:(kc + 1) * 128]), (hv[kc], hT[:, kc, c0:c0 + HT]), kc == 0, kc == KC - 1)
                    qo = q16[it % 3]
                    act(P, qo, pb, AF.Identity, bias=self.vcol("bq%d" % j, oc), extra_reads=[self.vecs])
                    P.dma("sp", self.qTd, self.qTd[oc * 128:(oc + 1) * 128, half * HT:(half + 1) * HT], qo, qo.ap())
                    it += 1
            for g in range(NKV):
                w = wr[(KC + g) % 3]
                P.dma("pool", w, w.ap(), wk, wk[g], max_dma_last_dim=8192)
                for (c0, n) in ((0, 512), (512, 512), (1024, 128)):
                    pb = self.ps[it % 4]
                    for kc in range(KC):
                        mm(P, (pb, pb[:, 0:n]), (w, w[:, kc * 128:(kc + 1) * 128]), (hv[kc], hT[:, kc, c0:c0 + n]),
                           kc == 0, kc == KC - 1)
                    ko = q16[it % 3]
                    act(P, (ko, ko[:, 0:n]), (pb, pb[:, 0:n]), AF.Identity, bias=self.vcol("bk%d" % j, g),
                        extra_reads=[self.vecs])
                    P.dma("sp", self.kTd, self.kTd[g, :, c0:c0 + n], ko, ko[:, 0:n])
                    it += 1
            for blk in range(9):
                pb = self.ps[4 + blk % 4]
                for vp_ in range(4):
                    for kc in range(KC):
                        mm(P, (pb, pb[:, vp_ * 128:(vp_ + 1) * 128]), (hv[kc], hT[:, kc, blk * 128:(blk + 1) * 128]),
                           (wvs[vp_], wvs[vp_][:, kc * 128:(kc + 1) * 128]), kc == 0, kc == KC - 1)
                v32 = vt[blk % 2]
                vp16 = vpad[blk % 2]
                tt(P, "dve", v32, pb, bv, ALU.add)
                src = v32.ap().rearrange("p (g d) -> p g d", g=8)
                cp(P, "act", (vp16, vp16[:, :, 0, 0:64]), (v32, src))
                cp(P, "dve", (vp16, vp16[:, :, 1, 64:128]), (v32, src))
                P.dma("sp", self.vpd, self.vpd[:, blk * 2048:(blk + 1) * 2048], vp16,
                      vp16.ap().rearrange("p g h c -> p (g h c)"))

    def phase_attn_core(self, l, t0, first):
        P = self.P
        j = l // 2
        NC = TT + 128
        bias_d = self.inp["biastab"]
        with P.scope():
            kT2 = P.sb("ac_k", [128, 8, NC], BF16)
            vp = P.sb("ac_v", [128, 9, 8, 2, 128], BF16)
            es = P.sb("ac_es", [1, 4096], F32)
            ones1 = P.sb("ac_one", [1, 128], F32)
            onesp = P.sb("ac_onep", [128, 2, 128], BF16)
            qgs = [P.sb("ac_q%d" % i, [128, 4, TT], BF16) for i in range(2)]
            bts = [P.sb("ac_b%d" % i, [128, 2, TT], F32) for i in range(2)]
            lts = [P.sb("ac_l%d" % i, [128, 512], F32) for i in range(2)]
            Es = [P.sb("ac_E%d" % i, [128, 512], BF16) for i in range(8)]
            recs = [P.sb("ac_r%d" % i, [128, 512], F32) for i in range(2)]
            o16s = [P.sb("ac_o%d" % i, [128, 512], BF16) for i in range(2)]
            P.dma("sp", kT2, kT2.ap(), self.kTd, self.kTd.ap().rearrange("g p n -> p g n"))
            P.dma("sp", vp, vp.ap().rearrange("p b g h c -> p (b g h c)"), self.vpd, self.vpd.ap())
            P.dma("sp", es, es.ap(), self.inp["sinkrep%d" % j], self.inp["sinkrep%d" % j].ap())
            act(P, es, es, AF.Exp)
            memset(P, "dve", ones1, 1.0)
            memset(P, "dve", onesp, 0.0)
            memset(P, "dve", (onesp, onesp[:, 0, 0:64]), 1.0)
            memset(P, "dve", (onesp, onesp[:, 1, 64:128]), 1.0)
            sit = 0
            nit = 0
            esbs = [P.sb("ac_esb%d" % i, [128, 512], F32) for i in range(2)]
            dens = [P.sb("ac_den%d" % i, [128, 512], F32) for i in range(2)]
            for g in range(NKV):
                qg, bt = qgs[g % 2], bts[g % 2]
                P.dma("sp", qg, qg.ap(), self.qTd, self.qTd[g * 512:(g + 1) * 512, :].rearrange("(c p) n -> p c n", p=128))
                P.dma("sp", bt, bt.ap(), bias_d, bias_d[2 * g:2 * g + 2].rearrange("c p n -> p c n"))
                esb = esbs[g % 2]
                psb = self.ps[sit % 4]
                sit += 1
                for pr in range(4):
                    e0 = (g * 4 + pr) * 128
                    mm(P, (psb, psb[:, pr * 128:(pr + 1) * 128]), (es, es[0:1, e0:e0 + 128]), ones1, True, True)
                cp(P, "dve", esb, psb)
                for n in range(8):
                    chunks = []
                    if not (first and n == 0):
                        chunks.append((0, n))
                    chunks.append((1, n + 1))
                    E = {}
                    eset = (nit % 2) * 4
                    for (ci, kb) in chunks:
                        for hf in range(2):
                            psb = self.ps[sit % 4]
                            lt = lts[sit % 2]
                            sit += 1
                            rows = slice(64 * hf, 64 * hf + 64)
                            mm(P, (psb, psb.ap().rearrange("p (a q) -> p a q", a=4)),
                               (kT2, kT2[rows, g, kb * 128:(kb + 1) * 128]),
                               (qg, qg[rows, :, n * 128:(n + 1) * 128]), True, True)
                            btv = bt[:, ci, :].rearrange("p (a b q) -> p a b q", a=4, b=2)[:, :, hf, :]
                            stt(P, (lt, lt.ap().rearrange("p (a q) -> p a q", a=4)),
                                (psb, psb.ap().rearrange("p (a q) -> p a q", a=4)), 0.125, (bt, btv), ALU.mult, ALU.add)
                            Et = Es[eset + ci * 2 + hf]
                            act(P, Et, lt, AF.Exp)
                            E[(ci, hf)] = Et
                    num = self.ps[4 + 2 * (nit % 2)]
                    den = self.ps[5 + 2 * (nit % 2)]
                    combos = [(ci, kb, hf) for (ci, kb) in chunks for hf in range(2)]
                    for k_, (ci, kb, hf) in enumerate(combos):
                        mm(P, num, (vp, vp[:, kb, g, hf, :]), E[(ci, hf)], k_ == 0, k_ == len(combos) - 1)
                    for k_, (ci, kb, hf) in enumerate(combos):
                        mm(P, den, (onesp, onesp[:, hf, :]), E[(ci, hf)], k_ == 0, k_ == len(combos) - 1)
                    rec = recs[nit % 2]
                    dn = dens[nit % 2]
                    o16 = o16s[nit % 2]
                    tt(P, "dve", dn, den, esb, ALU.add)
                    P.op("dve", lambda e, rec=rec, dn=dn: e.reciprocal(out=rec.ap(), in_=dn.ap()), reads=[dn], writes=[rec])
                    tt(P, "dve", o16, num, rec, ALU.mult)
                    r0 = g * 512
                    P.dma("act", self.uT, self.uT[r0:r0 + 512, n * 128:(n + 1) * 128].rearrange("(c p) q -> p c q", p=128),
                          o16, o16.ap().rearrange("p (c q) -> p c q", c=4))
                    nit += 1

    def phase_attn_out(self, l, xin, xout, t0):
        P = self.P
        j = l // 2
        wo = self.inp["wo%d" % j]
        with P.scope():
            o16 = P.sb("ao_o16", [128, KC, TT], BF16)
            wr = [P.sb("ao_w%d" % i, [128, 4096], BF16) for i in range(3)]
            mk = lambda nm, n=2: [P.sb("ao_%s%d" % (nm, i), [128, HT], F32) for i in range(n)]
            zts, sqs, xcs = mk("z"), mk("q"), mk("x")
            ov = []
            for k0 in range(0, KC, 8):
                v = o16.view(o16[:, k0:k0 + 8, :], "o16v%d" % k0)
                P.dma("sp", v, o16[:, k0:k0 + 8, :], self.uT,
                      self.uT[k0 * 128:(k0 + 8) * 128, :].rearrange("(j p) n -> p j n", p=128))
                ov.append(v)
            stats = [(self.ps[4], self.ps[5]), (self.ps[6], self.ps[7])]
            it = 0
            pend = []
            for oc in range(KC):
                w = wr[oc % 3]
                P.dma("pool", w, w.ap(), wo, wo[oc], max_dma_last_dim=8192)
                for half in range(2):
                    hs = slice(half * HT, (half + 1) * HT)
                    py = self.ps[it % 4]
                    self.z_prefetch(xin, t0, half, oc, xcs, it)
                    for kc in range(KC):
                        mm(P, py, (w, w[:, kc * 128:(kc + 1) * 128]), (ov[kc // 8], o16[:, kc, hs]), kc == 0, kc == KC - 1)
                    for p_ in pend:
                        p_()
                    pend = [self.z_epilogue(py, xin, t0, half, oc, l, 2, zts, sqs, xcs, stats[half], it)]
                    it += 1
            for p_ in pend:
                p_()
            for half in range(2):
                self.ln_finalize(xout, t0, half, stats[half][0], stats[half][1], l, 0)


def build_model(layers, ntok, plan):
    M = Model(layers, ntok)
    P = M.P
    M.setup()
    M.phase_mods()
    cur = M.xT
    pingpong = [M.xA, M.xB]
    for i, (l, kind) in enumerate(plan):
        last = i == len(plan) - 1
        nxt = M.outT if last else pingpong[i % 2]
        if kind == "ffn":
            for t in range(M.nt):
                M.phase_ffn_up(l, cur, t * TT)
                M.phase_ffn_down(l, cur, nxt, t * TT)
        elif l % 2 == 0:
            M.ssm_layer(l, cur, nxt)
        else:
            M.attn_layer(l, cur, nxt)
        cur = nxt
    P.finish()
    return M


FULL_PLAN = [(l, k) for l in range(DEPTH) for k in ("mix", "ffn")]
N_CORES_USED = 2
_CACHE = {}


def kernel(**inputs):
    x = np.asarray(inputs["x"])
    bsz, seq, _ = x.shape
    layers = list(range(DEPTH))
    key = ("full", seq)
    if key not in _CACHE:
        _CACHE[key] = build_model(layers, seq, FULL_PLAN)
    M = _CACHE[key]
    in_maps = []
    for b in range(bsz):
        im = prep_inputs(inputs, b, layers, seq)
        in_maps.append({k: v for k, v in im.items() if k in M.inp})
    res = run_bass_kernel_spmd(M.P.nc, in_maps, core_ids=list(range(bsz)))
    out = np.stack([np.ascontiguousarray(res.results[b]["outT"].T) for b in range(bsz)], axis=0)
    return out.astype(np.float32)
```
